# Optimizing a Trainium2 kernel written in Bass

```python
import jax, jax.numpy as jnp
from jax import lax
import numpy as np

D_MODEL = 1024
BATCH = 2
SEQ = 8192
DEPTH = 2

GRID_W = 64
CTX_LEN = 256
N_EVEN = (DEPTH + 1) // 2
N_ODD = DEPTH // 2
N_MOD = 6
EPS = 1e-6
NEG = -1e30
D_LRU = D_MODEL // 2
LRU_HEADS = 8
LRU_HEAD_DIM = D_LRU // LRU_HEADS
LRU_C = 8.0
CONV_A_W = 4
D_SC = D_MODEL // 2
CONV_B_W = 3
RC_IN_WIDTH = 2 * D_LRU + 3 * D_SC
HEAD_DIM = 64
N_Q_HEADS = D_MODEL // HEAD_DIM
N_KV_HEADS = 4
GQA_GROUP = N_Q_HEADS // N_KV_HEADS
D_Q = N_Q_HEADS * HEAD_DIM
D_KV = N_KV_HEADS * HEAD_DIM
WINDOW = 128
BLOCK = 128
ROPE_BASE = 10000.0
ROPE_FREQS = HEAD_DIM // 4
D_FF = -(-8 * D_MODEL // (3 * 256)) * 256

kernel_name = "hybrid_rglru_shortconv_swa_dit_prefix"


def rmsnorm(x, g):
    x32 = x.astype(jnp.float32)
    y = x32 * lax.rsqrt(jnp.mean(x32 * x32, axis=-1, keepdims=True) + EPS)
    return (y * g.astype(jnp.float32)).astype(x.dtype)


def modulate(h, shift, scale):
    return h * (1 + scale) + shift


def swiglu(h, w_in, w_out):
    g, u = jnp.split(h @ w_in, 2, axis=-1)
    return (jax.nn.silu(g) * u) @ w_out


def depthwise_conv(x, w, pad):
    return lax.conv_general_dilated(
        x, w[:, None, :], window_strides=(1,), padding=[pad],
        dimension_numbers=('NWC', 'WIO', 'NWC'), feature_group_count=x.shape[-1])


def block_diag(x, w, b):
    B_, T, _ = x.shape
    xh = x.reshape(B_, T, LRU_HEADS, LRU_HEAD_DIM)
    return (jnp.einsum('bthi,hij->bthj', xh, w) + b).reshape(B_, T, D_LRU)


def rglru_coeffs(xc, r_w, r_b, i_w, i_b, lam):
    r = jax.nn.sigmoid(block_diag(xc, r_w, r_b).astype(jnp.float32))
    ig = jax.nn.sigmoid(block_diag(xc, i_w, i_b).astype(jnp.float32))
    log_a = -LRU_C * r * jax.nn.softplus(-lam.astype(jnp.float32))
    a = jnp.exp(log_a)
    mult = jnp.sqrt(jnp.maximum(-jnp.expm1(2.0 * log_a), 1e-12))
    return a, mult * ig * xc.astype(jnp.float32)


def _lin_combine(left, right):
    a_l, b_l = left
    a_r, b_r = right
    return a_r * a_l, a_r * b_l + b_r


def linear_scan(a, b, h0):
    a_cum, b_cum = lax.associative_scan(_lin_combine, (a, b), axis=1)
    h = a_cum * h0[:, None, :] + b_cum
    return h, h[:, -1]


def bidir_rglru(xc_c, xc_l, r_w, r_b, i_w, i_b, lam):
    outs_c, outs_l = [], []
    for d in range(2):
        ac, bc = rglru_coeffs(xc_c, r_w[d], r_b[d], i_w[d], i_b[d], lam[d])
        al, bl = rglru_coeffs(xc_l, r_w[d], r_b[d], i_w[d], i_b[d], lam[d])
        if d == 1:
            ac, bc, al, bl = (jnp.flip(t, axis=1) for t in (ac, bc, al, bl))
        hc, hc_last = linear_scan(ac, bc, jnp.zeros_like(ac[:, 0]))
        hl, _ = linear_scan(al, bl, hc_last)
        if d == 1:
            hc, hl = jnp.flip(hc, axis=1), jnp.flip(hl, axis=1)
        outs_c.append(hc)
        outs_l.append(hl)
    return outs_c[0] + outs_c[1], outs_l[0] + outs_l[1]


def rc_mixer(hc, hl, w_in, conv_a_w, conv_a_b, r_w, r_b, i_w, i_b, lam, conv_b_w, w_out, ctx_out):
    splits = [D_LRU, 2 * D_LRU, 2 * D_LRU + D_SC, 2 * D_LRU + 2 * D_SC]
    xa_c, ga_c, bg_c, cg_c, v_c = jnp.split(hc @ w_in, splits, axis=-1)
    xa_l, ga_l, bg_l, cg_l, v_l = jnp.split(hl @ w_in, splits, axis=-1)
    xa_c = depthwise_conv(xa_c, conv_a_w, (2, 1)) + conv_a_b
    xa_l = depthwise_conv(xa_l, conv_a_w, (2, 1)) + conv_a_b
    hsum_c, hsum_l = bidir_rglru(xa_c, xa_l, r_w, r_b, i_w, i_b, lam)

    def merge(hsum, ga, bg, cg, v):
        ya = hsum.astype(ga.dtype) * jax.nn.gelu(ga)
        yb = bg * depthwise_conv(cg * v, conv_b_w, (1, 1))
        return jnp.concatenate([ya, yb], axis=-1) @ w_out

    yl = merge(hsum_l, ga_l, bg_l, cg_l, v_l)
    yc = merge(hsum_c, ga_c, bg_c, cg_c, v_c) if ctx_out else None
    return yc, yl


def rope_2d(x, cos, sin):
    S = x.shape[1]
    bshape = (1, S) + (1,) * (x.ndim - 3) + (2, ROPE_FREQS)
    c, s = cos.reshape(bshape), sin.reshape(bshape)
    xr = x.reshape(x.shape[:-1] + (2, 2, ROPE_FREQS))
    x1, x2 = xr[..., 0, :], xr[..., 1, :]
    out = jnp.stack([x1 * c - x2 * s, x2 * c + x1 * s], axis=-2)
    return out.reshape(x.shape)


def attn_mixer(hc, hl, w_qkv, sink, w_out, cos, sin, ctx_out):
    B_, S, _ = hl.shape
    T = hc.shape[1]
    nb = S // BLOCK
    scale = HEAD_DIM ** -0.5

    def qkv(h):
        L = h.shape[1]
        q, k, v = jnp.split(h @ w_qkv, [D_Q, D_Q + D_KV], axis=-1)
        return (q.reshape(B_, L, N_KV_HEADS, GQA_GROUP, HEAD_DIM),
                k.reshape(B_, L, N_KV_HEADS, HEAD_DIM),
                v.reshape(B_, L, N_KV_HEADS, HEAD_DIM))

    ql, kl, vl = qkv(hl)
    qc, kc, vc = qkv(hc)
    ql, kl = rope_2d(ql, cos, sin), rope_2d(kl, cos, sin)
    sink32 = sink.astype(jnp.float32).reshape(N_KV_HEADS, GQA_GROUP, 1, 1)

    qb = ql.reshape(B_, nb, BLOCK, N_KV_HEADS, GQA_GROUP, HEAD_DIM) * scale

    def band(t):
        tb = t.reshape(B_, nb, BLOCK, N_KV_HEADS, HEAD_DIM)
        tp = jnp.pad(tb, ((0, 0), (1, 1), (0, 0), (0, 0), (0, 0)))
        return jnp.concatenate([tp[:, :-2], tp[:, 1:-1], tp[:, 2:]], axis=2)

    kw, vw = band(kl), band(vl)
    qi = jnp.arange(BLOCK)[:, None]
    kj = jnp.arange(3 * BLOCK)[None, :]
    in_window = jnp.abs(kj - BLOCK - qi) <= WINDOW
    kblk = jnp.arange(nb)[:, None] + (jnp.arange(3 * BLOCK) // BLOCK)[None, :] - 1
    valid = (kblk >= 0) & (kblk < nb)
    mask = in_window[None] & valid[:, None, :]

    s_win = jnp.einsum('bnqhgd,bnkhd->bnhgqk', qb, kw).astype(jnp.float32)
    s_win = jnp.where(mask[None, :, None, None], s_win, NEG)
    s_ctx = jnp.einsum('bnqhgd,bchd->bnhgqc', qb, kc).astype(jnp.float32)
    m = jnp.maximum(jnp.maximum(s_win.max(-1, keepdims=True), s_ctx.max(-1, keepdims=True)), sink32)
    e_win = jnp.exp(s_win - m)
    e_ctx = jnp.exp(s_ctx - m)
    denom = e_win.sum(-1, keepdims=True) + e_ctx.sum(-1, keepdims=True) + jnp.exp(sink32 - m)
    o = (jnp.einsum('bnhgqk,bnkhd->bnhgqd', e_win.astype(vl.dtype), vw)
         + jnp.einsum('bnhgqc,bchd->bnhgqd', e_ctx.astype(vl.dtype), vc))
    o = (o / denom.astype(o.dtype)).transpose(0, 1, 4, 2, 3, 5).reshape(B_, S, D_Q)
    yl = o @ w_out

    yc = None
    if ctx_out:
        sc = jnp.einsum('bihgd,bjhd->bhgij', qc * scale, kc).astype(jnp.float32)
        mc = jnp.maximum(sc.max(-1, keepdims=True), sink32)
        ec = jnp.exp(sc - mc)
        dc = ec.sum(-1, keepdims=True) + jnp.exp(sink32 - mc)
        oc = jnp.einsum('bhgij,bjhd->bhgid', ec.astype(vc.dtype), vc) / dc.astype(vc.dtype)
        yc = oc.transpose(0, 3, 1, 2, 4).reshape(B_, T, D_Q) @ w_out
    return yc, yl


def setup_inputs(seed: int = 0) -> dict:
    key = jax.random.key(seed)
    ks = iter(jax.random.split(key, 40))
    D = D_MODEL

    def nrm(shape, scale):
        return jax.random.normal(next(ks), shape, jnp.float32) * scale

    a_c = jax.random.uniform(next(ks), (N_EVEN, 2, D_LRU), jnp.float32, minval=0.9, maxval=0.999)
    sig = a_c ** (1.0 / LRU_C)
    rc_lambda = jnp.log(sig) - jnp.log1p(-sig)
    return {
        "x": nrm((BATCH, SEQ, D), 1.0),
        "c": nrm((BATCH, D), 1.0),
        "ctx": nrm((BATCH, CTX_LEN, D), 1.0),
        "c_ctx": nrm((D,), 1.0),
        "ada_w": nrm((DEPTH, D, N_MOD * D), 0.5 * D ** -0.5),
        "ada_b": nrm((DEPTH, N_MOD * D), 0.02),
        "norm_mix_g": 1.0 + nrm((DEPTH, D), 0.05),
        "norm_ffn_g": 1.0 + nrm((DEPTH, D), 0.05),
        "norm_final_g": 1.0 + nrm((D,), 0.05),
        "ffn_w_in": nrm((DEPTH, D, 2 * D_FF), D ** -0.5),
        "ffn_w_out": nrm((DEPTH, D_FF, D), D_FF ** -0.5),
        "rc_w_in": nrm((N_EVEN, D, RC_IN_WIDTH), D ** -0.5),
        "rc_conv_a_w": nrm((N_EVEN, CONV_A_W, D_LRU), CONV_A_W ** -0.5),
        "rc_conv_a_b": nrm((N_EVEN, D_LRU), 0.02),
        "rc_gate_r_w": nrm((N_EVEN, 2, LRU_HEADS, LRU_HEAD_DIM, LRU_HEAD_DIM), LRU_HEAD_DIM ** -0.5),
        "rc_gate_r_b": nrm((N_EVEN, 2, LRU_HEADS, LRU_HEAD_DIM), 0.02),
        "rc_gate_i_w": nrm((N_EVEN, 2, LRU_HEADS, LRU_HEAD_DIM, LRU_HEAD_DIM), LRU_HEAD_DIM ** -0.5),
        "rc_gate_i_b": nrm((N_EVEN, 2, LRU_HEADS, LRU_HEAD_DIM), 0.02),
        "rc_lambda": rc_lambda,
        "rc_conv_b_w": nrm((N_EVEN, CONV_B_W, D_SC), CONV_B_W ** -0.5),
        "rc_w_out": nrm((N_EVEN, D_LRU + D_SC, D), (D_LRU + D_SC) ** -0.5),
        "at_w_qkv": nrm((N_ODD, D, D_Q + 2 * D_KV), D ** -0.5),
        "at_sink": nrm((N_ODD, N_Q_HEADS), 0.5),
        "at_w_out": nrm((N_ODD, D_Q, D), D_Q ** -0.5),
    }


def reference(x, c, ctx, c_ctx, ada_w, ada_b, norm_mix_g, norm_ffn_g, norm_final_g,
              ffn_w_in, ffn_w_out, rc_w_in, rc_conv_a_w, rc_conv_a_b, rc_gate_r_w, rc_gate_r_b,
              rc_gate_i_w, rc_gate_i_b, rc_lambda, rc_conv_b_w, rc_w_out,
              at_w_qkv, at_sink, at_w_out):
    S = x.shape[1]
    ROWS = S // GRID_W
    row = jnp.repeat(jnp.arange(ROWS), GRID_W).astype(jnp.float32)
    col = jnp.tile(jnp.arange(GRID_W), ROWS).astype(jnp.float32)
    inv_freq = ROPE_BASE ** (-jnp.arange(ROPE_FREQS, dtype=jnp.float32) / ROPE_FREQS)
    ang = jnp.stack([row[:, None] * inv_freq, col[:, None] * inv_freq], axis=1)
    cos, sin = jnp.cos(ang).astype(x.dtype), jnp.sin(ang).astype(x.dtype)

    silu_c = jax.nn.silu(c)
    silu_cc = jax.nn.silu(c_ctx)
    xl, xc = x, ctx
    for i in range(DEPTH):
        last = i == DEPTH - 1
        j = i // 2
        mod_l = jnp.split((silu_c @ ada_w[i] + ada_b[i])[:, None, :], N_MOD, axis=-1)
        mod_c = jnp.split(silu_cc @ ada_w[i] + ada_b[i], N_MOD, axis=-1)
        sh1_l, sc1_l, g1_l, sh2_l, sc2_l, g2_l = mod_l
        sh1_c, sc1_c, g1_c, sh2_c, sc2_c, g2_c = mod_c

        hl = modulate(rmsnorm(xl, norm_mix_g[i]), sh1_l, sc1_l)
        hc = modulate(rmsnorm(xc, norm_mix_g[i]), sh1_c, sc1_c)
        if i % 2 == 0:
            yc, yl = rc_mixer(hc, hl, rc_w_in[j], rc_conv_a_w[j], rc_conv_a_b[j],
                              rc_gate_r_w[j], rc_gate_r_b[j], rc_gate_i_w[j], rc_gate_i_b[j],
                              rc_lambda[j], rc_conv_b_w[j], rc_w_out[j], not last)
        else:
            yc, yl = attn_mixer(hc, hl, at_w_qkv[j], at_sink[j], at_w_out[j], cos, sin, not last)

        xl = xl + g1_l * yl
        hl = modulate(rmsnorm(xl, norm_ffn_g[i]), sh2_l, sc2_l)
        xl = xl + g2_l * swiglu(hl, ffn_w_in[i], ffn_w_out[i])
        if not last:
            xc = xc + g1_c * yc
            hc = modulate(rmsnorm(xc, norm_ffn_g[i]), sh2_c, sc2_c)
            xc = xc + g2_c * swiglu(hc, ffn_w_in[i], ffn_w_out[i])
    return rmsnorm(xl, norm_final_g)
```

```python
import contextlib
import numpy as np
import concourse.bass as bass
import concourse.mybir as mybir
from concourse.bass_utils import run_bass_kernel_spmd

F32 = mybir.dt.float32
BF16 = mybir.dt.bfloat16
AF = mybir.ActivationFunctionType
ALU = mybir.AluOpType

D = 1024
KC = 8
SEQ = 8192
OWN = 2048
CTX = 256
NF = 2308
NCARRY = 6144
DFF = 2816
FT = 22
EPS = 1e-6
NEGBIG = -30000.0

S_ADAB = 0
S_GMIX = S_ADAB + 96
S_GFFN = S_GMIX + 16
S_GFIN = S_GFFN + 16
S_CAB = S_GFIN + 8
S_CBW = S_CAB + 4
S_NS = S_CBW + 12
P_CVEC = 0
P_TAPS = P_CVEC + 16
P_GB = P_TAPS + 100
P_LAM = P_GB + 40
P_FLAGS = P_LAM + 20
P_SINK = P_FLAGS + 16
P_NP = P_SINK + 16


class _Stop(Exception):
    pass


class _StopGuard:
    def __enter__(self):
        return self

    def __exit__(self, et, ev, tb):
        return et is _Stop


class Buf:
    __slots__ = ("name", "w", "r", "excl")

    def __init__(self, name, excl=False):
        self.name = name
        self.w = None
        self.r = {}
        self.excl = excl


class Eng:
    def __init__(self, name, h, sems):
        self.name, self.h, self.sems = name, h, sems
        self.n = 0
        self.waited = {}
        self.pw = []
        self.pr = []

    def tok(self):
        if self.n == 0:
            return None
        return (self.sems[(self.n - 1) // 30000], (self.n - 1) % 30000 + 1)


class Tracker:
    def __init__(self, nc, es):
        self.nc = nc
        mk = lambda nm: es.enter_context(nc.semaphore(nm))
        self.E = {
            "pe": Eng("pe", nc.tensor, [mk("pe0"), mk("pe1")]),
            "act": Eng("act", nc.scalar, [mk("act0"), mk("act1")]),
            "dve": Eng("dve", nc.vector, [mk("dve0"), mk("dve1")]),
            "pool": Eng("pool", nc.gpsimd, [mk("pool0")]),
            "sp": Eng("sp", nc.sync, [mk("sp0")]),
        }
        self.dsems = [[mk(f"dma{i}"), 0] for i in range(36)]
        self.di = {"hw": 0, "sw": 0}

    def _wait(self, e, toks):
        best = {}
        for sem, val in toks:
            k = id(sem)
            if k not in best or best[k][1] < val:
                best[k] = (sem, val)
        for k, (sem, val) in best.items():
            if e.waited.get(k, 0) >= val:
                continue
            e.h.wait_ge(sem, val)
            e.waited[k] = val

    @staticmethod
    def _deps(reads, writes):
        toks = []
        for b in reads:
            if b.w is not None:
                toks.append(b.w)
            if b.excl:
                toks.extend(b.r.values())
        for b in writes:
            if b.w is not None:
                toks.append(b.w)
            toks.extend(b.r.values())
        return toks

    mute = False

    def op(self, en, fn, reads=(), writes=(), signal=True):
        if self.mute:
            return None
        e = self.E[en]
        toks = self._deps(reads, writes)
        if en == "pe":
            own = {id(s) for s in e.sems}
            toks = [t for t in toks if id(t[0]) not in own]
        self._wait(e, toks)
        inst = fn()
        e.pr.extend(reads)
        e.pw.extend(writes)
        if signal:
            e.n += 1
            tok = e.tok()
            inst.then_inc(tok[0], 1)
            wset = {id(b) for b in e.pw}
            for b in e.pw:
                b.w = tok
                b.r = {}
            for b in e.pr:
                if id(b) not in wset:
                    b.r[id(tok[0])] = tok
            e.pw, e.pr = [], []
        return inst

    def dma(self, en, out_ap, in_ap, reads=(), writes=(), cast=False):
        if self.mute:
            return
        e = self.E[en]
        if en == "pool":
            slot = self.dsems[24 + self.di["sw"]]
            self.di["sw"] = (self.di["sw"] + 1) % 12
        else:
            slot = self.dsems[self.di["hw"]]
            self.di["hw"] = (self.di["hw"] + 1) % 24
        sem, val = slot
        toks = self._deps(reads, writes)
        if val > 0:
            toks.append((sem, val))
        self._wait(e, toks)
        e.h.dma_start(out=out_ap, in_=in_ap).then_inc(sem, 16)
        slot[1] = val + 16
        tok = (sem, val + 16)
        for b in writes:
            b.w = tok
            b.r = {}
        for b in reads:
            b.r[id(sem)] = tok

    def barrier(self):
        toks = []
        for e in self.E.values():
            t = e.tok()
            if t is not None:
                toks.append(t)
        for sem, val in self.dsems:
            if val > 0:
                toks.append((sem, val))
        for e in self.E.values():
            self._wait(e, toks)


import itertools
import os
_uid = itertools.count()
_SKIP = os.environ.get("K_SKIP", "")


def build_nc(debug=False, stop=None):
    nc = bass.Bass("TRN2", target_bir_lowering=False)
    es = contextlib.ExitStack()

    def din(name, shape, dt=F32):
        return nc.dram_tensor(name, list(shape), dt, kind="ExternalInput").ap()

    def dscr(name, shape, dt=F32, out=False):
        kind = "ExternalOutput" if out else "Internal"
        return nc.dram_tensor(name, list(shape), dt, kind=kind).ap()

    xT_full = din("xT_full", [128, KC, NF])
    xT_ctx = din("xT_ctx", [128, KC, CTX])
    xT_carry = din("xT_carry", [128, KC, NCARRY])
    mk_full = din("mk_full", [128, NF])
    mk_carry = din("mk_carry", [128, NCARRY])
    pf_full = din("pf_full", [1, NF])
    pf_carry = din("pf_carry", [1, NCARRY])
    smalls = din("smalls", [128, S_NS])
    pc = din("pc", [128, P_NP])
    gw_all = din("gw_all", [128, 5 * 2 * 4 * 128])
    ada_w = din("ada_w", [2, D, 6 * D])
    ffn_w_in = din("ffn_w_in", [2, D, 2 * DFF])
    ffn_w_out = din("ffn_w_out", [2, DFF, D])
    rc_w_in = din("rc_w_in", [D, 2560])
    rc_w_out = din("rc_w_out", [D, D])
    at_w_qkv = din("at_w_qkv", [D, 1536])
    at_w_out = din("at_w_out", [D, D])
    rope_q = din("rope_q", [128, 2, OWN])
    rope_k = din("rope_k", [128, 2, OWN + 256])
    amask = din("amask", [128, 4, 512])
    consts = din("consts", [128, 3 * 128])

    xmid = dscr("xmid", [128, KC, NF + CTX], out=debug)
    xl1 = dscr("xl1", [128, KC, NF + CTX], out=debug)
    xmid1 = dscr("xmid1", [128, KC, OWN])
    outT = dscr("outT", [128, KC, OWN], out=True)
    modv_dbg = dscr("modv_dbg", [128, 192], out=True) if debug else None

    with es:
        T = Tracker(nc, es)
        sb = lambda st, name, shape, dt=F32: st.enter_context(nc.sbuf_tensor("s_" + name + "_" + str(next(_uid)), list(shape), dt))

        sm = sb(es, "sm", [128, S_NS]); b_sm = Buf("sm")
        pcs = sb(es, "pcs", [128, P_NP]); b_pc = Buf("pc")
        cst = sb(es, "cst", [128, 384], BF16); b_cst = Buf("cst")
        modv = sb(es, "modv", [128, 2, 48, 2]); b_modv = Buf("modv")
        nA = sb(es, "nA", [128, 5, KC, 2]); b_nA = Buf("nA")
        gwb = sb(es, "gwb", [128, 5, 2, 4, 128], BF16); b_gw = Buf("gw")
        cc = sb(es, "cc", [128, 2, 5, 4]); b_cc = Buf("cc")
        fin = sb(es, "fin", [128, 8, 4]); b_fin = Buf("fin")
        ones_bf = sb(es, "ones_bf", [128, 128], BF16); b_ones = Buf("ones")
        negrow = sb(es, "negrow", [1, 128], BF16)
        pfF = sb(es, "pfF", [1, NF], BF16); b_pfF = Buf("pfF")

        psum = [es.enter_context(nc.psum_tensor(f"ps{i}", [128, 512], F32)) for i in range(8)]
        b_ps = [Buf(f"ps{i}", excl=True) for i in range(8)]
        pctr = [0]

        def next_ps():
            i = pctr[0] % 8
            pctr[0] += 1
            return psum[i], b_ps[i]

        ident = cst[:, 0:128]
        ropeP = cst[:, 128:256]

        T.dma("sp", sm[:], smalls, writes=[b_sm])
        T.dma("sp", pcs[:], pc, writes=[b_pc])
        T.dma("pool", cst[:], consts, writes=[b_cst], cast=True)
        for i in range(5):
            T.dma("pool", gwb[:, i].rearrange("p b c d -> p (b c d)"), gw_all[:, i * 1024:(i + 1) * 1024], writes=[b_gw], cast=True)
        T.dma("pool", pfF[:, 0:1154], pf_full[:, 0:1154], writes=[b_pfF], cast=True)
        T.dma("pool", pfF[:, 1154:NF], pf_full[:, 1154:NF], writes=[b_pfF], cast=True)
        T.op("dve", lambda: nc.vector.memset(ones_bf[:], 1.0), writes=[b_ones])
        T.op("dve", lambda: nc.vector.memset(negrow[:], NEGBIG), writes=[b_ones])

        with contextlib.ExitStack() as st:
            tmp = sb(st, "lam_tmp", [128, 20]); b_tmp = Buf("lamtmp")
            T.op("act", lambda: nc.scalar.activation(out=tmp[:], in_=pcs[:, P_LAM:P_LAM + 20], func=AF.Exp, scale=-1.0), reads=[b_pc], writes=[b_tmp])
            T.op("act", lambda: nc.scalar.activation(out=tmp[:], in_=tmp[:], func=AF.Ln, bias=1.0), writes=[b_tmp])
            T.op("dve", lambda: nc.vector.tensor_scalar(out=cc[:, 0].rearrange("p a b -> p (a b)"), in0=tmp[:], scalar1=-8.0, scalar2=None, op0=ALU.mult), reads=[b_tmp], writes=[b_cc])
            T.op("dve", lambda: nc.vector.tensor_scalar(out=cc[:, 1].rearrange("p a b -> p (a b)"), in0=tmp[:], scalar1=-16.0, scalar2=None, op0=ALU.mult), reads=[b_tmp], writes=[b_cc])
            T.barrier()

        if stop is not None and stop.startswith("only1"):
            T.mute = True
            stop = stop[5:] or None
        with contextlib.ExitStack() as st:
            sil = sb(st, "sil", [128, KC, 2], BF16); b_sil = Buf("sil")
            T.op("act", lambda: nc.scalar.activation(out=sil[:].rearrange("p a b -> p (a b)"), in_=pcs[:, P_CVEC:P_CVEC + 16], func=AF.Silu), reads=[b_pc], writes=[b_sil])
            wb = [sb(st, f"adaw{i}", [128, KC, 1536], BF16) for i in range(2)]
            b_wb = [Buf("adaw0"), Buf("adaw1")]
            cnt = 0
            for l in range(2):
                ps, bps = next_ps()
                for piece in range(4):
                    w, bw = wb[cnt % 2], b_wb[cnt % 2]
                    cnt += 1
                    src = ada_w[l].rearrange("(kc p) f -> p kc f", p=128)[:, :, piece * 1536:(piece + 1) * 1536]
                    T.dma("pool", w[:], src, writes=[bw], cast=True)
                    for ft in range(12):
                        col = (piece * 12 + ft) * 2
                        for kc in range(KC):
                            T.op("pe", lambda kc=kc, ft=ft, col=col, w=w, ps=ps: nc.tensor.matmul(
                                ps[:, col:col + 2], w[:, kc, ft * 128:(ft + 1) * 128], sil[:, kc, :],
                                start=(kc == 0), stop=(kc == KC - 1)),
                                reads=[bw, b_sil], writes=[bps], signal=(kc == KC - 1))
                for j in range(2):
                    T.op("dve", lambda l=l, j=j, ps=ps: nc.vector.tensor_tensor(
                        out=modv[:, l, :, j], in0=ps[:, 0:96].rearrange("p (a b) -> p a b", b=2)[:, :, j],
                        in1=sm[:, S_ADAB + l * 48:S_ADAB + (l + 1) * 48], op=ALU.add),
                        reads=[bps, b_sm], writes=[b_modv])
            for idx, (l, m, goff) in enumerate([(0, 1, S_GMIX), (0, 4, S_GFFN), (1, 1, S_GMIX + 8), (1, 4, S_GFFN + 8)]):
                for j in range(2):
                    T.op("dve", lambda idx=idx, l=l, m=m, goff=goff, j=j: nc.vector.scalar_tensor_tensor(
                        out=nA[:, idx, :, j], in0=modv[:, l, m * 8:(m + 1) * 8, j], scalar=1.0,
                        in1=sm[:, goff:goff + 8], op0=ALU.add, op1=ALU.mult),
                        reads=[b_modv, b_sm], writes=[b_nA])
            if debug:
                T.dma("sp", modv_dbg, modv[:].rearrange("p a b c -> p (a b c)"), reads=[b_modv])
            T.barrier()

        def rmsnorm_mod(st_tiles, xt, bx, n, A_ap, B_ap, hb, bh):
            sq, bsq, rs, brs, tt, btt = st_tiles
            T.op("act", lambda: nc.scalar.activation(out=sq[:, :, :n], in_=xt[:, :, :n], func=AF.Square), reads=[bx], writes=[bsq])
            ps, bps = next_ps()
            for kc in range(KC):
                T.op("pe", lambda kc=kc: nc.tensor.matmul(ps[:, :n], ones_bf[:], sq[:, kc, :n], start=(kc == 0), stop=(kc == KC - 1)),
                     reads=[bsq, b_ones], writes=[bps], signal=(kc == KC - 1))
            T.op("act", lambda: nc.scalar.activation(out=rs[:, :n], in_=ps[:, :n], func=AF.Sqrt, scale=1.0 / D, bias=EPS), reads=[bps], writes=[brs])
            T.op("dve", lambda: nc.vector.reciprocal(out=rs[:, :n], in_=rs[:, :n]), writes=[brs])
            for ch in range(KC):
                T.op("dve", lambda ch=ch: nc.vector.scalar_tensor_tensor(
                    out=tt[:, ch, :n], in0=xt[:, ch, :n], scalar=A_ap(ch), in1=rs[:, :n], op0=ALU.mult, op1=ALU.mult),
                    reads=[bx, brs, b_nA, b_sm], writes=[btt])
            for ch in range(KC):
                if B_ap is not None:
                    T.op("act", lambda ch=ch: nc.scalar.activation(out=hb[:, ch, :n], in_=tt[:, ch, :n], func=AF.Identity, bias=B_ap(ch)),
                         reads=[btt, b_modv], writes=[bh])
                else:
                    T.op("act", lambda ch=ch: nc.scalar.activation(out=hb[:, ch, :n], in_=tt[:, ch, :n], func=AF.Identity),
                         reads=[btt], writes=[bh])

        def norm_tiles(st, n, tag):
            sq = sb(st, f"sq{tag}", [128, KC, n], BF16)
            rs = sb(st, f"rs{tag}", [128, n])
            tt = sb(st, f"tt{tag}", [128, KC, n])
            return (sq, Buf("sq"), rs, Buf("rs"), tt, Buf("tt"))

        def build_diag(dg, b_dg, taps_ap_fn, count):
            for i in range(count):
                T.op("dve", lambda i=i: nc.vector.tensor_scalar(out=dg[:, i, :], in0=ident, scalar1=taps_ap_fn(i), scalar2=None, op0=ALU.mult),
                     reads=[b_cst, b_pc, b_sm], writes=[b_dg])

        def lru_dir(n_cols, tiles, xcb, b_xcb, xcf, b_xcf, idx, g, pf_row, b_pf, pf_off, bufs, init_ap, reverse, hout, b_hout):
            (r, br), (ig, big), (a, ba) = bufs
            for (c0, n) in tiles:
                for q, (dst, bdst) in enumerate(((r, br), (ig, big))):
                    ps, bps = next_ps()
                    T.op("pe", lambda ps=ps, q=q, c0=c0, n=n: nc.tensor.matmul(ps[:, :n], gwb[:, idx, q, g, :], xcb[:, c0:c0 + n], start=True, stop=(pf_row is None)),
                         reads=[b_gw, b_xcb], writes=[bps], signal=(pf_row is None))
                    if pf_row is not None:
                        T.op("pe", lambda ps=ps, c0=c0, n=n: nc.tensor.matmul(ps[:, :n], negrow[:], pf_row[:, pf_off + c0:pf_off + c0 + n], start=False, stop=True),
                             reads=[b_pf, b_ones], writes=[bps])
                    T.op("act", lambda ps=ps, q=q, dst=dst, c0=c0, n=n: nc.scalar.activation(
                        out=dst[:, c0:c0 + n], in_=ps[:, :n], func=AF.Sigmoid,
                        bias=pcs[:, P_GB + idx * 8 + q * 4 + g:P_GB + idx * 8 + q * 4 + g + 1]),
                        reads=[bps, b_pc], writes=[bdst])
            N = n_cols
            T.op("act", lambda: nc.scalar.activation(out=a[:, :N], in_=r[:, :N], func=AF.Exp, scale=cc[:, 0, idx, g:g + 1]), reads=[br, b_cc], writes=[ba])
            T.op("act", lambda: nc.scalar.activation(out=r[:, :N], in_=r[:, :N], func=AF.Exp, scale=cc[:, 1, idx, g:g + 1]), reads=[b_cc], writes=[br])
            T.op("act", lambda: nc.scalar.activation(out=r[:, :N], in_=r[:, :N], func=AF.Sqrt, scale=-1.0, bias=1.0000002), writes=[br])
            T.op("dve", lambda: nc.vector.tensor_tensor(out=ig[:, :N], in0=ig[:, :N], in1=r[:, :N], op=ALU.mult), reads=[br], writes=[big])
            T.op("dve", lambda: nc.vector.tensor_tensor(out=ig[:, :N], in0=ig[:, :N], in1=xcf[:, :N], op=ALU.mult), reads=[b_xcf], writes=[big])
            if reverse:
                T.op("dve", lambda: nc.vector.tensor_tensor_scan(out=hout[:, :N][:, ::-1], data0=a[:, :N][:, ::-1], data1=ig[:, :N][:, ::-1],
                                                               initial=init_ap, op0=ALU.mult, op1=ALU.add),
                     reads=[ba, big, b_fin], writes=[b_hout])
            else:
                T.op("dve", lambda: nc.vector.tensor_tensor_scan(out=hout[:, :N], data0=a[:, :N], data1=ig[:, :N],
                                                               initial=init_ap, op0=ALU.mult, op1=ALU.add),
                     reads=[ba, big, b_fin], writes=[b_hout])

        def conv_pe(dg, b_dg, dbase, ntaps, off0, src, b_src, s0, tiles, evac):
            for (c0, n) in tiles:
                ps, bps = next_ps()
                for o in range(ntaps):
                    T.op("pe", lambda ps=ps, o=o, c0=c0, n=n: nc.tensor.matmul(
                        ps[:, :n], dg[:, dbase + o, :], src[:, s0 + c0 + off0 + o:s0 + c0 + off0 + o + n], start=(o == 0), stop=(o == ntaps - 1)),
                        reads=[b_dg, b_src], writes=[bps], signal=(o == ntaps - 1))
                evac(ps, bps, c0, n)

        def tiles_of(n, step=512):
            return [(c0, min(step, n - c0)) for c0 in range(0, n, step)]

        with contextlib.ExitStack() as L0:
            w_in = sb(L0, "rc_w_in", [128, KC, 2560], BF16); b_win = Buf("rc_w_in")
            for h in range(2):
                T.dma("pool", w_in[:, :, h * 1280:(h + 1) * 1280], rc_w_in.rearrange("(kc p) f -> p kc f", p=128)[:, :, h * 1280:(h + 1) * 1280], writes=[b_win], cast=True)
            dgN = sb(L0, "dgN", [128, 20, 128], BF16); b_dgN = Buf("dgN")
            build_diag(dgN, b_dgN, lambda i: pcs[:, P_TAPS + i:P_TAPS + i + 1], 20)
            dgB = sb(L0, "dgB", [128, 12, 128], BF16); b_dgB = Buf("dgB")
            build_diag(dgB, b_dgB, lambda i: sm[:, S_CBW + i:S_CBW + i + 1], 12)
            T.op("dve", lambda: nc.vector.memset(fin[:], 0.0), writes=[b_fin])
            ya_c = sb(L0, "ya_c", [128, 4, CTX], BF16); b_ya_c = Buf("ya_c")
            yb_c = sb(L0, "yb_c", [128, 4, CTX], BF16); b_yb_c = Buf("yb_c")

            def flag(i):
                return pcs[:, P_FLAGS + i:P_FLAGS + i + 1]

            def phase_A(stA, xsrc, mksrc, ncols, j, xa, b_xa, cv, b_cv, ya, b_ya, yb, b_yb):
                xt2 = [sb(stA, f"ax{i}", [128, KC, 256]) for i in range(2)]; bxt2 = [Buf("ax0"), Buf("ax1")]
                mk2 = [sb(stA, f"am{i}", [128, 256]) for i in range(2)]; bmk2 = [Buf("am0"), Buf("am1")]
                hbA = sb(stA, "ah", [128, KC, 256], BF16); b_hbA = Buf("ah")
                cgt = sb(stA, "acg", [128, 256]); b_cgt = Buf("acg")
                ntA = norm_tiles(stA, 256, "a")
                for ti, (c0, n) in enumerate(tiles_of(ncols, 256)):
                    xt, bx = xt2[ti % 2], bxt2[ti % 2]
                    mk, bmk = mk2[ti % 2], bmk2[ti % 2]
                    T.dma("sp", xt[:, :, :n], xsrc[:, :, c0:c0 + n], writes=[bx])
                    if mksrc is not None:
                        T.dma("sp", mk[:, :n], mksrc[:, c0:c0 + n], writes=[bmk])
                    else:
                        T.op("dve", lambda: nc.vector.memset(mk[:], 1.0), writes=[bmk])
                    rmsnorm_mod(ntA, xt, bx, n, lambda ch: nA[:, 0, ch, j:j + 1], lambda ch: modv[:, 0, 0 * 8 + ch, j:j + 1], hbA, b_hbA)

                    def proj(m):
                        ps, bps = next_ps()
                        for kc in range(KC):
                            T.op("pe", lambda: nc.tensor.matmul(ps[:, :n], w_in[:, kc, m * 128:(m + 1) * 128], hbA[:, kc, :n],
                                                                start=(kc == 0), stop=(kc == KC - 1)),
                                 reads=[b_win, b_hbA], writes=[bps], signal=(kc == KC - 1))
                        return ps, bps
                    for g in range(4):
                        ps, bps = proj(g)
                        T.op("dve", lambda: nc.vector.tensor_tensor(out=xa[:, g, 2 + c0:2 + c0 + n], in0=ps[:, :n], in1=mk[:, :n], op=ALU.mult),
                             reads=[bps, bmk], writes=[b_xa])
                        ps, bps = proj(4 + g)
                        T.op("act", lambda: nc.scalar.activation(out=ya[:, g, c0:c0 + n], in_=ps[:, :n], func=AF.Gelu_apprx_tanh),
                             reads=[bps], writes=[b_ya])
                        ps, bps = proj(8 + g)
                        T.op("act", lambda: nc.scalar.activation(out=yb[:, g, c0:c0 + n], in_=ps[:, :n], func=AF.Identity),
                             reads=[bps], writes=[b_yb])
                        ps, bps = proj(12 + g)
                        T.op("dve", lambda: nc.vector.tensor_tensor(out=cgt[:, :n], in0=ps[:, :n], in1=mk[:, :n], op=ALU.mult),
                             reads=[bps, bmk], writes=[b_cgt])
                        ps, bps = proj(16 + g)
                        T.op("dve", lambda: nc.vector.tensor_tensor(out=cv[:, g, 1 + c0:1 + c0 + n], in0=ps[:, :n], in1=cgt[:, :n], op=ALU.mult),
                             reads=[bps, b_cgt], writes=[b_cv])

            def phase_BC(stB, ncols, xa, b_xa, cv, b_cv, ya, b_ya, yb, b_yb, pf_row, b_pf, initF, initB, finF, finB, lo, hi):
                N = hi - lo
                xcf = sb(stB, "b_xcf", [128, N]); b_xcf = Buf("b_xcf")
                xcb = sb(stB, "b_xcb", [128, N], BF16); b_xcb = Buf("b_xcb")
                r_ = sb(stB, "b_r", [128, N]); ig_ = sb(stB, "b_ig", [128, N]); a_ = sb(stB, "b_a", [128, N])
                bufs = ((r_, Buf("b_r")), (ig_, Buf("b_ig")), (a_, Buf("b_a")))
                big = bufs[1][1]
                hf = sb(stB, "b_hf", [128, N]); b_hf = Buf("b_hf")
                tl = tiles_of(N)
                for g in range(4):
                    def evac(ps, bps, c0, n):
                        T.op("act", lambda: nc.scalar.activation(out=xcf[:, c0:c0 + n], in_=ps[:, :n], func=AF.Identity, bias=sm[:, S_CAB + g:S_CAB + g + 1]),
                             reads=[bps, b_sm], writes=[b_xcf])
                        T.op("act", lambda: nc.scalar.activation(out=xcb[:, c0:c0 + n], in_=ps[:, :n], func=AF.Identity, bias=sm[:, S_CAB + g:S_CAB + g + 1]),
                             reads=[bps, b_sm], writes=[b_xcb])
                    conv_pe(dgN, b_dgN, g * 5, 5, -2, xa[:, g, :], b_xa, 2 + lo, tl, evac)
                    lru_dir(N, tl, xcb, b_xcb, xcf, b_xcf, 0, g, pf_row, b_pf, lo, bufs, initF(g), False, hf, b_hf)
                    if finF is not None:
                        T.op("dve", lambda: nc.vector.tensor_copy(out=fin[:, finF, g:g + 1], in_=hf[:, N - 1:N]), reads=[b_hf], writes=[b_fin])
                    lru_dir(N, tl, xcb, b_xcb, xcf, b_xcf, 1, g, pf_row, b_pf, lo, bufs, initB(g), True, ig_, big)
                    if finB is not None:
                        T.op("dve", lambda: nc.vector.tensor_copy(out=fin[:, finB, g:g + 1], in_=ig_[:, 0:1]), reads=[big], writes=[b_fin])
                    T.op("dve", lambda: nc.vector.tensor_tensor(out=hf[:, :N], in0=hf[:, :N], in1=ig_[:, :N], op=ALU.add), reads=[big], writes=[b_hf])
                    T.op("dve", lambda: nc.vector.tensor_tensor(out=ya[:, g, lo:hi], in0=hf[:, :N], in1=ya[:, g, lo:hi], op=ALU.mult),
                         reads=[b_hf], writes=[b_ya])

                    def evacB(ps, bps, c0, n):
                        T.op("dve", lambda: nc.vector.tensor_tensor(out=yb[:, g, c0:c0 + n], in0=ps[:, :n], in1=yb[:, g, c0:c0 + n], op=ALU.mult),
                             reads=[bps], writes=[b_yb])
                    conv_pe(dgB, b_dgB, g * 3, 3, -1, cv[:, g, :], b_cv, 1, tiles_of(ncols), evacB)

            with contextlib.ExitStack() as stc:
                xa = sb(stc, "xa_c", [128, 4, CTX + 4], BF16); b_xa = Buf("xa_c")
                cv = sb(stc, "cv_c", [128, 4, CTX + 2], BF16); b_cv = Buf("cv_c")
                T.op("dve", lambda: nc.vector.memset(xa[:], 0.0), writes=[b_xa])
                T.op("dve", lambda: nc.vector.memset(cv[:], 0.0), writes=[b_cv])
                with contextlib.ExitStack() as stA:
                    phase_A(stA, xT_ctx, None, CTX, 1, xa, b_xa, cv, b_cv, ya_c, b_ya_c, yb_c, b_yb_c)
                    T.barrier()
                with contextlib.ExitStack() as stB:
                    phase_BC(stB, CTX, xa, b_xa, cv, b_cv, ya_c, b_ya_c, yb_c, b_yb_c, None, None, lambda g: 0.0, lambda g: 0.0, 0, 1, 0, CTX)
                    T.barrier()

            with contextlib.ExitStack() as st:
                pfC = sb(st, "pfC", [1, NCARRY], BF16); b_pfC = Buf("pfC")
                for i in range(3):
                    T.dma("pool", pfC[:, i * 2048:(i + 1) * 2048], pf_carry[:, i * 2048:(i + 1) * 2048], writes=[b_pfC], cast=True)
                dgC = sb(st, "dgC", [128, 60, 128], BF16); b_dgC = Buf("dgC")
                build_diag(dgC, b_dgC, lambda i: pcs[:, P_TAPS + 40 + i:P_TAPS + 40 + i + 1], 60)
                xac = sb(st, "xac", [128, 4, NCARRY + 4], BF16); b_xac = Buf("xac")
                T.op("dve", lambda: nc.vector.memset(xac[:, :, 0:2], 0.0), writes=[b_xac])
                T.op("dve", lambda: nc.vector.memset(xac[:, :, NCARRY + 2:NCARRY + 4], 0.0), writes=[b_xac])
                with contextlib.ExitStack() as stA:
                    CN = 256
                    xt2 = [sb(stA, f"cx{i}", [128, KC, CN]) for i in range(2)]; bxt2 = [Buf("cx0"), Buf("cx1")]
                    mk2 = [sb(stA, f"cm{i}", [128, CN]) for i in range(2)]; bmk2 = [Buf("cm0"), Buf("cm1")]
                    hb = sb(stA, "ch", [128, KC, CN], BF16); b_hb = Buf("ch")
                    nt = norm_tiles(stA, CN, "c")
                    for ti, (c0, n) in enumerate(tiles_of(NCARRY, CN)):
                        xt, bx = xt2[ti % 2], bxt2[ti % 2]
                        mk, bmk = mk2[ti % 2], bmk2[ti % 2]
                        T.dma("sp", xt[:], xT_carry[:, :, c0:c0 + n], writes=[bx])
                        T.dma("sp", mk[:], mk_carry[:, c0:c0 + n], writes=[bmk])
                        rmsnorm_mod(nt, xt, bx, n, lambda ch: nA[:, 0, ch, 0:1], lambda ch: modv[:, 0, 0 * 8 + ch, 0:1], hb, b_hb)
                        for g in range(4):
                            ps, bps = next_ps()
                            for kc in range(KC):
                                T.op("pe", lambda: nc.tensor.matmul(ps[:, :n], w_in[:, kc, g * 128:(g + 1) * 128], hb[:, kc, :n],
                                                                    start=(kc == 0), stop=(kc == KC - 1)),
                                     reads=[b_win, b_hb], writes=[bps], signal=(kc == KC - 1))
                            T.op("dve", lambda: nc.vector.tensor_tensor(out=xac[:, g, 2 + c0:2 + c0 + n], in0=ps[:, :n], in1=mk[:, :n], op=ALU.mult),
                                 reads=[bps, bmk], writes=[b_xac])
                    T.barrier()
                with contextlib.ExitStack() as stB:
                    SEGN = 2048
                    xcf = sb(stB, "c_xcf", [128, SEGN]); b_xcf = Buf("c_xcf")
                    xcb = sb(stB, "c_xcb", [128, SEGN], BF16); b_xcb = Buf("c_xcb")
                    r_ = sb(stB, "c_r", [128, SEGN]); ig_ = sb(stB, "c_ig", [128, SEGN]); a_ = sb(stB, "c_a", [128, SEGN])
                    bufs = ((r_, Buf("c_r")), (ig_, Buf("c_ig")), (a_, Buf("c_a")))
                    big = bufs[1][1]
                    for s in range(3):
                        idx = 2 + s
                        T.op("dve", lambda: nc.vector.tensor_scalar(out=fin[:, 5, :], in0=fin[:, 0, :], scalar1=flag(3 * s), scalar2=None, op0=ALU.mult), reads=[b_pc], writes=[b_fin])
                        T.op("dve", lambda: nc.vector.scalar_tensor_tensor(out=fin[:, 5, :], in0=fin[:, 1, :], scalar=flag(3 * s + 1), in1=fin[:, 5, :], op0=ALU.mult, op1=ALU.add), reads=[b_pc], writes=[b_fin])
                        if s > 0:
                            T.op("dve", lambda: nc.vector.scalar_tensor_tensor(out=fin[:, 5, :], in0=fin[:, 2 + s - 1, :], scalar=flag(3 * s + 2), in1=fin[:, 5, :], op0=ALU.mult, op1=ALU.add), reads=[b_pc], writes=[b_fin])
                        for g in range(4):
                            def evac(ps, bps, c0, n):
                                T.op("act", lambda: nc.scalar.activation(out=xcf[:, c0:c0 + n], in_=ps[:, :n], func=AF.Identity, bias=sm[:, S_CAB + g:S_CAB + g + 1]),
                                     reads=[bps, b_sm], writes=[b_xcf])
                                T.op("act", lambda: nc.scalar.activation(out=xcb[:, c0:c0 + n], in_=ps[:, :n], func=AF.Identity, bias=sm[:, S_CAB + g:S_CAB + g + 1]),
                                     reads=[bps, b_sm], writes=[b_xcb])
                            tl = tiles_of(SEGN)
                            conv_pe(dgC, b_dgC, (s * 4 + g) * 5, 5, -2, xac[:, g, :], b_xac, 2 + s * SEGN, tl, evac)
                            lru_dir(SEGN, tl, xcb, b_xcb, xcf, b_xcf, idx, g, pfC, b_pfC, s * SEGN, bufs,
                                    fin[:, 5, g:g + 1], False, ig_, big)
                            T.op("dve", lambda: nc.vector.tensor_copy(out=fin[:, 2 + s, g:g + 1], in_=ig_[:, SEGN - 1:SEGN]), reads=[big], writes=[b_fin])
                    T.op("dve", lambda: nc.vector.tensor_scalar(out=fin[:, 6, :], in0=fin[:, 0, :], scalar1=flag(9), scalar2=None, op0=ALU.mult), reads=[b_pc], writes=[b_fin])
                    for s in range(3):
                        T.op("dve", lambda: nc.vector.scalar_tensor_tensor(out=fin[:, 6, :], in0=fin[:, 2 + s, :], scalar=flag(10 + s), in1=fin[:, 6, :], op0=ALU.mult, op1=ALU.add), reads=[b_pc], writes=[b_fin])
                    T.op("dve", lambda: nc.vector.tensor_scalar(out=fin[:, 7, :], in0=fin[:, 1, :], scalar1=flag(13), scalar2=None, op0=ALU.mult), reads=[b_pc], writes=[b_fin])
                    T.op("dve", lambda: nc.vector.scalar_tensor_tensor(out=fin[:, 7, :], in0=fin[:, 4, :], scalar=flag(14), in1=fin[:, 7, :], op0=ALU.mult, op1=ALU.add), reads=[b_pc], writes=[b_fin])
                    T.barrier()

            with contextlib.ExitStack() as stf:
                ya_f = sb(stf, "ya_f", [128, 4, NF], BF16); b_ya_f = Buf("ya_f")
                yb_f = sb(stf, "yb_f", [128, 4, NF], BF16); b_yb_f = Buf("yb_f")
                with contextlib.ExitStack() as stx:
                    xa = sb(stx, "xa_f", [128, 4, NF + 4], BF16); b_xa = Buf("xa_f")
                    cv = sb(stx, "cv_f", [128, 4, NF + 2], BF16); b_cv = Buf("cv_f")
                    T.op("dve", lambda: nc.vector.memset(xa[:, :, 0:2], 0.0), writes=[b_xa])
                    T.op("dve", lambda: nc.vector.memset(xa[:, :, NF + 2:NF + 4], 0.0), writes=[b_xa])
                    T.op("dve", lambda: nc.vector.memset(cv[:, :, 0:1], 0.0), writes=[b_cv])
                    T.op("dve", lambda: nc.vector.memset(cv[:, :, NF + 1:NF + 2], 0.0), writes=[b_cv])
                    with contextlib.ExitStack() as stA:
                        phase_A(stA, xT_full, mk_full, NF, 0, xa, b_xa, cv, b_cv, ya_f, b_ya_f, yb_f, b_yb_f)
                        T.barrier()
                    with contextlib.ExitStack() as stB:
                        phase_BC(stB, NF, xa, b_xa, cv, b_cv, ya_f, b_ya_f, yb_f, b_yb_f, pfF, b_pfF,
                                 lambda g: fin[:, 6, g:g + 1], lambda g: fin[:, 7, g:g + 1], None, None, 2, NF - 2)
                        T.barrier()

                with contextlib.ExitStack() as st:
                    w_out = sb(st, "rc_w_out", [128, KC, D], BF16); b_wout = Buf("rc_w_out")
                    T.dma("pool", w_out[:], rc_w_out.rearrange("(kc p) f -> p kc f", p=128), writes=[b_wout], cast=True)
                    xt2 = [sb(st, f"dx{i}", [128, KC, 512]) for i in range(2)]; bxt2 = [Buf("dx0"), Buf("dx1")]
                    regions = [(xT_full, c0, n, 0, c0, ya_f, yb_f, c0) for (c0, n) in tiles_of(NF)] + [(xT_ctx, 0, CTX, 1, NF, ya_c, yb_c, 0)]
                    b_xmid = Buf("xmid")
                    for ti, (xsrc, c0, n, j, wc, ya, yb, yc0) in enumerate(regions):
                        xt, bx = xt2[ti % 2], bxt2[ti % 2]
                        T.dma("sp", xt[:, :, :n], xsrc[:, :, c0:c0 + n], writes=[bx])
                        for m in range(KC):
                            ps, bps = next_ps()
                            for kc in range(KC):
                                src = ya if kc < 4 else yb
                                T.op("pe", lambda: nc.tensor.matmul(ps[:, :n], w_out[:, kc, m * 128:(m + 1) * 128], src[:, kc % 4, yc0:yc0 + n],
                                                                    start=(kc == 0), stop=(kc == KC - 1)),
                                     reads=[b_wout, b_ya_f, b_yb_f, b_ya_c, b_yb_c], writes=[bps], signal=(kc == KC - 1))
                            T.op("dve", lambda: nc.vector.scalar_tensor_tensor(
                                out=xt[:, m, :n], in0=ps[:, :n], scalar=modv[:, 0, 2 * 8 + m, j:j + 1], in1=xt[:, m, :n], op0=ALU.mult, op1=ALU.add),
                                reads=[bps, b_modv], writes=[bx])
                        T.dma("sp", xmid[:, :, wc:wc + n], xt[:, :, :n], reads=[bx])
                    T.barrier()

        def ffn_phase(layer, src_dram, dst_dram, regions, final_norm):
            with contextlib.ExitStack() as st:
                wi = sb(st, "ffn_wi", [128, KC, 2 * DFF], BF16); b_wi = Buf("ffn_wi")
                wo = sb(st, "ffn_wo", [128, FT, D], BF16); b_wo = Buf("ffn_wo")
                for h in range(4):
                    T.dma("pool", wi[:, :, h * 1408:(h + 1) * 1408], ffn_w_in[layer].rearrange("(kc p) f -> p kc f", p=128)[:, :, h * 1408:(h + 1) * 1408], writes=[b_wi], cast=True)
                for h in range(2):
                    T.dma("pool", wo[:, h * 11:(h + 1) * 11, :], ffn_w_out[layer].rearrange("(kc p) f -> p kc f", p=128)[:, h * 11:(h + 1) * 11, :], writes=[b_wo], cast=True)
                NT = 256
                xt2 = [sb(st, f"fx{i}", [128, KC, NT]) for i in range(2)]; bxt2 = [Buf("fx0"), Buf("fx1")]
                hb = sb(st, "fh", [128, KC, NT], BF16); b_hb = Buf("fh")
                act = sb(st, "fact", [128, FT, NT], BF16); b_act = Buf("fact")
                sg2 = [sb(st, f"fsg{i}", [128, NT]) for i in range(2)]; bsg2 = [Buf("fsg0"), Buf("fsg1")]
                nt = norm_tiles(st, NT, "f")
                b_dst = Buf("ffn_dst")
                tiles = []
                for (c0, n, j) in regions:
                    tiles += [(c0 + t0, tn, j) for (t0, tn) in tiles_of(n, NT)]
                for ti, (c0, n, j) in enumerate(tiles):
                    xt, bx = xt2[ti % 2], bxt2[ti % 2]
                    T.dma("sp", xt[:, :, :n], src_dram[:, :, c0:c0 + n], writes=[bx])
                    ni = 1 + 2 * layer
                    rmsnorm_mod(nt, xt, bx, n, lambda ch: nA[:, ni, ch, j:j + 1], lambda ch: modv[:, layer, 3 * 8 + ch, j:j + 1], hb, b_hb)
                    for f in range(FT):
                        psg, bpsg = next_ps()
                        psu, bpsu = next_ps()
                        for kc in range(KC):
                            T.op("pe", lambda kc=kc, f=f, psg=psg: nc.tensor.matmul(psg[:, :n], wi[:, kc, f * 128:(f + 1) * 128], hb[:, kc, :n], start=(kc == 0), stop=(kc == KC - 1)),
                                 reads=[b_wi, b_hb], writes=[bpsg], signal=(kc == KC - 1))
                        for kc in range(KC):
                            T.op("pe", lambda kc=kc, f=f, psu=psu: nc.tensor.matmul(psu[:, :n], wi[:, kc, DFF + f * 128:DFF + (f + 1) * 128], hb[:, kc, :n], start=(kc == 0), stop=(kc == KC - 1)),
                                 reads=[b_wi, b_hb], writes=[bpsu], signal=(kc == KC - 1))
                        sg, bsg = sg2[f % 2], bsg2[f % 2]
                        T.op("act", lambda psg=psg, sg=sg: nc.scalar.activation(out=sg[:, :n], in_=psg[:, :n], func=AF.Silu), reads=[bpsg], writes=[bsg])
                        T.op("dve", lambda psu=psu, sg=sg, f=f: nc.vector.tensor_tensor(out=act[:, f, :n], in0=psu[:, :n], in1=sg[:, :n], op=ALU.mult),
                             reads=[bpsu, bsg], writes=[b_act])
                    for m in range(KC):
                        ps, bps = next_ps()
                        for f in range(FT):
                            T.op("pe", lambda ps=ps, f=f, m=m: nc.tensor.matmul(ps[:, :n], wo[:, f, m * 128:(m + 1) * 128], act[:, f, :n], start=(f == 0), stop=(f == FT - 1)),
                                 reads=[b_wo, b_act], writes=[bps], signal=(f == FT - 1))
                        T.op("dve", lambda ps=ps, m=m, xt=xt: nc.vector.scalar_tensor_tensor(
                            out=xt[:, m, :n], in0=ps[:, :n], scalar=modv[:, layer, 5 * 8 + m, j:j + 1], in1=xt[:, m, :n], op0=ALU.mult, op1=ALU.add),
                            reads=[bps, b_modv], writes=[bx])
                    if final_norm:
                        sq, bsq, rs, brs, tt, btt = nt
                        T.op("act", lambda xt=xt: nc.scalar.activation(out=sq[:, :, :n], in_=xt[:, :, :n], func=AF.Square), reads=[bx], writes=[bsq])
                        ps, bps = next_ps()
                        for kc in range(KC):
                            T.op("pe", lambda kc=kc, ps=ps: nc.tensor.matmul(ps[:, :n], ones_bf[:], sq[:, kc, :n], start=(kc == 0), stop=(kc == KC - 1)),
                                 reads=[bsq, b_ones], writes=[bps], signal=(kc == KC - 1))
                        T.op("act", lambda ps=ps: nc.scalar.activation(out=rs[:, :n], in_=ps[:, :n], func=AF.Sqrt, scale=1.0 / D, bias=EPS), reads=[bps], writes=[brs])
                        T.op("dve", lambda: nc.vector.reciprocal(out=rs[:, :n], in_=rs[:, :n]), writes=[brs])
                        for ch in range(KC):
                            T.op("dve", lambda ch=ch, xt=xt: nc.vector.scalar_tensor_tensor(
                                out=xt[:, ch, :n], in0=xt[:, ch, :n], scalar=sm[:, S_GFIN + ch:S_GFIN + ch + 1], in1=rs[:, :n], op0=ALU.mult, op1=ALU.mult),
                                reads=[brs, b_sm], writes=[bx])
                    T.dma("sp", dst_dram[:, :, c0:c0 + n], xt[:, :, :n], reads=[bx])
                T.barrier()

        ffn_phase(0, xmid, xl1, [(0, NF, 0), (NF, CTX, 1)], False)

        T.mute = False
        NKEY = OWN + 256 + CTX
        NBLK = NKEY // 128
        with _StopGuard(), contextlib.ExitStack() as L1:
            qT = sb(L1, "qT", [128, KC, OWN], BF16); b_qT = Buf("qT")
            kTe = sb(L1, "kTe", [128, 4, NKEY], BF16); kTo = sb(L1, "kTo", [128, 4, NKEY], BF16); b_kT = Buf("kT")
            T.op("dve", lambda: nc.vector.memset(kTe[:].rearrange("p a b -> p (a b)"), 0.0), writes=[b_kT])
            T.op("dve", lambda: nc.vector.memset(kTo[:].rearrange("p a b -> p (a b)"), 0.0), writes=[b_kT])
            vv = sb(L1, "vv", [128, NBLK, 4, 2, 128], BF16); b_vv = Buf("vv")
            T.op("dve", lambda: nc.vector.memset(vv[:].rearrange("p a b c d -> p (a b c d)"), 0.0), writes=[b_vv])
            b_xl1 = Buf("xl1r")
            with contextlib.ExitStack() as st:
                wq = sb(st, "wq", [128, KC, D], BF16); b_wq = Buf("wq")
                wkd = sb(st, "wkd", [128, KC, 4, 2, 64], BF16); b_wkd = Buf("wkd")
                wv = sb(st, "wv", [128, KC, 256], BF16); b_wv = Buf("wv")
                qkv_v = at_w_qkv.rearrange("(kc p) f -> p kc f", p=128)
                T.dma("pool", wq[:], qkv_v[:, :, 0:1024], writes=[b_wq], cast=True)
                for jh in range(4):
                    for dup in range(2):
                        T.dma("pool", wkd[:, :, jh, dup, :], qkv_v[:, :, 1024 + jh * 64:1024 + (jh + 1) * 64], writes=[b_wkd], cast=True)
                T.dma("pool", wv[:], qkv_v[:, :, 1280:1536], writes=[b_wv], cast=True)
                NT = 256
                xt2 = [sb(st, f"px{i}", [128, KC, NT]) for i in range(1)] * 2; bxt2 = [Buf("px0")] * 2
                rk2 = [sb(st, f"prk{i}", [128, 2, NT]) for i in range(1)] * 2; brk2 = [Buf("prk0")] * 2
                rq2 = [sb(st, f"prq{i}", [128, 2, NT]) for i in range(1)] * 2; brq2 = [Buf("prq0")] * 2
                hb = sb(st, "ph", [128, KC, NT], BF16); b_hb = Buf("ph")
                xb2 = [sb(st, f"pxb{i}", [128, NT], BF16) for i in range(2)]; bxb2 = [Buf("pxb0"), Buf("pxb1")]
                t12 = [sb(st, f"pt1{i}", [128, NT]) for i in range(2)]; bt12 = [Buf("pt10"), Buf("pt11")]
                t22 = [sb(st, f"pt2{i}", [128, NT]) for i in range(2)]; bt22 = [Buf("pt20"), Buf("pt21")]
                nt = norm_tiles(st, NT, "p")
                rcnt = [0]

                def rope_evac(ps, bps, n, tab, btab, tcol, dst_ap, bdst, dst_halves=None):
                    i = rcnt[0] % 2
                    rcnt[0] += 1
                    xb, bxb, t1, bt1, t2, bt2 = xb2[i], bxb2[i], t12[i], bt12[i], t22[i], bt22[i]
                    T.op("act", lambda: nc.scalar.activation(out=xb[:, :n], in_=ps[:, :n], func=AF.Identity), reads=[bps], writes=[bxb])
                    ps2, bps2 = next_ps()
                    T.op("pe", lambda: nc.tensor.matmul(ps2[:, :n], ropeP, xb[:, :n], start=True, stop=True), reads=[bxb, b_cst], writes=[bps2])
                    T.op("dve", lambda: nc.vector.tensor_tensor(out=t1[:, :n], in0=ps[:, :n], in1=tab[:, 0, tcol:tcol + n], op=ALU.mult), reads=[bps, btab], writes=[bt1])
                    T.op("dve", lambda: nc.vector.tensor_tensor(out=t2[:, :n], in0=ps2[:, :n], in1=tab[:, 1, tcol:tcol + n], op=ALU.mult), reads=[bps2, btab], writes=[bt2])
                    if dst_halves is None:
                        T.op("dve", lambda: nc.vector.tensor_tensor(out=dst_ap, in0=t1[:, :n], in1=t2[:, :n], op=ALU.add), reads=[bt1, bt2], writes=[bdst])
                    else:
                        for (r0, dap) in dst_halves:
                            T.op("dve", lambda: nc.vector.tensor_tensor(out=dap, in0=t1[r0:r0 + 64, :n], in1=t2[r0:r0 + 64, :n], op=ALU.add), reads=[bt1, bt2], writes=[bdst])

                p1_tiles = NKEY // NT
                if stop is not None and stop.startswith("p1:"):
                    p1_tiles = int(stop.split(":")[1])
                    stop = "p1"
                for ti in range(p1_tiles):
                    k0 = ti * NT
                    is_ctx = k0 >= OWN + 256
                    j = 1 if is_ctx else 0
                    src0 = (2308 + (k0 - 2304)) if is_ctx else (k0 + 2)
                    xt, bx = xt2[ti % 2], bxt2[ti % 2]
                    rk, brk = rk2[ti % 2], brk2[ti % 2]
                    rq, brq = rq2[ti % 2], brq2[ti % 2]
                    T.dma("sp", xt[:], xl1[:, :, src0:src0 + NT], reads=[b_xl1], writes=[bx])
                    oa = max(k0, 128) - k0
                    ob = min(k0 + NT, 128 + OWN) - k0
                    has_q = (not is_ctx) and ob > oa
                    if not is_ctx:
                        T.dma("sp", rk[:], rope_k[:, :, k0:k0 + NT], writes=[brk])
                    if has_q:
                        qa = k0 + oa - 128
                        T.dma("sp", rq[:, :, oa:ob], rope_q[:, :, qa:qa + (ob - oa)], writes=[brq])
                    rmsnorm_mod(nt, xt, bx, NT, lambda ch: nA[:, 2, ch, j:j + 1], lambda ch: modv[:, 1, 0 * 8 + ch, j:j + 1], hb, b_hb)
                    for jh in range(0 if "K" in _SKIP else 4):
                        ps, bps = next_ps()
                        for kc in range(KC):
                            T.op("pe", lambda: nc.tensor.matmul(ps[:, :NT], wkd[:, kc, jh].rearrange("p a b -> p (a b)"), hb[:, kc, :],
                                                                start=(kc == 0), stop=(kc == KC - 1)),
                                 reads=[b_wkd, b_hb], writes=[bps], signal=(kc == KC - 1))
                        halves = [(0, kTe[0:64, jh, k0:k0 + NT]), (64, kTo[64:128, jh, k0:k0 + NT])]
                        if is_ctx:
                            for (r0, dap) in halves:
                                T.op("act", lambda: nc.scalar.activation(out=dap, in_=ps[r0:r0 + 64, :NT], func=AF.Identity), reads=[bps], writes=[b_kT])
                        else:
                            rope_evac(ps, bps, NT, rk, brk, 0, None, b_kT, dst_halves=halves)
                    if has_q and "Q" not in _SKIP:
                        nq = ob - oa
                        for m in range(KC):
                            ps, bps = next_ps()
                            for kc in range(KC):
                                T.op("pe", lambda: nc.tensor.matmul(ps[:, :nq], wq[:, kc, m * 128:(m + 1) * 128], hb[:, kc, oa:ob],
                                                                    start=(kc == 0), stop=(kc == KC - 1)),
                                     reads=[b_wq, b_hb], writes=[bps], signal=(kc == KC - 1))
                            rope_evac(ps, bps, nq, rq, brq, oa, qT[:, m, qa:qa + nq], b_qT)
                    for bi in range(0 if "V" in _SKIP else NT // 128):
                        blk = k0 // 128 + bi
                        ps, bps = next_ps()
                        for kc in range(KC):
                            T.op("pe", lambda: nc.tensor.matmul(ps[:, :256], hb[:, kc, bi * 128:(bi + 1) * 128], wv[:, kc, :],
                                                                start=(kc == 0), stop=(kc == KC - 1)),
                                 reads=[b_wv, b_hb], writes=[bps], signal=(kc == KC - 1))
                        T.op("act", lambda: nc.scalar.activation(out=vv[:, blk, :, 0, 0:64], in_=ps[:, 0:256].rearrange("p (a b) -> p a b", b=64), func=AF.Identity),
                             reads=[bps], writes=[b_vv])
                        T.op("dve", lambda: nc.vector.tensor_copy(out=vv[:, blk, :, 1, 64:128], in_=ps[:, 0:256].rearrange("p (a b) -> p a b", b=64)),
                             reads=[bps], writes=[b_vv])
                T.barrier()

            with contextlib.ExitStack() as st:
                oT = sb(st, "oT", [128, KC, OWN], BF16); b_oT = Buf("oT")
                with contextlib.ExitStack() as st2:
                    amb = sb(st2, "amb", [128, 4, 512], BF16); b_amb = Buf("amb")
                    for mi in range(4):
                        T.dma("pool", amb[:, mi, :], amask[:, mi, :], writes=[b_amb], cast=True)
                    ones_eo = sb(st2, "ones_eo", [128, 2, 128], BF16); b_oeo = Buf("ones_eo")
                    T.op("dve", lambda: nc.vector.memset(ones_eo[:].rearrange("p a b -> p (a b)"), 0.0), writes=[b_oeo])
                    T.op("dve", lambda: nc.vector.memset(ones_eo[:, 0, 0:64], 1.0), writes=[b_oeo])
                    T.op("dve", lambda: nc.vector.memset(ones_eo[:, 1, 64:128], 1.0), writes=[b_oeo])
                    es16 = sb(st2, "es16", [128, 16]); b_es = Buf("es16")
                    T.op("act", lambda: nc.scalar.activation(out=es16[:], in_=pcs[:, P_SINK:P_SINK + 16], func=AF.Exp), reads=[b_pc], writes=[b_es])
                    onesf = sb(st2, "onesf", [128, 128]); b_onesf = Buf("onesf")
                    T.op("dve", lambda: nc.vector.memset(onesf[:], 1.0), writes=[b_onesf])
                    esk = sb(st2, "esk", [128, 4, 256]); b_esk = Buf("esk")
                    for jh in range(4):
                        for g in range(4):
                            r0 = (g % 2) * 64
                            c0 = (g // 2) * 128
                            T.op("dve", lambda: nc.vector.tensor_scalar(out=esk[r0:r0 + 64, jh, c0:c0 + 128], in0=onesf[r0:r0 + 64, :],
                                                                        scalar1=es16[r0:r0 + 64, 4 * jh + g:4 * jh + g + 1], scalar2=None, op0=ALU.mult),
                                 reads=[b_es, b_onesf], writes=[b_esk])
                    pT2 = [sb(st2, f"pT{i}", [128, 5, 512], BF16) for i in range(2)]; bpT2 = [Buf("pT0"), Buf("pT1")]
                    dn2 = [sb(st2, f"dn{i}", [128, 256]) for i in range(2)]; bdn2 = [Buf("dn0"), Buf("dn1")]
                    units = [(i, jh) for i in range(16) for jh in range(4)]

                    def key_blocks(i):
                        return [(i, 2 if i == 0 else 0), (i + 1, None), (i + 2, 3 if i == 15 else 1), (18, None), (19, None)]

                    def emit_scores(u):
                        i, jh = units[u]
                        pT, bpT = pT2[u % 2], bpT2[u % 2]
                        qc = slice(i * 128, (i + 1) * 128)
                        for kb, (blk, mk) in enumerate(key_blocks(i)):
                            ps, bps = next_ps()
                            kc_ = slice(blk * 128, (blk + 1) * 128)
                            first = True
                            if mk is not None:
                                T.op("pe", lambda: nc.tensor.matmul(ps[:, :], ident, amb[:, mk, :], start=True, stop=False),
                                     reads=[b_cst, b_amb], writes=[bps], signal=False)
                                first = False
                            T.op("pe", lambda: nc.tensor.matmul(ps[:, 0:256], kTe[:, jh, kc_], qT[:, 2 * jh:2 * jh + 2, qc], start=first, stop=False),
                                 reads=[b_kT, b_qT], writes=[bps], signal=False)
                            T.op("pe", lambda: nc.tensor.matmul(ps[:, 256:512], kTo[:, jh, kc_], qT[:, 2 * jh:2 * jh + 2, qc], start=False, stop=True),
                                 reads=[b_kT, b_qT], writes=[bps])
                            T.op("act", lambda: nc.scalar.activation(out=pT[:, kb, :], in_=ps[:, :], func=AF.Exp), reads=[bps], writes=[bpT])

                    def emit_pv(u):
                        i, jh = units[u]
                        pT, bpT = pT2[u % 2], bpT2[u % 2]
                        dn, bdn = dn2[u % 2], bdn2[u % 2]
                        blks = [b_ for (b_, _) in key_blocks(i)]
                        psO, bpsO = next_ps()
                        for kb, blk in enumerate(blks):
                            for par in range(2):
                                last = (kb == 4 and par == 1)
                                T.op("pe", lambda: nc.tensor.matmul(psO[:, 0:256], vv[:, blk, jh, par, :], pT[:, kb, par * 256:(par + 1) * 256],
                                                                    start=(kb == 0 and par == 0), stop=last),
                                     reads=[b_vv, bpT], writes=[bpsO], signal=last)
                        psD, bpsD = next_ps()
                        for kb, blk in enumerate(blks):
                            for par in range(2):
                                last = (kb == 4 and par == 1)
                                T.op("pe", lambda: nc.tensor.matmul(psD[:, 0:256], ones_eo[:, par, :], pT[:, kb, par * 256:(par + 1) * 256],
                                                                    start=(kb == 0 and par == 0), stop=last),
                                     reads=[b_oeo, bpT], writes=[bpsD], signal=last)
                        T.op("dve", lambda: nc.vector.tensor_tensor(out=dn[:], in0=psD[:, 0:256], in1=esk[:, jh, :], op=ALU.add), reads=[bpsD, b_esk], writes=[bdn])
                        T.op("dve", lambda: nc.vector.reciprocal(out=dn[:], in_=dn[:]), writes=[bdn])
                        T.op("dve", lambda: nc.vector.tensor_tensor(out=oT[:, 2 * jh:2 * jh + 2, i * 128:(i + 1) * 128],
                                                                    in0=psO[:, 0:256].rearrange("p (a b) -> p a b", b=128),
                                                                    in1=dn[:].rearrange("p (a b) -> p a b", b=128), op=ALU.mult),
                             reads=[bpsO, bdn], writes=[b_oT])

                    nun = 0 if stop == "p1" else len(units)
                    if nun:
                        emit_scores(0)
                    for u in range(nun):
                        if u + 1 < nun:
                            emit_scores(u + 1)
                        emit_pv(u)
                    T.barrier()

                with contextlib.ExitStack() as st3:
                    w_o = sb(st3, "at_w_o", [128, KC, D], BF16); b_wo_ = Buf("at_w_o")
                    T.dma("pool", w_o[:], at_w_out.rearrange("(kc p) f -> p kc f", p=128), writes=[b_wo_], cast=True)
                    NT = 256
                    xt2 = [sb(st3, f"ox{i}", [128, KC, NT]) for i in range(2)]; bxt2 = [Buf("ox0"), Buf("ox1")]
                    b_xm1 = Buf("xmid1")
                    for ti, (c0, n) in enumerate(tiles_of(OWN, NT) if stop not in ("p1", "p2") else []):
                        xt, bx = xt2[ti % 2], bxt2[ti % 2]
                        T.dma("sp", xt[:, :, :n], xl1[:, :, 130 + c0:130 + c0 + n], reads=[b_xl1], writes=[bx])
                        for m in range(KC):
                            ps, bps = next_ps()
                            for kc in range(KC):
                                T.op("pe", lambda: nc.tensor.matmul(ps[:, :n], w_o[:, kc, m * 128:(m + 1) * 128], oT[:, kc, c0:c0 + n],
                                                                    start=(kc == 0), stop=(kc == KC - 1)),
                                     reads=[b_wo_, b_oT], writes=[bps], signal=(kc == KC - 1))
                            T.op("dve", lambda: nc.vector.scalar_tensor_tensor(
                                out=xt[:, m, :n], in0=ps[:, :n], scalar=modv[:, 1, 2 * 8 + m, 0:1], in1=xt[:, m, :n], op0=ALU.mult, op1=ALU.add),
                                reads=[bps, b_modv], writes=[bx])
                        T.dma("sp", xmid1[:, :, c0:c0 + n], xt[:, :, :n], reads=[bx])
                    T.barrier()

        if stop is None:
            ffn_phase(1, xmid1, outT, [(0, OWN, 0)], True)

        fin_toks = []
        for sem, val in T.dsems:
            if val > 0:
                fin_toks.append((sem, val))
        T._wait(T.E["sp"], fin_toks)
    return nc


def _fm(a):
    T_ = a.shape[0]
    return np.ascontiguousarray(a.reshape(T_, KC, 128).transpose(2, 1, 0))


def _chan(v):
    return np.ascontiguousarray(v.reshape(4, 128).T)


def _blockdiag(w):
    out = np.zeros((128, 4, 128), np.float32)
    for g in range(4):
        for hh in range(2):
            out[hh * 64:(hh + 1) * 64, g, hh * 64:(hh + 1) * 64] = w[2 * g + hh]
    return out


def prepare_inputs(inp):
    f32 = np.float32
    x = np.asarray(inp["x"], f32); c = np.asarray(inp["c"], f32); ctx = np.asarray(inp["ctx"], f32)
    c_ctx = np.asarray(inp["c_ctx"], f32)
    shared = {}
    for k in ["ada_w", "ffn_w_in", "ffn_w_out"]:
        shared[k] = np.ascontiguousarray(np.asarray(inp[k], f32))
    shared["rc_w_in"] = np.ascontiguousarray(np.asarray(inp["rc_w_in"], f32)[0])
    shared["rc_w_out"] = np.ascontiguousarray(np.asarray(inp["rc_w_out"], f32)[0])
    shared["at_w_qkv"] = np.ascontiguousarray(np.asarray(inp["at_w_qkv"], f32)[0])
    shared["at_w_out"] = np.ascontiguousarray(np.asarray(inp["at_w_out"], f32)[0])

    sm = np.zeros((128, S_NS), f32)
    ada_b = np.asarray(inp["ada_b"], f32)
    for l in range(2):
        sm[:, S_ADAB + l * 48:S_ADAB + (l + 1) * 48] = ada_b[l].reshape(48, 128).T
        sm[:, S_GMIX + l * 8:S_GMIX + (l + 1) * 8] = np.asarray(inp["norm_mix_g"], f32)[l].reshape(8, 128).T
        sm[:, S_GFFN + l * 8:S_GFFN + (l + 1) * 8] = np.asarray(inp["norm_ffn_g"], f32)[l].reshape(8, 128).T
    sm[:, S_GFIN:S_GFIN + 8] = np.asarray(inp["norm_final_g"], f32).reshape(8, 128).T
    sm[:, S_CAB:S_CAB + 4] = _chan(np.asarray(inp["rc_conv_a_b"], f32)[0])
    cbw = np.asarray(inp["rc_conv_b_w"], f32)[0]
    for g in range(4):
        for o in range(3):
            sm[:, S_CBW + g * 3 + o] = cbw[o, g * 128:(g + 1) * 128]
    shared["smalls"] = sm

    cst = np.zeros((128, 384), f32)
    cst[:, 0:128] = np.eye(128, dtype=f32)
    P = np.zeros((128, 128), f32)
    for m in range(128):
        d = m % 64
        base = m - d
        within = d % 32
        if within < 16:
            P[base + (d + 16), m] = -1.0
        else:
            P[base + (d - 16), m] = 1.0
    cst[:, 128:256] = P
    shared["consts"] = cst

    caw = np.asarray(inp["rc_conv_a_w"], f32)[0]
    r_w = np.asarray(inp["rc_gate_r_w"], f32)[0]; r_b = np.asarray(inp["rc_gate_r_b"], f32)[0]
    i_w = np.asarray(inp["rc_gate_i_w"], f32)[0]; i_b = np.asarray(inp["rc_gate_i_b"], f32)[0]
    lam = np.asarray(inp["rc_lambda"], f32)[0]
    sink = np.asarray(inp["at_sink"], f32)[0]

    inv_freq = (10000.0 ** (-np.arange(16, dtype=f32) / 16)).astype(f32)

    def rope_tab(tok, scale):
        tok = np.asarray(tok)
        row = (tok // 64).astype(f32); col = (tok % 64).astype(f32)
        ang = np.stack([row[:, None] * inv_freq, col[:, None] * inv_freq], axis=1).astype(f32)
        cos = np.cos(ang).astype(f32); sin = np.sin(ang).astype(f32)
        tab = np.zeros((128, 2, len(tok)), f32)
        for p in range(128):
            d = p % 64
            ax = d // 32; fr = d % 16
            tab[p, 0] = cos[:, ax, fr] * scale
            tab[p, 1] = sin[:, ax, fr] * scale
        return tab

    per_core = []
    for core in range(8):
        b, k = core // 4, core % 4
        m = {}
        t0 = 2048 * k - 130
        xf = np.zeros((NF, D), f32); mk = np.zeros((NF,), f32)
        lo = max(0, -t0); hi = min(NF, SEQ - t0)
        xf[lo:hi] = x[b, t0 + lo:t0 + hi]; mk[lo:hi] = 1.0
        m["xT_full"] = _fm(xf)
        m["mk_full"] = np.ascontiguousarray(np.broadcast_to(mk[None, :], (128, NF)))
        m["pf_full"] = np.ascontiguousarray((1.0 - mk)[None, :])
        m["xT_ctx"] = _fm(ctx[b])
        xc_ = np.zeros((NCARRY, D), f32); mkc = np.zeros((NCARRY,), f32); pfc = np.ones((NCARRY,), f32)
        if k >= 1:
            nfw = 2048 * k - 128
            xc_[126:126 + nfw] = x[b, 0:nfw]; mkc[126:126 + nfw] = 1.0; pfc[126:126 + nfw] = 0.0
            xc_[2048 * k - 2] = x[b, nfw]; mkc[2048 * k - 2] = 1.0
        if k <= 2:
            tlast = 2048 * (k + 1) + 128
            toks = np.arange(SEQ - 1, tlast - 1, -1)
            c0 = 2048 * k + 126
            xc_[c0:c0 + len(toks)] = x[b, toks]; mkc[c0:c0 + len(toks)] = 1.0; pfc[c0:c0 + len(toks)] = 0.0
            assert c0 + len(toks) == 6142
            xc_[6142] = x[b, tlast - 1]; xc_[6143] = x[b, tlast - 2]; mkc[6142:6144] = 1.0
        m["xT_carry"] = _fm(xc_)
        m["mk_carry"] = np.ascontiguousarray(np.broadcast_to(mkc[None, :], (128, NCARRY)))
        m["pf_carry"] = np.ascontiguousarray(pfc[None, :])
        pcv = np.zeros((128, P_NP), f32)
        pcv[:, P_CVEC:P_CVEC + 16] = np.stack([c[b].reshape(8, 128).T, c_ctx.reshape(8, 128).T], axis=-1).reshape(128, 16)
        dirs = [0, 1] + [0 if s < k else 1 for s in range(3)]
        natural = [True, True] + [s < k for s in range(3)]
        gw = np.zeros((128, 5, 2, 4, 128), f32)
        for idx in range(5):
            d = dirs[idx]
            gw[:, idx, 0] = _blockdiag(r_w[d]); gw[:, idx, 1] = _blockdiag(i_w[d])
            pcv[:, P_GB + idx * 8:P_GB + idx * 8 + 4] = _chan(r_b[d].reshape(512))
            pcv[:, P_GB + idx * 8 + 4:P_GB + idx * 8 + 8] = _chan(i_b[d].reshape(512))
            pcv[:, P_LAM + idx * 4:P_LAM + idx * 4 + 4] = _chan(lam[d])
            for g in range(4):
                for o5 in range(5):
                    o = o5 - 2
                    if natural[idx]:
                        j = o + 2
                    else:
                        j = 2 - o
                    if 0 <= j <= 3:
                        pcv[:, P_TAPS + (idx * 4 + g) * 5 + o5] = caw[j, g * 128:(g + 1) * 128]
        m["gw_all"] = gw.reshape(128, -1)
        fl = np.zeros(16, f32)
        for s in range(3):
            fwd = s < k
            if s == 0:
                fl[0 if fwd else 1] = 1.0
            elif fwd or s > k:
                fl[3 * s + 2] = 1.0
            else:
                fl[3 * s + 1] = 1.0
        if k == 0:
            fl[9] = 1.0
        else:
            fl[10 + (k - 1)] = 1.0
        if k == 3:
            fl[13] = 1.0
        else:
            fl[14] = 1.0
        pcv[:, P_FLAGS:P_FLAGS + 16] = fl[None, :]
        pcv[:, P_SINK:P_SINK + 16] = sink[None, :]
        m["pc"] = pcv
        m["rope_q"] = rope_tab(np.arange(2048 * k, 2048 * (k + 1)), 0.125)
        m["rope_k"] = rope_tab(np.clip(np.arange(2048 * k - 128, 2048 * (k + 1) + 128), 0, SEQ - 1), 1.0)
        am = np.zeros((128, 4, 512), f32)
        jj = np.arange(128)[:, None]; ii = np.arange(128)[None, :]
        prev = np.where(jj >= ii, 0.0, NEGBIG).astype(f32)
        nxt = np.where(jj <= ii, 0.0, NEGBIG).astype(f32)
        am[:, 0] = np.tile(prev, (1, 4)); am[:, 1] = np.tile(nxt, (1, 4))
        am[:, 2] = NEGBIG if k == 0 else am[:, 0]
        am[:, 3] = NEGBIG if k == 3 else am[:, 1]
        m["amask"] = am
        m.update(shared)
        per_core.append(m)
    return per_core


_NC_CACHE = {}


def kernel(**inputs):
    if "nc" not in _NC_CACHE:
        _NC_CACHE["nc"] = build_nc(False)
    nc = _NC_CACHE["nc"]
    in_maps = prepare_inputs(inputs)
    res = run_bass_kernel_spmd(nc, in_maps, core_ids=list(range(8)))
    out = np.zeros((2, SEQ, D), np.float32)
    for core in range(8):
        b, k = core // 4, core % 4
        oT = np.asarray(res.results[core]["outT"])
        out[b, 2048 * k:2048 * (k + 1)] = oT.transpose(2, 1, 0).reshape(OWN, D)
    return out
```

```python
import contextlib
import numpy as np
import concourse.bass as bass
import concourse.mybir as mybir
from concourse.bass_utils import run_bass_kernel_spmd

F32 = mybir.dt.float32
BF16 = mybir.dt.bfloat16
AF = mybir.ActivationFunctionType
ALU = mybir.AluOpType

D = 1024
KC = 8
SEQ = 8192
OWN = 2048
CTX = 256
NF = 2308
NCARRY = 6144
DFF = 2816
FT = 22
EPS = 1e-6
NEGBIG = -30000.0

S_ADAB = 0
S_GMIX = S_ADAB + 96
S_GFFN = S_GMIX + 16
S_GFIN = S_GFFN + 16
S_CAB = S_GFIN + 8
S_CBW = S_CAB + 4
S_NS = S_CBW + 12
P_CVEC = 0
P_TAPS = P_CVEC + 16
P_GB = P_TAPS + 100
P_LAM = P_GB + 40
P_FLAGS = P_LAM + 20
P_SINK = P_FLAGS + 16
P_NP = P_SINK + 16


class _Stop(Exception):
    pass


class _StopGuard:
    def __enter__(self):
        return self

    def __exit__(self, et, ev, tb):
        return et is _Stop


class Buf:
    __slots__ = ("name", "w", "r", "excl")

    def __init__(self, name, excl=False):
        self.name = name
        self.w = None
        self.r = {}
        self.excl = excl


class Eng:
    def __init__(self, name, h, sems):
        self.name, self.h, self.sems = name, h, sems
        self.n = 0
        self.waited = {}
        self.pw = []
        self.pr = []

    def tok(self):
        if self.n == 0:
            return None
        return (self.sems[(self.n - 1) // 30000], (self.n - 1) % 30000 + 1)


class Tracker:
    def __init__(self, nc, es):
        self.nc = nc
        mk = lambda nm: es.enter_context(nc.semaphore(nm))
        self.E = {
            "pe": Eng("pe", nc.tensor, [mk("pe0"), mk("pe1")]),
            "act": Eng("act", nc.scalar, [mk("act0"), mk("act1")]),
            "dve": Eng("dve", nc.vector, [mk("dve0"), mk("dve1")]),
            "pool": Eng("pool", nc.gpsimd, [mk("pool0")]),
            "sp": Eng("sp", nc.sync, [mk("sp0")]),
        }
        self.dsems = [[mk(f"dma{i}"), 0] for i in range(36)]
        self.di = {"hw": 0, "sw": 0}

    def _wait(self, e, toks):
        best = {}
        for sem, val in toks:
            k = id(sem)
            if k not in best or best[k][1] < val:
                best[k] = (sem, val)
        for k, (sem, val) in best.items():
            if e.waited.get(k, 0) >= val:
                continue
            e.h.wait_ge(sem, val)
            e.waited[k] = val

    @staticmethod
    def _deps(reads, writes):
        toks = []
        for b in reads:
            if b.w is not None:
                toks.append(b.w)
            if b.excl:
                toks.extend(b.r.values())
        for b in writes:
            if b.w is not None:
                toks.append(b.w)
            toks.extend(b.r.values())
        return toks

    mute = False

    def op(self, en, fn, reads=(), writes=(), signal=True):
        if self.mute:
            return None
        e = self.E[en]
        toks = self._deps(reads, writes)
        if en == "pe":
            own = {id(s) for s in e.sems}
            toks = [t for t in toks if id(t[0]) not in own]
        self._wait(e, toks)
        inst = fn()
        e.pr.extend(reads)
        e.pw.extend(writes)
        if signal:
            e.n += 1
            tok = e.tok()
            inst.then_inc(tok[0], 1)
            wset = {id(b) for b in e.pw}
            for b in e.pw:
                b.w = tok
                b.r = {}
            for b in e.pr:
                if id(b) not in wset:
                    b.r[id(tok[0])] = tok
            e.pw, e.pr = [], []
        return inst

    def dma(self, en, out_ap, in_ap, reads=(), writes=(), cast=False):
        if self.mute:
            return
        e = self.E[en]
        if en == "pool":
            slot = self.dsems[24 + self.di["sw"]]
            self.di["sw"] = (self.di["sw"] + 1) % 12
        else:
            slot = self.dsems[self.di["hw"]]
            self.di["hw"] = (self.di["hw"] + 1) % 24
        sem, val = slot
        toks = self._deps(reads, writes)
        if val > 0:
            toks.append((sem, val))
        self._wait(e, toks)
        e.h.dma_start(out=out_ap, in_=in_ap).then_inc(sem, 16)
        slot[1] = val + 16
        tok = (sem, val + 16)
        for b in writes:
            b.w = tok
            b.r = {}
        for b in reads:
            b.r[id(sem)] = tok

    def barrier(self):
        toks = []
        for e in self.E.values():
            t = e.tok()
            if t is not None:
                toks.append(t)
        for sem, val in self.dsems:
            if val > 0:
                toks.append((sem, val))
        for e in self.E.values():
            self._wait(e, toks)


import itertools
import os
_uid = itertools.count()
_SKIP = os.environ.get("K_SKIP", "")


def build_nc(debug=False, stop=None):
    nc = bass.Bass("TRN2", target_bir_lowering=False)
    es = contextlib.ExitStack()

    def din(name, shape, dt=F32):
        return nc.dram_tensor(name, list(shape), dt, kind="ExternalInput").ap()

    def dscr(name, shape, dt=F32, out=False):
        kind = "ExternalOutput" if out else "Internal"
        return nc.dram_tensor(name, list(shape), dt, kind=kind).ap()

    xT_full = din("xT_full", [128, KC, NF])
    xT_ctx = din("xT_ctx", [128, KC, CTX])
    xT_carry = din("xT_carry", [128, KC, NCARRY])
    mk_full = din("mk_full", [128, NF])
    mk_carry = din("mk_carry", [128, NCARRY])
    pf_full = din("pf_full", [1, NF])
    pf_carry = din("pf_carry", [1, NCARRY])
    smalls = din("smalls", [128, S_NS])
    pc = din("pc", [128, P_NP])
    gw_all = din("gw_all", [128, 5 * 2 * 4 * 128])
    ada_w = din("ada_w", [2, D, 6 * D])
    ffn_w_in = din("ffn_w_in", [2, D, 2 * DFF])
    ffn_w_out = din("ffn_w_out", [2, DFF, D])
    rc_w_in = din("rc_w_in", [D, 2560])
    rc_w_out = din("rc_w_out", [D, D])
    at_w_qkv = din("at_w_qkv", [D, 1536])
    at_w_out = din("at_w_out", [D, D])
    rope_q = din("rope_q", [128, 2, OWN])
    rope_k = din("rope_k", [128, 2, OWN + 256])
    amask = din("amask", [128, 4, 512])
    consts = din("consts", [128, 3 * 128])

    xmid = dscr("xmid", [128, KC, NF + CTX], out=debug)
    xl1 = dscr("xl1", [128, KC, NF + CTX], out=debug)
    xmid1 = dscr("xmid1", [128, KC, OWN])
    outT = dscr("outT", [128, KC, OWN], out=True)
    modv_dbg = dscr("modv_dbg", [128, 192], out=True) if debug else None

    with es:
        T = Tracker(nc, es)
        sb = lambda st, name, shape, dt=F32: st.enter_context(nc.sbuf_tensor("s_" + name + "_" + str(next(_uid)), list(shape), dt))

        sm = sb(es, "sm", [128, S_NS]); b_sm = Buf("sm")
        pcs = sb(es, "pcs", [128, P_NP]); b_pc = Buf("pc")
        cst = sb(es, "cst", [128, 384], BF16); b_cst = Buf("cst")
        modv = sb(es, "modv", [128, 2, 48, 2]); b_modv = Buf("modv")
        nA = sb(es, "nA", [128, 5, KC, 2]); b_nA = Buf("nA")
        gwb = sb(es, "gwb", [128, 5, 2, 4, 128], BF16); b_gw = Buf("gw")
        cc = sb(es, "cc", [128, 2, 5, 4]); b_cc = Buf("cc")
        fin = sb(es, "fin", [128, 8, 4]); b_fin = Buf("fin")
        ones_bf = sb(es, "ones_bf", [128, 128], BF16); b_ones = Buf("ones")
        negrow = sb(es, "negrow", [1, 128], BF16)
        pfF = sb(es, "pfF", [1, NF], BF16); b_pfF = Buf("pfF")

        psum = [es.enter_context(nc.psum_tensor(f"ps{i}", [128, 512], F32)) for i in range(8)]
        b_ps = [Buf(f"ps{i}", excl=True) for i in range(8)]
        pctr = [0]

        def next_ps():
            i = pctr[0] % 8
            pctr[0] += 1
            return psum[i], b_ps[i]

        ident = cst[:, 0:128]
        ropeP = cst[:, 128:256]

        T.dma("sp", sm[:], smalls, writes=[b_sm])
        T.dma("sp", pcs[:], pc, writes=[b_pc])
        T.dma("pool", cst[:], consts, writes=[b_cst], cast=True)
        for i in range(5):
            T.dma("pool", gwb[:, i].rearrange("p b c d -> p (b c d)"), gw_all[:, i * 1024:(i + 1) * 1024], writes=[b_gw], cast=True)
        T.dma("pool", pfF[:, 0:1154], pf_full[:, 0:1154], writes=[b_pfF], cast=True)
        T.dma("pool", pfF[:, 1154:NF], pf_full[:, 1154:NF], writes=[b_pfF], cast=True)
        T.op("dve", lambda: nc.vector.memset(ones_bf[:], 1.0), writes=[b_ones])
        T.op("dve", lambda: nc.vector.memset(negrow[:], NEGBIG), writes=[b_ones])

        with contextlib.ExitStack() as st:
            tmp = sb(st, "lam_tmp", [128, 20]); b_tmp = Buf("lamtmp")
            T.op("act", lambda: nc.scalar.activation(out=tmp[:], in_=pcs[:, P_LAM:P_LAM + 20], func=AF.Exp, scale=-1.0), reads=[b_pc], writes=[b_tmp])
            T.op("act", lambda: nc.scalar.activation(out=tmp[:], in_=tmp[:], func=AF.Ln, bias=1.0), writes=[b_tmp])
            T.op("dve", lambda: nc.vector.tensor_scalar(out=cc[:, 0].rearrange("p a b -> p (a b)"), in0=tmp[:], scalar1=-8.0, scalar2=None, op0=ALU.mult), reads=[b_tmp], writes=[b_cc])
            T.op("dve", lambda: nc.vector.tensor_scalar(out=cc[:, 1].rearrange("p a b -> p (a b)"), in0=tmp[:], scalar1=-16.0, scalar2=None, op0=ALU.mult), reads=[b_tmp], writes=[b_cc])
            T.barrier()

        if stop is not None and stop.startswith("only1"):
            T.mute = True
            stop = stop[5:] or None
        with contextlib.ExitStack() as st:
            sil = sb(st, "sil", [128, KC, 2], BF16); b_sil = Buf("sil")
            T.op("act", lambda: nc.scalar.activation(out=sil[:].rearrange("p a b -> p (a b)"), in_=pcs[:, P_CVEC:P_CVEC + 16], func=AF.Silu), reads=[b_pc], writes=[b_sil])
            wb = [sb(st, f"adaw{i}", [128, KC, 1536], BF16) for i in range(2)]
            b_wb = [Buf("adaw0"), Buf("adaw1")]
            cnt = 0
            for l in range(2):
                ps, bps = next_ps()
                for piece in range(4):
                    w, bw = wb[cnt % 2], b_wb[cnt % 2]
                    cnt += 1
                    src = ada_w[l].rearrange("(kc p) f -> p kc f", p=128)[:, :, piece * 1536:(piece + 1) * 1536]
                    T.dma("pool", w[:], src, writes=[bw], cast=True)
                    for ft in range(12):
                        col = (piece * 12 + ft) * 2
                        for kc in range(KC):
                            T.op("pe", lambda kc=kc, ft=ft, col=col, w=w, ps=ps: nc.tensor.matmul(
                                ps[:, col:col + 2], w[:, kc, ft * 128:(ft + 1) * 128], sil[:, kc, :],
                                start=(kc == 0), stop=(kc == KC - 1)),
                                reads=[bw, b_sil], writes=[bps], signal=(kc == KC - 1))
                for j in range(2):
                    T.op("dve", lambda l=l, j=j, ps=ps: nc.vector.tensor_tensor(
                        out=modv[:, l, :, j], in0=ps[:, 0:96].rearrange("p (a b) -> p a b", b=2)[:, :, j],
                        in1=sm[:, S_ADAB + l * 48:S_ADAB + (l + 1) * 48], op=ALU.add),
                        reads=[bps, b_sm], writes=[b_modv])
            for idx, (l, m, goff) in enumerate([(0, 1, S_GMIX), (0, 4, S_GFFN), (1, 1, S_GMIX + 8), (1, 4, S_GFFN + 8)]):
                for j in range(2):
                    T.op("dve", lambda idx=idx, l=l, m=m, goff=goff, j=j: nc.vector.scalar_tensor_tensor(
                        out=nA[:, idx, :, j], in0=modv[:, l, m * 8:(m + 1) * 8, j], scalar=1.0,
                        in1=sm[:, goff:goff + 8], op0=ALU.add, op1=ALU.mult),
                        reads=[b_modv, b_sm], writes=[b_nA])
            if debug:
                T.dma("sp", modv_dbg, modv[:].rearrange("p a b c -> p (a b c)"), reads=[b_modv])
            T.barrier()

        def rmsnorm_thunks(st_tiles, xt, bx, n, A_ap, B_ap, hb, bh):
            sq, bsq, rs, brs, tt, btt = st_tiles
            hold = {}
            th = []
            th.append(lambda: T.op("act", lambda: nc.scalar.activation(out=sq[:, :, :n], in_=xt[:, :, :n], func=AF.Square), reads=[bx], writes=[bsq]))

            def mm():
                ps, bps = next_ps()
                hold["ps"] = (ps, bps)
                for kc in range(KC):
                    T.op("pe", lambda: nc.tensor.matmul(ps[:, :n], ones_bf[:], sq[:, kc, :n], start=(kc == 0), stop=(kc == KC - 1)),
                         reads=[bsq, b_ones], writes=[bps], signal=(kc == KC - 1))
            th.append(mm)
            th.append(lambda: T.op("act", lambda: nc.scalar.activation(out=rs[:, :n], in_=hold["ps"][0][:, :n], func=AF.Sqrt, scale=1.0 / D, bias=EPS),
                                   reads=[hold["ps"][1]], writes=[brs]))
            th.append(lambda: T.op("dve", lambda: nc.vector.reciprocal(out=rs[:, :n], in_=rs[:, :n]), writes=[brs]))
            for ch in range(KC):
                th.append(lambda ch=ch: T.op("dve", lambda: nc.vector.scalar_tensor_tensor(
                    out=tt[:, ch, :n], in0=xt[:, ch, :n], scalar=A_ap(ch), in1=rs[:, :n], op0=ALU.mult, op1=ALU.mult),
                    reads=[bx, brs, b_nA, b_sm], writes=[btt]))
            for ch in range(KC):
                if B_ap is not None:
                    th.append(lambda ch=ch: T.op("act", lambda: nc.scalar.activation(out=hb[:, ch, :n], in_=tt[:, ch, :n], func=AF.Identity, bias=B_ap(ch)),
                                                 reads=[btt, b_modv], writes=[bh]))
                else:
                    th.append(lambda ch=ch: T.op("act", lambda: nc.scalar.activation(out=hb[:, ch, :n], in_=tt[:, ch, :n], func=AF.Identity),
                                                 reads=[btt], writes=[bh]))
            return th

        def rmsnorm_mod(st_tiles, xt, bx, n, A_ap, B_ap, hb, bh):
            for t_ in rmsnorm_thunks(st_tiles, xt, bx, n, A_ap, B_ap, hb, bh):
                t_()

        def norm_tiles(st, n, tag):
            sq = sb(st, f"sq{tag}", [128, KC, n], BF16)
            rs = sb(st, f"rs{tag}", [128, n])
            tt = sb(st, f"tt{tag}", [128, KC, n])
            return (sq, Buf("sq"), rs, Buf("rs"), tt, Buf("tt"))

        def build_diag(dg, b_dg, taps_ap_fn, count):
            for i in range(count):
                T.op("dve", lambda i=i: nc.vector.tensor_scalar(out=dg[:, i, :], in0=ident, scalar1=taps_ap_fn(i), scalar2=None, op0=ALU.mult),
                     reads=[b_cst, b_pc, b_sm], writes=[b_dg])

        def lru_dir(n_cols, tiles, xcb, b_xcb, xcf, b_xcf, idx, g, pf_row, b_pf, pf_off, bufs, init_ap, reverse, hout, b_hout):
            (r, br), (ig, big), (a, ba) = bufs
            for (c0, n) in tiles:
                for q, (dst, bdst) in enumerate(((r, br), (ig, big))):
                    ps, bps = next_ps()
                    T.op("pe", lambda ps=ps, q=q, c0=c0, n=n: nc.tensor.matmul(ps[:, :n], gwb[:, idx, q, g, :], xcb[:, c0:c0 + n], start=True, stop=(pf_row is None)),
                         reads=[b_gw, b_xcb], writes=[bps], signal=(pf_row is None))
                    if pf_row is not None:
                        T.op("pe", lambda ps=ps, c0=c0, n=n: nc.tensor.matmul(ps[:, :n], negrow[:], pf_row[:, pf_off + c0:pf_off + c0 + n], start=False, stop=True),
                             reads=[b_pf, b_ones], writes=[bps])
                    T.op("act", lambda ps=ps, q=q, dst=dst, c0=c0, n=n: nc.scalar.activation(
                        out=dst[:, c0:c0 + n], in_=ps[:, :n], func=AF.Sigmoid,
                        bias=pcs[:, P_GB + idx * 8 + q * 4 + g:P_GB + idx * 8 + q * 4 + g + 1]),
                        reads=[bps, b_pc], writes=[bdst])
            N = n_cols
            T.op("act", lambda: nc.scalar.activation(out=a[:, :N], in_=r[:, :N], func=AF.Exp, scale=cc[:, 0, idx, g:g + 1]), reads=[br, b_cc], writes=[ba])
            T.op("act", lambda: nc.scalar.activation(out=r[:, :N], in_=r[:, :N], func=AF.Exp, scale=cc[:, 1, idx, g:g + 1]), reads=[b_cc], writes=[br])
            T.op("act", lambda: nc.scalar.activation(out=r[:, :N], in_=r[:, :N], func=AF.Sqrt, scale=-1.0, bias=1.0000002), writes=[br])
            T.op("dve", lambda: nc.vector.tensor_tensor(out=ig[:, :N], in0=ig[:, :N], in1=r[:, :N], op=ALU.mult), reads=[br], writes=[big])
            T.op("dve", lambda: nc.vector.tensor_tensor(out=ig[:, :N], in0=ig[:, :N], in1=xcf[:, :N], op=ALU.mult), reads=[b_xcf], writes=[big])
            if reverse:
                T.op("dve", lambda: nc.vector.tensor_tensor_scan(out=hout[:, :N][:, ::-1], data0=a[:, :N][:, ::-1], data1=ig[:, :N][:, ::-1],
                                                               initial=init_ap, op0=ALU.mult, op1=ALU.add),
                     reads=[ba, big, b_fin], writes=[b_hout])
            else:
                T.op("dve", lambda: nc.vector.tensor_tensor_scan(out=hout[:, :N], data0=a[:, :N], data1=ig[:, :N],
                                                               initial=init_ap, op0=ALU.mult, op1=ALU.add),
                     reads=[ba, big, b_fin], writes=[b_hout])

        def conv_pe(dg, b_dg, dbase, ntaps, off0, src, b_src, s0, tiles, evac):
            for (c0, n) in tiles:
                ps, bps = next_ps()
                for o in range(ntaps):
                    T.op("pe", lambda ps=ps, o=o, c0=c0, n=n: nc.tensor.matmul(
                        ps[:, :n], dg[:, dbase + o, :], src[:, s0 + c0 + off0 + o:s0 + c0 + off0 + o + n], start=(o == 0), stop=(o == ntaps - 1)),
                        reads=[b_dg, b_src], writes=[bps], signal=(o == ntaps - 1))
                evac(ps, bps, c0, n)

        def tiles_of(n, step=512):
            return [(c0, min(step, n - c0)) for c0 in range(0, n, step)]

        with contextlib.ExitStack() as L0:
            w_in = sb(L0, "rc_w_in", [128, KC, 2560], BF16); b_win = Buf("rc_w_in")
            for h in range(2):
                T.dma("pool", w_in[:, :, h * 1280:(h + 1) * 1280], rc_w_in.rearrange("(kc p) f -> p kc f", p=128)[:, :, h * 1280:(h + 1) * 1280], writes=[b_win], cast=True)
            dgN = sb(L0, "dgN", [128, 20, 128], BF16); b_dgN = Buf("dgN")
            build_diag(dgN, b_dgN, lambda i: pcs[:, P_TAPS + i:P_TAPS + i + 1], 20)
            dgB = sb(L0, "dgB", [128, 12, 128], BF16); b_dgB = Buf("dgB")
            build_diag(dgB, b_dgB, lambda i: sm[:, S_CBW + i:S_CBW + i + 1], 12)
            T.op("dve", lambda: nc.vector.memset(fin[:], 0.0), writes=[b_fin])
            ya_c = sb(L0, "ya_c", [128, 4, CTX], BF16); b_ya_c = Buf("ya_c")
            yb_c = sb(L0, "yb_c", [128, 4, CTX], BF16); b_yb_c = Buf("yb_c")

            def flag(i):
                return pcs[:, P_FLAGS + i:P_FLAGS + i + 1]

            def phase_A(stA, xsrc, mksrc, ncols, j, xa, b_xa, cv, b_cv, ya, b_ya, yb, b_yb):
                xt2 = [sb(stA, f"ax{i}", [128, KC, 256]) for i in range(2)]; bxt2 = [Buf("ax0"), Buf("ax1")]
                mk2 = [sb(stA, f"am{i}", [128, 256]) for i in range(2)]; bmk2 = [Buf("am0"), Buf("am1")]
                hbA = sb(stA, "ah", [128, KC, 256], BF16); b_hbA = Buf("ah")
                cgt = sb(stA, "acg", [128, 256]); b_cgt = Buf("acg")
                ntA = norm_tiles(stA, 256, "a")
                for ti, (c0, n) in enumerate(tiles_of(ncols, 256)):
                    xt, bx = xt2[ti % 2], bxt2[ti % 2]
                    mk, bmk = mk2[ti % 2], bmk2[ti % 2]
                    T.dma("sp", xt[:, :, :n], xsrc[:, :, c0:c0 + n], writes=[bx])
                    if mksrc is not None:
                        T.dma("sp", mk[:, :n], mksrc[:, c0:c0 + n], writes=[bmk])
                    else:
                        T.op("dve", lambda: nc.vector.memset(mk[:], 1.0), writes=[bmk])
                    rmsnorm_mod(ntA, xt, bx, n, lambda ch: nA[:, 0, ch, j:j + 1], lambda ch: modv[:, 0, 0 * 8 + ch, j:j + 1], hbA, b_hbA)

                    def proj(m):
                        ps, bps = next_ps()
                        for kc in range(KC):
                            T.op("pe", lambda: nc.tensor.matmul(ps[:, :n], w_in[:, kc, m * 128:(m + 1) * 128], hbA[:, kc, :n],
                                                                start=(kc == 0), stop=(kc == KC - 1)),
                                 reads=[b_win, b_hbA], writes=[bps], signal=(kc == KC - 1))
                        return ps, bps
                    for g in range(4):
                        ps, bps = proj(g)
                        T.op("dve", lambda: nc.vector.tensor_tensor(out=xa[:, g, 2 + c0:2 + c0 + n], in0=ps[:, :n], in1=mk[:, :n], op=ALU.mult),
                             reads=[bps, bmk], writes=[b_xa])
                        ps, bps = proj(4 + g)
                        T.op("act", lambda: nc.scalar.activation(out=ya[:, g, c0:c0 + n], in_=ps[:, :n], func=AF.Gelu_apprx_tanh),
                             reads=[bps], writes=[b_ya])
                        ps, bps = proj(8 + g)
                        T.op("act", lambda: nc.scalar.activation(out=yb[:, g, c0:c0 + n], in_=ps[:, :n], func=AF.Identity),
                             reads=[bps], writes=[b_yb])
                        ps, bps = proj(12 + g)
                        T.op("dve", lambda: nc.vector.tensor_tensor(out=cgt[:, :n], in0=ps[:, :n], in1=mk[:, :n], op=ALU.mult),
                             reads=[bps, bmk], writes=[b_cgt])
                        ps, bps = proj(16 + g)
                        T.op("dve", lambda: nc.vector.tensor_tensor(out=cv[:, g, 1 + c0:1 + c0 + n], in0=ps[:, :n], in1=cgt[:, :n], op=ALU.mult),
                             reads=[bps, b_cgt], writes=[b_cv])

            def phase_BC(stB, ncols, xa, b_xa, cv, b_cv, ya, b_ya, yb, b_yb, pf_row, b_pf, initF, initB, finF, finB, lo, hi):
                N = hi - lo
                xcf = sb(stB, "b_xcf", [128, N]); b_xcf = Buf("b_xcf")
                xcb = sb(stB, "b_xcb", [128, N], BF16); b_xcb = Buf("b_xcb")
                r_ = sb(stB, "b_r", [128, N]); ig_ = sb(stB, "b_ig", [128, N]); a_ = sb(stB, "b_a", [128, N])
                bufs = ((r_, Buf("b_r")), (ig_, Buf("b_ig")), (a_, Buf("b_a")))
                big = bufs[1][1]
                hf = sb(stB, "b_hf", [128, N]); b_hf = Buf("b_hf")
                tl = tiles_of(N)
                for g in range(4):
                    def evac(ps, bps, c0, n):
                        T.op("act", lambda: nc.scalar.activation(out=xcf[:, c0:c0 + n], in_=ps[:, :n], func=AF.Identity, bias=sm[:, S_CAB + g:S_CAB + g + 1]),
                             reads=[bps, b_sm], writes=[b_xcf])
                        T.op("act", lambda: nc.scalar.activation(out=xcb[:, c0:c0 + n], in_=ps[:, :n], func=AF.Identity, bias=sm[:, S_CAB + g:S_CAB + g + 1]),
                             reads=[bps, b_sm], writes=[b_xcb])
                    conv_pe(dgN, b_dgN, g * 5, 5, -2, xa[:, g, :], b_xa, 2 + lo, tl, evac)
                    lru_dir(N, tl, xcb, b_xcb, xcf, b_xcf, 0, g, pf_row, b_pf, lo, bufs, initF(g), False, hf, b_hf)
                    if finF is not None:
                        T.op("dve", lambda: nc.vector.tensor_copy(out=fin[:, finF, g:g + 1], in_=hf[:, N - 1:N]), reads=[b_hf], writes=[b_fin])
                    lru_dir(N, tl, xcb, b_xcb, xcf, b_xcf, 1, g, pf_row, b_pf, lo, bufs, initB(g), True, ig_, big)
                    if finB is not None:
                        T.op("dve", lambda: nc.vector.tensor_copy(out=fin[:, finB, g:g + 1], in_=ig_[:, 0:1]), reads=[big], writes=[b_fin])
                    T.op("dve", lambda: nc.vector.tensor_tensor(out=hf[:, :N], in0=hf[:, :N], in1=ig_[:, :N], op=ALU.add), reads=[big], writes=[b_hf])
                    T.op("dve", lambda: nc.vector.tensor_tensor(out=ya[:, g, lo:hi], in0=hf[:, :N], in1=ya[:, g, lo:hi], op=ALU.mult),
                         reads=[b_hf], writes=[b_ya])

                    def evacB(ps, bps, c0, n):
                        T.op("dve", lambda: nc.vector.tensor_tensor(out=yb[:, g, c0:c0 + n], in0=ps[:, :n], in1=yb[:, g, c0:c0 + n], op=ALU.mult),
                             reads=[bps], writes=[b_yb])
                    conv_pe(dgB, b_dgB, g * 3, 3, -1, cv[:, g, :], b_cv, 1, tiles_of(ncols), evacB)

            with contextlib.ExitStack() as stc:
                xa = sb(stc, "xa_c", [128, 4, CTX + 4], BF16); b_xa = Buf("xa_c")
                cv = sb(stc, "cv_c", [128, 4, CTX + 2], BF16); b_cv = Buf("cv_c")
                T.op("dve", lambda: nc.vector.memset(xa[:], 0.0), writes=[b_xa])
                T.op("dve", lambda: nc.vector.memset(cv[:], 0.0), writes=[b_cv])
                with contextlib.ExitStack() as stA:
                    phase_A(stA, xT_ctx, None, CTX, 1, xa, b_xa, cv, b_cv, ya_c, b_ya_c, yb_c, b_yb_c)
                    T.barrier()
                with contextlib.ExitStack() as stB:
                    phase_BC(stB, CTX, xa, b_xa, cv, b_cv, ya_c, b_ya_c, yb_c, b_yb_c, None, None, lambda g: 0.0, lambda g: 0.0, 0, 1, 0, CTX)
                    T.barrier()

            with contextlib.ExitStack() as st:
                pfC = sb(st, "pfC", [1, NCARRY], BF16); b_pfC = Buf("pfC")
                for i in range(3):
                    T.dma("pool", pfC[:, i * 2048:(i + 1) * 2048], pf_carry[:, i * 2048:(i + 1) * 2048], writes=[b_pfC], cast=True)
                dgC = sb(st, "dgC", [128, 60, 128], BF16); b_dgC = Buf("dgC")
                build_diag(dgC, b_dgC, lambda i: pcs[:, P_TAPS + 40 + i:P_TAPS + 40 + i + 1], 60)
                xac = sb(st, "xac", [128, 4, NCARRY + 4], BF16); b_xac = Buf("xac")
                T.op("dve", lambda: nc.vector.memset(xac[:, :, 0:2], 0.0), writes=[b_xac])
                T.op("dve", lambda: nc.vector.memset(xac[:, :, NCARRY + 2:NCARRY + 4], 0.0), writes=[b_xac])
                with contextlib.ExitStack() as stA:
                    CN = 256
                    xt2 = [sb(stA, f"cx{i}", [128, KC, CN]) for i in range(2)]; bxt2 = [Buf("cx0"), Buf("cx1")]
                    mk2 = [sb(stA, f"cm{i}", [128, CN]) for i in range(2)]; bmk2 = [Buf("cm0"), Buf("cm1")]
                    hb = sb(stA, "ch", [128, KC, CN], BF16); b_hb = Buf("ch")
                    nt = norm_tiles(stA, CN, "c")
                    for ti, (c0, n) in enumerate(tiles_of(NCARRY, CN)):
                        xt, bx = xt2[ti % 2], bxt2[ti % 2]
                        mk, bmk = mk2[ti % 2], bmk2[ti % 2]
                        T.dma("sp", xt[:], xT_carry[:, :, c0:c0 + n], writes=[bx])
                        T.dma("sp", mk[:], mk_carry[:, c0:c0 + n], writes=[bmk])
                        rmsnorm_mod(nt, xt, bx, n, lambda ch: nA[:, 0, ch, 0:1], lambda ch: modv[:, 0, 0 * 8 + ch, 0:1], hb, b_hb)
                        for g in range(4):
                            ps, bps = next_ps()
                            for kc in range(KC):
                                T.op("pe", lambda: nc.tensor.matmul(ps[:, :n], w_in[:, kc, g * 128:(g + 1) * 128], hb[:, kc, :n],
                                                                    start=(kc == 0), stop=(kc == KC - 1)),
                                     reads=[b_win, b_hb], writes=[bps], signal=(kc == KC - 1))
                            T.op("dve", lambda: nc.vector.tensor_tensor(out=xac[:, g, 2 + c0:2 + c0 + n], in0=ps[:, :n], in1=mk[:, :n], op=ALU.mult),
                                 reads=[bps, bmk], writes=[b_xac])
                    T.barrier()
                with contextlib.ExitStack() as stB:
                    SEGN = 2048
                    xcf = sb(stB, "c_xcf", [128, SEGN]); b_xcf = Buf("c_xcf")
                    xcb = sb(stB, "c_xcb", [128, SEGN], BF16); b_xcb = Buf("c_xcb")
                    r_ = sb(stB, "c_r", [128, SEGN]); ig_ = sb(stB, "c_ig", [128, SEGN]); a_ = sb(stB, "c_a", [128, SEGN])
                    bufs = ((r_, Buf("c_r")), (ig_, Buf("c_ig")), (a_, Buf("c_a")))
                    big = bufs[1][1]
                    for s in range(3):
                        idx = 2 + s
                        T.op("dve", lambda: nc.vector.tensor_scalar(out=fin[:, 5, :], in0=fin[:, 0, :], scalar1=flag(3 * s), scalar2=None, op0=ALU.mult), reads=[b_pc], writes=[b_fin])
                        T.op("dve", lambda: nc.vector.scalar_tensor_tensor(out=fin[:, 5, :], in0=fin[:, 1, :], scalar=flag(3 * s + 1), in1=fin[:, 5, :], op0=ALU.mult, op1=ALU.add), reads=[b_pc], writes=[b_fin])
                        if s > 0:
                            T.op("dve", lambda: nc.vector.scalar_tensor_tensor(out=fin[:, 5, :], in0=fin[:, 2 + s - 1, :], scalar=flag(3 * s + 2), in1=fin[:, 5, :], op0=ALU.mult, op1=ALU.add), reads=[b_pc], writes=[b_fin])
                        for g in range(4):
                            def evac(ps, bps, c0, n):
                                T.op("act", lambda: nc.scalar.activation(out=xcf[:, c0:c0 + n], in_=ps[:, :n], func=AF.Identity, bias=sm[:, S_CAB + g:S_CAB + g + 1]),
                                     reads=[bps, b_sm], writes=[b_xcf])
                                T.op("act", lambda: nc.scalar.activation(out=xcb[:, c0:c0 + n], in_=ps[:, :n], func=AF.Identity, bias=sm[:, S_CAB + g:S_CAB + g + 1]),
                                     reads=[bps, b_sm], writes=[b_xcb])
                            tl = tiles_of(SEGN)
                            conv_pe(dgC, b_dgC, (s * 4 + g) * 5, 5, -2, xac[:, g, :], b_xac, 2 + s * SEGN, tl, evac)
                            lru_dir(SEGN, tl, xcb, b_xcb, xcf, b_xcf, idx, g, pfC, b_pfC, s * SEGN, bufs,
                                    fin[:, 5, g:g + 1], False, ig_, big)
                            T.op("dve", lambda: nc.vector.tensor_copy(out=fin[:, 2 + s, g:g + 1], in_=ig_[:, SEGN - 1:SEGN]), reads=[big], writes=[b_fin])
                    T.op("dve", lambda: nc.vector.tensor_scalar(out=fin[:, 6, :], in0=fin[:, 0, :], scalar1=flag(9), scalar2=None, op0=ALU.mult), reads=[b_pc], writes=[b_fin])
                    for s in range(3):
                        T.op("dve", lambda: nc.vector.scalar_tensor_tensor(out=fin[:, 6, :], in0=fin[:, 2 + s, :], scalar=flag(10 + s), in1=fin[:, 6, :], op0=ALU.mult, op1=ALU.add), reads=[b_pc], writes=[b_fin])
                    T.op("dve", lambda: nc.vector.tensor_scalar(out=fin[:, 7, :], in0=fin[:, 1, :], scalar1=flag(13), scalar2=None, op0=ALU.mult), reads=[b_pc], writes=[b_fin])
                    T.op("dve", lambda: nc.vector.scalar_tensor_tensor(out=fin[:, 7, :], in0=fin[:, 4, :], scalar=flag(14), in1=fin[:, 7, :], op0=ALU.mult, op1=ALU.add), reads=[b_pc], writes=[b_fin])
                    T.barrier()

            with contextlib.ExitStack() as stf:
                ya_f = sb(stf, "ya_f", [128, 4, NF], BF16); b_ya_f = Buf("ya_f")
                yb_f = sb(stf, "yb_f", [128, 4, NF], BF16); b_yb_f = Buf("yb_f")
                with contextlib.ExitStack() as stx:
                    xa = sb(stx, "xa_f", [128, 4, NF + 4], BF16); b_xa = Buf("xa_f")
                    cv = sb(stx, "cv_f", [128, 4, NF + 2], BF16); b_cv = Buf("cv_f")
                    T.op("dve", lambda: nc.vector.memset(xa[:, :, 0:2], 0.0), writes=[b_xa])
                    T.op("dve", lambda: nc.vector.memset(xa[:, :, NF + 2:NF + 4], 0.0), writes=[b_xa])
                    T.op("dve", lambda: nc.vector.memset(cv[:, :, 0:1], 0.0), writes=[b_cv])
                    T.op("dve", lambda: nc.vector.memset(cv[:, :, NF + 1:NF + 2], 0.0), writes=[b_cv])
                    with contextlib.ExitStack() as stA:
                        phase_A(stA, xT_full, mk_full, NF, 0, xa, b_xa, cv, b_cv, ya_f, b_ya_f, yb_f, b_yb_f)
                        T.barrier()
                    with contextlib.ExitStack() as stB:
                        phase_BC(stB, NF, xa, b_xa, cv, b_cv, ya_f, b_ya_f, yb_f, b_yb_f, pfF, b_pfF,
                                 lambda g: fin[:, 6, g:g + 1], lambda g: fin[:, 7, g:g + 1], None, None, 2, NF - 2)
                        T.barrier()

                with contextlib.ExitStack() as st:
                    w_out = sb(st, "rc_w_out", [128, KC, D], BF16); b_wout = Buf("rc_w_out")
                    T.dma("pool", w_out[:], rc_w_out.rearrange("(kc p) f -> p kc f", p=128), writes=[b_wout], cast=True)
                    xt2 = [sb(st, f"dx{i}", [128, KC, 512]) for i in range(2)]; bxt2 = [Buf("dx0"), Buf("dx1")]
                    regions = [(xT_full, c0, n, 0, c0, ya_f, yb_f, c0) for (c0, n) in tiles_of(NF)] + [(xT_ctx, 0, CTX, 1, NF, ya_c, yb_c, 0)]
                    b_xmid = Buf("xmid")
                    for ti, (xsrc, c0, n, j, wc, ya, yb, yc0) in enumerate(regions):
                        xt, bx = xt2[ti % 2], bxt2[ti % 2]
                        T.dma("sp", xt[:, :, :n], xsrc[:, :, c0:c0 + n], writes=[bx])
                        for m in range(KC):
                            ps, bps = next_ps()
                            for kc in range(KC):
                                src = ya if kc < 4 else yb
                                T.op("pe", lambda: nc.tensor.matmul(ps[:, :n], w_out[:, kc, m * 128:(m + 1) * 128], src[:, kc % 4, yc0:yc0 + n],
                                                                    start=(kc == 0), stop=(kc == KC - 1)),
                                     reads=[b_wout, b_ya_f, b_yb_f, b_ya_c, b_yb_c], writes=[bps], signal=(kc == KC - 1))
                            T.op("dve", lambda: nc.vector.scalar_tensor_tensor(
                                out=xt[:, m, :n], in0=ps[:, :n], scalar=modv[:, 0, 2 * 8 + m, j:j + 1], in1=xt[:, m, :n], op0=ALU.mult, op1=ALU.add),
                                reads=[bps, b_modv], writes=[bx])
                        T.dma("sp", xmid[:, :, wc:wc + n], xt[:, :, :n], reads=[bx])
                    T.barrier()

        def ffn_phase(layer, src_dram, dst_dram, regions, final_norm):
            with contextlib.ExitStack() as st:
                wi = sb(st, "ffn_wi", [128, KC, 2 * DFF], BF16); b_wi = Buf("ffn_wi")
                wo = sb(st, "ffn_wo", [128, FT, D], BF16); b_wo = Buf("ffn_wo")
                for h in range(4):
                    T.dma("pool", wi[:, :, h * 1408:(h + 1) * 1408], ffn_w_in[layer].rearrange("(kc p) f -> p kc f", p=128)[:, :, h * 1408:(h + 1) * 1408], writes=[b_wi], cast=True)
                for h in range(2):
                    T.dma("pool", wo[:, h * 11:(h + 1) * 11, :], ffn_w_out[layer].rearrange("(kc p) f -> p kc f", p=128)[:, h * 11:(h + 1) * 11, :], writes=[b_wo], cast=True)
                NT = 256
                xt2 = [sb(st, f"fx{i}", [128, KC, NT]) for i in range(2)]; bxt2 = [Buf("fx0"), Buf("fx1")]
                hb2 = [sb(st, f"fh{i}", [128, KC, NT], BF16) for i in range(2)]; b_hb2 = [Buf("fh0"), Buf("fh1")]
                act = sb(st, "fact", [128, FT, NT], BF16); b_act = Buf("fact")
                sg2 = [sb(st, f"fsg{i}", [128, NT]) for i in range(2)]; bsg2 = [Buf("fsg0"), Buf("fsg1")]
                nt = norm_tiles(st, NT, "f")
                b_dst = Buf("ffn_dst")
                tiles = []
                for (c0, n, j) in regions:
                    tiles += [(c0 + t0, tn, j) for (t0, tn) in tiles_of(n, NT)]
                ni = 1 + 2 * layer

                def start_tile(ti):
                    c0_, n_, j_ = tiles[ti]
                    xt_, bx_ = xt2[ti % 2], bxt2[ti % 2]
                    T.dma("sp", xt_[:, :, :n_], src_dram[:, :, c0_:c0_ + n_], writes=[bx_])
                    return rmsnorm_thunks(nt, xt_, bx_, n_, lambda ch: nA[:, ni, ch, j_:j_ + 1], lambda ch: modv[:, layer, 3 * 8 + ch, j_:j_ + 1],
                                          hb2[ti % 2], b_hb2[ti % 2])
                for t_ in start_tile(0):
                    t_()
                for ti, (c0, n, j) in enumerate(tiles):
                    xt, bx = xt2[ti % 2], bxt2[ti % 2]
                    hb, b_hb = hb2[ti % 2], b_hb2[ti % 2]
                    nxt = start_tile(ti + 1) if ti + 1 < len(tiles) else []
                    for f in range(FT):
                        if nxt and f >= 1:
                            nxt.pop(0)()
                        psg, bpsg = next_ps()
                        psu, bpsu = next_ps()
                        for kc in range(KC):
                            T.op("pe", lambda kc=kc, f=f, psg=psg: nc.tensor.matmul(psg[:, :n], wi[:, kc, f * 128:(f + 1) * 128], hb[:, kc, :n], start=(kc == 0), stop=(kc == KC - 1)),
                                 reads=[b_wi, b_hb], writes=[bpsg], signal=(kc == KC - 1))
                        for kc in range(KC):
                            T.op("pe", lambda kc=kc, f=f, psu=psu: nc.tensor.matmul(psu[:, :n], wi[:, kc, DFF + f * 128:DFF + (f + 1) * 128], hb[:, kc, :n], start=(kc == 0), stop=(kc == KC - 1)),
                                 reads=[b_wi, b_hb], writes=[bpsu], signal=(kc == KC - 1))
                        sg, bsg = sg2[f % 2], bsg2[f % 2]
                        T.op("act", lambda psg=psg, sg=sg: nc.scalar.activation(out=sg[:, :n], in_=psg[:, :n], func=AF.Silu), reads=[bpsg], writes=[bsg])
                        T.op("dve", lambda psu=psu, sg=sg, f=f: nc.vector.tensor_tensor(out=act[:, f, :n], in0=psu[:, :n], in1=sg[:, :n], op=ALU.mult),
                             reads=[bpsu, bsg], writes=[b_act])
                    while nxt:
                        nxt.pop(0)()
                    for m in range(KC):
                        ps, bps = next_ps()
                        for f in range(FT):
                            T.op("pe", lambda ps=ps, f=f, m=m: nc.tensor.matmul(ps[:, :n], wo[:, f, m * 128:(m + 1) * 128], act[:, f, :n], start=(f == 0), stop=(f == FT - 1)),
                                 reads=[b_wo, b_act], writes=[bps], signal=(f == FT - 1))
                        T.op("dve", lambda ps=ps, m=m, xt=xt: nc.vector.scalar_tensor_tensor(
                            out=xt[:, m, :n], in0=ps[:, :n], scalar=modv[:, layer, 5 * 8 + m, j:j + 1], in1=xt[:, m, :n], op0=ALU.mult, op1=ALU.add),
                            reads=[bps, b_modv], writes=[bx])
                    if final_norm:
                        sq, bsq, rs, brs, tt, btt = nt
                        T.op("act", lambda xt=xt: nc.scalar.activation(out=sq[:, :, :n], in_=xt[:, :, :n], func=AF.Square), reads=[bx], writes=[bsq])
                        ps, bps = next_ps()
                        for kc in range(KC):
                            T.op("pe", lambda kc=kc, ps=ps: nc.tensor.matmul(ps[:, :n], ones_bf[:], sq[:, kc, :n], start=(kc == 0), stop=(kc == KC - 1)),
                                 reads=[bsq, b_ones], writes=[bps], signal=(kc == KC - 1))
                        T.op("act", lambda ps=ps: nc.scalar.activation(out=rs[:, :n], in_=ps[:, :n], func=AF.Sqrt, scale=1.0 / D, bias=EPS), reads=[bps], writes=[brs])
                        T.op("dve", lambda: nc.vector.reciprocal(out=rs[:, :n], in_=rs[:, :n]), writes=[brs])
                        for ch in range(KC):
                            T.op("dve", lambda ch=ch, xt=xt: nc.vector.scalar_tensor_tensor(
                                out=xt[:, ch, :n], in0=xt[:, ch, :n], scalar=sm[:, S_GFIN + ch:S_GFIN + ch + 1], in1=rs[:, :n], op0=ALU.mult, op1=ALU.mult),
                                reads=[brs, b_sm], writes=[bx])
                    T.dma("sp", dst_dram[:, :, c0:c0 + n], xt[:, :, :n], reads=[bx])
                T.barrier()

        ffn_phase(0, xmid, xl1, [(0, NF, 0), (NF, CTX, 1)], False)

        T.mute = False
        NKEY = OWN + 256 + CTX
        NBLK = NKEY // 128
        with _StopGuard(), contextlib.ExitStack() as L1:
            qT = sb(L1, "qT", [128, KC, OWN], BF16); b_qT = Buf("qT")
            kTe = sb(L1, "kTe", [128, 4, NKEY], BF16); kTo = sb(L1, "kTo", [128, 4, NKEY], BF16); b_kT = Buf("kT")
            T.op("dve", lambda: nc.vector.memset(kTe[:].rearrange("p a b -> p (a b)"), 0.0), writes=[b_kT])
            T.op("dve", lambda: nc.vector.memset(kTo[:].rearrange("p a b -> p (a b)"), 0.0), writes=[b_kT])
            vv = sb(L1, "vv", [128, NBLK, 4, 2, 128], BF16); b_vv = Buf("vv")
            T.op("dve", lambda: nc.vector.memset(vv[:].rearrange("p a b c d -> p (a b c d)"), 0.0), writes=[b_vv])
            b_xl1 = Buf("xl1r")
            with contextlib.ExitStack() as st:
                wq = sb(st, "wq", [128, KC, D], BF16); b_wq = Buf("wq")
                wkd = sb(st, "wkd", [128, KC, 4, 2, 64], BF16); b_wkd = Buf("wkd")
                wv = sb(st, "wv", [128, KC, 256], BF16); b_wv = Buf("wv")
                qkv_v = at_w_qkv.rearrange("(kc p) f -> p kc f", p=128)
                T.dma("pool", wq[:], qkv_v[:, :, 0:1024], writes=[b_wq], cast=True)
                for jh in range(4):
                    for dup in range(2):
                        T.dma("pool", wkd[:, :, jh, dup, :], qkv_v[:, :, 1024 + jh * 64:1024 + (jh + 1) * 64], writes=[b_wkd], cast=True)
                T.dma("pool", wv[:], qkv_v[:, :, 1280:1536], writes=[b_wv], cast=True)
                NT = 256
                xt2 = [sb(st, f"px{i}", [128, KC, NT]) for i in range(1)] * 2; bxt2 = [Buf("px0")] * 2
                rk2 = [sb(st, f"prk{i}", [128, 2, NT]) for i in range(1)] * 2; brk2 = [Buf("prk0")] * 2
                rq2 = [sb(st, f"prq{i}", [128, 2, NT]) for i in range(1)] * 2; brq2 = [Buf("prq0")] * 2
                hb = sb(st, "ph", [128, KC, NT], BF16); b_hb = Buf("ph")
                xb2 = [sb(st, f"pxb{i}", [128, NT], BF16) for i in range(2)]; bxb2 = [Buf("pxb0"), Buf("pxb1")]
                t12 = [sb(st, f"pt1{i}", [128, NT]) for i in range(2)]; bt12 = [Buf("pt10"), Buf("pt11")]
                t22 = [sb(st, f"pt2{i}", [128, NT]) for i in range(2)]; bt22 = [Buf("pt20"), Buf("pt21")]
                nt = norm_tiles(st, NT, "p")
                rcnt = [0]

                def rope_evac(ps, bps, n, tab, btab, tcol, dst_ap, bdst, dst_halves=None):
                    i = rcnt[0] % 2
                    rcnt[0] += 1
                    xb, bxb, t1, bt1, t2, bt2 = xb2[i], bxb2[i], t12[i], bt12[i], t22[i], bt22[i]
                    T.op("act", lambda: nc.scalar.activation(out=xb[:, :n], in_=ps[:, :n], func=AF.Identity), reads=[bps], writes=[bxb])
                    ps2, bps2 = next_ps()
                    T.op("pe", lambda: nc.tensor.matmul(ps2[:, :n], ropeP, xb[:, :n], start=True, stop=True), reads=[bxb, b_cst], writes=[bps2])
                    T.op("dve", lambda: nc.vector.tensor_tensor(out=t1[:, :n], in0=ps[:, :n], in1=tab[:, 0, tcol:tcol + n], op=ALU.mult), reads=[bps, btab], writes=[bt1])
                    T.op("dve", lambda: nc.vector.tensor_tensor(out=t2[:, :n], in0=ps2[:, :n], in1=tab[:, 1, tcol:tcol + n], op=ALU.mult), reads=[bps2, btab], writes=[bt2])
                    if dst_halves is None:
                        T.op("dve", lambda: nc.vector.tensor_tensor(out=dst_ap, in0=t1[:, :n], in1=t2[:, :n], op=ALU.add), reads=[bt1, bt2], writes=[bdst])
                    else:
                        for (r0, dap) in dst_halves:
                            T.op("dve", lambda: nc.vector.tensor_tensor(out=dap, in0=t1[r0:r0 + 64, :n], in1=t2[r0:r0 + 64, :n], op=ALU.add), reads=[bt1, bt2], writes=[bdst])

                p1_tiles = NKEY // NT
                if stop is not None and stop.startswith("p1:"):
                    p1_tiles = int(stop.split(":")[1])
                    stop = "p1"
                for ti in range(p1_tiles):
                    k0 = ti * NT
                    is_ctx = k0 >= OWN + 256
                    j = 1 if is_ctx else 0
                    src0 = (2308 + (k0 - 2304)) if is_ctx else (k0 + 2)
                    xt, bx = xt2[ti % 2], bxt2[ti % 2]
                    rk, brk = rk2[ti % 2], brk2[ti % 2]
                    rq, brq = rq2[ti % 2], brq2[ti % 2]
                    T.dma("sp", xt[:], xl1[:, :, src0:src0 + NT], reads=[b_xl1], writes=[bx])
                    oa = max(k0, 128) - k0
                    ob = min(k0 + NT, 128 + OWN) - k0
                    has_q = (not is_ctx) and ob > oa
                    if not is_ctx:
                        T.dma("sp", rk[:], rope_k[:, :, k0:k0 + NT], writes=[brk])
                    if has_q:
                        qa = k0 + oa - 128
                        T.dma("sp", rq[:, :, oa:ob], rope_q[:, :, qa:qa + (ob - oa)], writes=[brq])
                    rmsnorm_mod(nt, xt, bx, NT, lambda ch: nA[:, 2, ch, j:j + 1], lambda ch: modv[:, 1, 0 * 8 + ch, j:j + 1], hb, b_hb)
                    for jh in range(0 if "K" in _SKIP else 4):
                        ps, bps = next_ps()
                        for kc in range(KC):
                            T.op("pe", lambda: nc.tensor.matmul(ps[:, :NT], wkd[:, kc, jh].rearrange("p a b -> p (a b)"), hb[:, kc, :],
                                                                start=(kc == 0), stop=(kc == KC - 1)),
                                 reads=[b_wkd, b_hb], writes=[bps], signal=(kc == KC - 1))
                        halves = [(0, kTe[0:64, jh, k0:k0 + NT]), (64, kTo[64:128, jh, k0:k0 + NT])]
                        if is_ctx:
                            for (r0, dap) in halves:
                                T.op("act", lambda: nc.scalar.activation(out=dap, in_=ps[r0:r0 + 64, :NT], func=AF.Identity), reads=[bps], writes=[b_kT])
                        else:
                            rope_evac(ps, bps, NT, rk, brk, 0, None, b_kT, dst_halves=halves)
                    if has_q and "Q" not in _SKIP:
                        nq = ob - oa
                        for m in range(KC):
                            ps, bps = next_ps()
                            for kc in range(KC):
                                T.op("pe", lambda: nc.tensor.matmul(ps[:, :nq], wq[:, kc, m * 128:(m + 1) * 128], hb[:, kc, oa:ob],
                                                                    start=(kc == 0), stop=(kc == KC - 1)),
                                     reads=[b_wq, b_hb], writes=[bps], signal=(kc == KC - 1))
                            rope_evac(ps, bps, nq, rq, brq, oa, qT[:, m, qa:qa + nq], b_qT)
                    for bi in range(0 if "V" in _SKIP else NT // 128):
                        blk = k0 // 128 + bi
                        ps, bps = next_ps()
                        for kc in range(KC):
                            T.op("pe", lambda: nc.tensor.matmul(ps[:, :256], hb[:, kc, bi * 128:(bi + 1) * 128], wv[:, kc, :],
                                                                start=(kc == 0), stop=(kc == KC - 1)),
                                 reads=[b_wv, b_hb], writes=[bps], signal=(kc == KC - 1))
                        T.op("act", lambda: nc.scalar.activation(out=vv[:, blk, :, 0, 0:64], in_=ps[:, 0:256].rearrange("p (a b) -> p a b", b=64), func=AF.Identity),
                             reads=[bps], writes=[b_vv])
                        T.op("dve", lambda: nc.vector.tensor_copy(out=vv[:, blk, :, 1, 64:128], in_=ps[:, 0:256].rearrange("p (a b) -> p a b", b=64)),
                             reads=[bps], writes=[b_vv])
                T.barrier()

            with contextlib.ExitStack() as st:
                oT = sb(st, "oT", [128, KC, OWN], BF16); b_oT = Buf("oT")
                with contextlib.ExitStack() as st2:
                    amb = sb(st2, "amb", [128, 4, 512], BF16); b_amb = Buf("amb")
                    for mi in range(4):
                        T.dma("pool", amb[:, mi, :], amask[:, mi, :], writes=[b_amb], cast=True)
                    ones_eo = sb(st2, "ones_eo", [128, 2, 128], BF16); b_oeo = Buf("ones_eo")
                    T.op("dve", lambda: nc.vector.memset(ones_eo[:].rearrange("p a b -> p (a b)"), 0.0), writes=[b_oeo])
                    T.op("dve", lambda: nc.vector.memset(ones_eo[:, 0, 0:64], 1.0), writes=[b_oeo])
                    T.op("dve", lambda: nc.vector.memset(ones_eo[:, 1, 64:128], 1.0), writes=[b_oeo])
                    es16 = sb(st2, "es16", [128, 16]); b_es = Buf("es16")
                    T.op("act", lambda: nc.scalar.activation(out=es16[:], in_=pcs[:, P_SINK:P_SINK + 16], func=AF.Exp), reads=[b_pc], writes=[b_es])
                    onesf = sb(st2, "onesf", [128, 128]); b_onesf = Buf("onesf")
                    T.op("dve", lambda: nc.vector.memset(onesf[:], 1.0), writes=[b_onesf])
                    esk = sb(st2, "esk", [128, 4, 256]); b_esk = Buf("esk")
                    for jh in range(4):
                        for g in range(4):
                            r0 = (g % 2) * 64
                            c0 = (g // 2) * 128
                            T.op("dve", lambda: nc.vector.tensor_scalar(out=esk[r0:r0 + 64, jh, c0:c0 + 128], in0=onesf[r0:r0 + 64, :],
                                                                        scalar1=es16[r0:r0 + 64, 4 * jh + g:4 * jh + g + 1], scalar2=None, op0=ALU.mult),
                                 reads=[b_es, b_onesf], writes=[b_esk])
                    pT2 = [sb(st2, f"pT{i}", [128, 5, 512], BF16) for i in range(2)]; bpT2 = [Buf("pT0"), Buf("pT1")]
                    dn2 = [sb(st2, f"dn{i}", [128, 256]) for i in range(2)]; bdn2 = [Buf("dn0"), Buf("dn1")]
                    units = [(i, jh) for i in range(16) for jh in range(4)]

                    def key_blocks(i):
                        return [(i, 2 if i == 0 else 0), (i + 1, None), (i + 2, 3 if i == 15 else 1), (18, None), (19, None)]

                    def emit_scores(u):
                        i, jh = units[u]
                        pT, bpT = pT2[u % 2], bpT2[u % 2]
                        qc = slice(i * 128, (i + 1) * 128)
                        for kb, (blk, mk) in enumerate(key_blocks(i)):
                            ps, bps = next_ps()
                            kc_ = slice(blk * 128, (blk + 1) * 128)
                            first = True
                            if mk is not None:
                                T.op("pe", lambda: nc.tensor.matmul(ps[:, :], ident, amb[:, mk, :], start=True, stop=False),
                                     reads=[b_cst, b_amb], writes=[bps], signal=False)
                                first = False
                            T.op("pe", lambda: nc.tensor.matmul(ps[:, 0:256], kTe[:, jh, kc_], qT[:, 2 * jh:2 * jh + 2, qc], start=first, stop=False),
                                 reads=[b_kT, b_qT], writes=[bps], signal=False)
                            T.op("pe", lambda: nc.tensor.matmul(ps[:, 256:512], kTo[:, jh, kc_], qT[:, 2 * jh:2 * jh + 2, qc], start=False, stop=True),
                                 reads=[b_kT, b_qT], writes=[bps])
                            T.op("act", lambda: nc.scalar.activation(out=pT[:, kb, :], in_=ps[:, :], func=AF.Exp), reads=[bps], writes=[bpT])

                    def emit_pv(u):
                        i, jh = units[u]
                        pT, bpT = pT2[u % 2], bpT2[u % 2]
                        dn, bdn = dn2[u % 2], bdn2[u % 2]
                        blks = [b_ for (b_, _) in key_blocks(i)]
                        psO, bpsO = next_ps()
                        for kb, blk in enumerate(blks):
                            for par in range(2):
                                last = (kb == 4 and par == 1)
                                T.op("pe", lambda: nc.tensor.matmul(psO[:, 0:256], vv[:, blk, jh, par, :], pT[:, kb, par * 256:(par + 1) * 256],
                                                                    start=(kb == 0 and par == 0), stop=last),
                                     reads=[b_vv, bpT], writes=[bpsO], signal=last)
                        psD, bpsD = next_ps()
                        for kb, blk in enumerate(blks):
                            for par in range(2):
                                last = (kb == 4 and par == 1)
                                T.op("pe", lambda: nc.tensor.matmul(psD[:, 0:256], ones_eo[:, par, :], pT[:, kb, par * 256:(par + 1) * 256],
                                                                    start=(kb == 0 and par == 0), stop=last),
                                     reads=[b_oeo, bpT], writes=[bpsD], signal=last)
                        T.op("dve", lambda: nc.vector.tensor_tensor(out=dn[:], in0=psD[:, 0:256], in1=esk[:, jh, :], op=ALU.add), reads=[bpsD, b_esk], writes=[bdn])
                        T.op("dve", lambda: nc.vector.reciprocal(out=dn[:], in_=dn[:]), writes=[bdn])
                        T.op("dve", lambda: nc.vector.tensor_tensor(out=oT[:, 2 * jh:2 * jh + 2, i * 128:(i + 1) * 128],
                                                                    in0=psO[:, 0:256].rearrange("p (a b) -> p a b", b=128),
                                                                    in1=dn[:].rearrange("p (a b) -> p a b", b=128), op=ALU.mult),
                             reads=[bpsO, bdn], writes=[b_oT])

                    nun = 0 if stop == "p1" else len(units)
                    if nun:
                        emit_scores(0)
                    for u in range(nun):
                        if u + 1 < nun:
                            emit_scores(u + 1)
                        emit_pv(u)
                    T.barrier()

                with contextlib.ExitStack() as st3:
                    w_o = sb(st3, "at_w_o", [128, KC, D], BF16); b_wo_ = Buf("at_w_o")
                    T.dma("pool", w_o[:], at_w_out.rearrange("(kc p) f -> p kc f", p=128), writes=[b_wo_], cast=True)
                    NT = 256
                    xt2 = [sb(st3, f"ox{i}", [128, KC, NT]) for i in range(2)]; bxt2 = [Buf("ox0"), Buf("ox1")]
                    b_xm1 = Buf("xmid1")
                    for ti, (c0, n) in enumerate(tiles_of(OWN, NT) if stop not in ("p1", "p2") else []):
                        xt, bx = xt2[ti % 2], bxt2[ti % 2]
                        T.dma("sp", xt[:, :, :n], xl1[:, :, 130 + c0:130 + c0 + n], reads=[b_xl1], writes=[bx])
                        for m in range(KC):
                            ps, bps = next_ps()
                            for kc in range(KC):
                                T.op("pe", lambda: nc.tensor.matmul(ps[:, :n], w_o[:, kc, m * 128:(m + 1) * 128], oT[:, kc, c0:c0 + n],
                                                                    start=(kc == 0), stop=(kc == KC - 1)),
                                     reads=[b_wo_, b_oT], writes=[bps], signal=(kc == KC - 1))
                            T.op("dve", lambda: nc.vector.scalar_tensor_tensor(
                                out=xt[:, m, :n], in0=ps[:, :n], scalar=modv[:, 1, 2 * 8 + m, 0:1], in1=xt[:, m, :n], op0=ALU.mult, op1=ALU.add),
                                reads=[bps, b_modv], writes=[bx])
                        T.dma("sp", xmid1[:, :, c0:c0 + n], xt[:, :, :n], reads=[bx])
                    T.barrier()

        if stop is None:
            ffn_phase(1, xmid1, outT, [(0, OWN, 0)], True)

        fin_toks = []
        for sem, val in T.dsems:
            if val > 0:
                fin_toks.append((sem, val))
        T._wait(T.E["sp"], fin_toks)
    return nc


def _fm(a):
    T_ = a.shape[0]
    return np.ascontiguousarray(a.reshape(T_, KC, 128).transpose(2, 1, 0))


def _chan(v):
    return np.ascontiguousarray(v.reshape(4, 128).T)


def _blockdiag(w):
    out = np.zeros((128, 4, 128), np.float32)
    for g in range(4):
        for hh in range(2):
            out[hh * 64:(hh + 1) * 64, g, hh * 64:(hh + 1) * 64] = w[2 * g + hh]
    return out


def prepare_inputs(inp):
    f32 = np.float32
    x = np.asarray(inp["x"], f32); c = np.asarray(inp["c"], f32); ctx = np.asarray(inp["ctx"], f32)
    c_ctx = np.asarray(inp["c_ctx"], f32)
    shared = {}
    for k in ["ada_w", "ffn_w_in", "ffn_w_out"]:
        shared[k] = np.ascontiguousarray(np.asarray(inp[k], f32))
    shared["rc_w_in"] = np.ascontiguousarray(np.asarray(inp["rc_w_in"], f32)[0])
    shared["rc_w_out"] = np.ascontiguousarray(np.asarray(inp["rc_w_out"], f32)[0])
    shared["at_w_qkv"] = np.ascontiguousarray(np.asarray(inp["at_w_qkv"], f32)[0])
    shared["at_w_out"] = np.ascontiguousarray(np.asarray(inp["at_w_out"], f32)[0])

    sm = np.zeros((128, S_NS), f32)
    ada_b = np.asarray(inp["ada_b"], f32)
    for l in range(2):
        sm[:, S_ADAB + l * 48:S_ADAB + (l + 1) * 48] = ada_b[l].reshape(48, 128).T
        sm[:, S_GMIX + l * 8:S_GMIX + (l + 1) * 8] = np.asarray(inp["norm_mix_g"], f32)[l].reshape(8, 128).T
        sm[:, S_GFFN + l * 8:S_GFFN + (l + 1) * 8] = np.asarray(inp["norm_ffn_g"], f32)[l].reshape(8, 128).T
    sm[:, S_GFIN:S_GFIN + 8] = np.asarray(inp["norm_final_g"], f32).reshape(8, 128).T
    sm[:, S_CAB:S_CAB + 4] = _chan(np.asarray(inp["rc_conv_a_b"], f32)[0])
    cbw = np.asarray(inp["rc_conv_b_w"], f32)[0]
    for g in range(4):
        for o in range(3):
            sm[:, S_CBW + g * 3 + o] = cbw[o, g * 128:(g + 1) * 128]
    shared["smalls"] = sm

    cst = np.zeros((128, 384), f32)
    cst[:, 0:128] = np.eye(128, dtype=f32)
    P = np.zeros((128, 128), f32)
    for m in range(128):
        d = m % 64
        base = m - d
        within = d % 32
        if within < 16:
            P[base + (d + 16), m] = -1.0
        else:
            P[base + (d - 16), m] = 1.0
    cst[:, 128:256] = P
    shared["consts"] = cst

    caw = np.asarray(inp["rc_conv_a_w"], f32)[0]
    r_w = np.asarray(inp["rc_gate_r_w"], f32)[0]; r_b = np.asarray(inp["rc_gate_r_b"], f32)[0]
    i_w = np.asarray(inp["rc_gate_i_w"], f32)[0]; i_b = np.asarray(inp["rc_gate_i_b"], f32)[0]
    lam = np.asarray(inp["rc_lambda"], f32)[0]
    sink = np.asarray(inp["at_sink"], f32)[0]

    inv_freq = (10000.0 ** (-np.arange(16, dtype=f32) / 16)).astype(f32)

    def rope_tab(tok, scale):
        tok = np.asarray(tok)
        row = (tok // 64).astype(f32); col = (tok % 64).astype(f32)
        ang = np.stack([row[:, None] * inv_freq, col[:, None] * inv_freq], axis=1).astype(f32)
        cos = np.cos(ang).astype(f32); sin = np.sin(ang).astype(f32)
        tab = np.zeros((128, 2, len(tok)), f32)
        for p in range(128):
            d = p % 64
            ax = d // 32; fr = d % 16
            tab[p, 0] = cos[:, ax, fr] * scale
            tab[p, 1] = sin[:, ax, fr] * scale
        return tab

    per_core = []
    for core in range(8):
        b, k = core // 4, core % 4
        m = {}
        t0 = 2048 * k - 130
        xf = np.zeros((NF, D), f32); mk = np.zeros((NF,), f32)
        lo = max(0, -t0); hi = min(NF, SEQ - t0)
        xf[lo:hi] = x[b, t0 + lo:t0 + hi]; mk[lo:hi] = 1.0
        m["xT_full"] = _fm(xf)
        m["mk_full"] = np.ascontiguousarray(np.broadcast_to(mk[None, :], (128, NF)))
        m["pf_full"] = np.ascontiguousarray((1.0 - mk)[None, :])
        m["xT_ctx"] = _fm(ctx[b])
        xc_ = np.zeros((NCARRY, D), f32); mkc = np.zeros((NCARRY,), f32); pfc = np.ones((NCARRY,), f32)
        if k >= 1:
            nfw = 2048 * k - 128
            xc_[126:126 + nfw] = x[b, 0:nfw]; mkc[126:126 + nfw] = 1.0; pfc[126:126 + nfw] = 0.0
            xc_[2048 * k - 2] = x[b, nfw]; mkc[2048 * k - 2] = 1.0
        if k <= 2:
            tlast = 2048 * (k + 1) + 128
            toks = np.arange(SEQ - 1, tlast - 1, -1)
            c0 = 2048 * k + 126
            xc_[c0:c0 + len(toks)] = x[b, toks]; mkc[c0:c0 + len(toks)] = 1.0; pfc[c0:c0 + len(toks)] = 0.0
            assert c0 + len(toks) == 6142
            xc_[6142] = x[b, tlast - 1]; xc_[6143] = x[b, tlast - 2]; mkc[6142:6144] = 1.0
        m["xT_carry"] = _fm(xc_)
        m["mk_carry"] = np.ascontiguousarray(np.broadcast_to(mkc[None, :], (128, NCARRY)))
        m["pf_carry"] = np.ascontiguousarray(pfc[None, :])
        pcv = np.zeros((128, P_NP), f32)
        pcv[:, P_CVEC:P_CVEC + 16] = np.stack([c[b].reshape(8, 128).T, c_ctx.reshape(8, 128).T], axis=-1).reshape(128, 16)
        dirs = [0, 1] + [0 if s < k else 1 for s in range(3)]
        natural = [True, True] + [s < k for s in range(3)]
        gw = np.zeros((128, 5, 2, 4, 128), f32)
        for idx in range(5):
            d = dirs[idx]
            gw[:, idx, 0] = _blockdiag(r_w[d]); gw[:, idx, 1] = _blockdiag(i_w[d])
            pcv[:, P_GB + idx * 8:P_GB + idx * 8 + 4] = _chan(r_b[d].reshape(512))
            pcv[:, P_GB + idx * 8 + 4:P_GB + idx * 8 + 8] = _chan(i_b[d].reshape(512))
            pcv[:, P_LAM + idx * 4:P_LAM + idx * 4 + 4] = _chan(lam[d])
            for g in range(4):
                for o5 in range(5):
                    o = o5 - 2
                    if natural[idx]:
                        j = o + 2
                    else:
                        j = 2 - o
                    if 0 <= j <= 3:
                        pcv[:, P_TAPS + (idx * 4 + g) * 5 + o5] = caw[j, g * 128:(g + 1) * 128]
        m["gw_all"] = gw.reshape(128, -1)
        fl = np.zeros(16, f32)
        for s in range(3):
            fwd = s < k
            if s == 0:
                fl[0 if fwd else 1] = 1.0
            elif fwd or s > k:
                fl[3 * s + 2] = 1.0
            else:
                fl[3 * s + 1] = 1.0
        if k == 0:
            fl[9] = 1.0
        else:
            fl[10 + (k - 1)] = 1.0
        if k == 3:
            fl[13] = 1.0
        else:
            fl[14] = 1.0
        pcv[:, P_FLAGS:P_FLAGS + 16] = fl[None, :]
        pcv[:, P_SINK:P_SINK + 16] = sink[None, :]
        m["pc"] = pcv
        m["rope_q"] = rope_tab(np.arange(2048 * k, 2048 * (k + 1)), 0.125)
        m["rope_k"] = rope_tab(np.clip(np.arange(2048 * k - 128, 2048 * (k + 1) + 128), 0, SEQ - 1), 1.0)
        am = np.zeros((128, 4, 512), f32)
        jj = np.arange(128)[:, None]; ii = np.arange(128)[None, :]
        prev = np.where(jj >= ii, 0.0, NEGBIG).astype(f32)
        nxt = np.where(jj <= ii, 0.0, NEGBIG).astype(f32)
        am[:, 0] = np.tile(prev, (1, 4)); am[:, 1] = np.tile(nxt, (1, 4))
        am[:, 2] = NEGBIG if k == 0 else am[:, 0]
        am[:, 3] = NEGBIG if k == 3 else am[:, 1]
        m["amask"] = am
        m.update(shared)
        per_core.append(m)
    return per_core


_NC_CACHE = {}


def kernel(**inputs):
    if "nc" not in _NC_CACHE:
        _NC_CACHE["nc"] = build_nc(False)
    nc = _NC_CACHE["nc"]
    in_maps = prepare_inputs(inputs)
    res = run_bass_kernel_spmd(nc, in_maps, core_ids=list(range(8)))
    out = np.zeros((2, SEQ, D), np.float32)
    for core in range(8):
        b, k = core // 4, core % 4
        oT = np.asarray(res.results[core]["outT"])
        out[b, 2048 * k:2048 * (k + 1)] = oT.transpose(2, 1, 0).reshape(OWN, D)
    return out
```

```python
import contextlib
import numpy as np
import concourse.bass as bass
import concourse.mybir as mybir
from concourse.bass_utils import run_bass_kernel_spmd

F32 = mybir.dt.float32
BF16 = mybir.dt.bfloat16
AF = mybir.ActivationFunctionType
ALU = mybir.AluOpType

D = 1024
KC = 8
SEQ = 8192
OWN = 2048
CTX = 256
NF = 2308
NCARRY = 6144
DFF = 2816
FT = 22
EPS = 1e-6
NEGBIG = -30000.0

S_ADAB = 0
S_GMIX = S_ADAB + 96
S_GFFN = S_GMIX + 16
S_GFIN = S_GFFN + 16
S_CAB = S_GFIN + 8
S_CBW = S_CAB + 4
S_NS = S_CBW + 12
P_CVEC = 0
P_TAPS = P_CVEC + 16
P_GB = P_TAPS + 100
P_LAM = P_GB + 40
P_FLAGS = P_LAM + 20
P_SINK = P_FLAGS + 16
P_NP = P_SINK + 16


class _Stop(Exception):
    pass


class _StopGuard:
    def __enter__(self):
        return self

    def __exit__(self, et, ev, tb):
        return et is _Stop


class Buf:
    __slots__ = ("name", "w", "r", "excl")

    def __init__(self, name, excl=False):
        self.name = name
        self.w = None
        self.r = {}
        self.excl = excl


class Eng:
    def __init__(self, name, h, sems):
        self.name, self.h, self.sems = name, h, sems
        self.n = 0
        self.waited = {}
        self.pw = []
        self.pr = []

    def tok(self):
        if self.n == 0:
            return None
        return (self.sems[(self.n - 1) // 30000], (self.n - 1) % 30000 + 1)


class Tracker:
    def __init__(self, nc, es):
        self.nc = nc
        mk = lambda nm: es.enter_context(nc.semaphore(nm))
        self.E = {
            "pe": Eng("pe", nc.tensor, [mk("pe0"), mk("pe1")]),
            "act": Eng("act", nc.scalar, [mk("act0"), mk("act1")]),
            "dve": Eng("dve", nc.vector, [mk("dve0"), mk("dve1")]),
            "pool": Eng("pool", nc.gpsimd, [mk("pool0")]),
            "sp": Eng("sp", nc.sync, [mk("sp0")]),
        }
        self.dsems = [[mk(f"dma{i}"), 0] for i in range(36)]
        self.di = {"hw": 0, "sw": 0}

    def _wait(self, e, toks):
        best = {}
        for sem, val in toks:
            k = id(sem)
            if k not in best or best[k][1] < val:
                best[k] = (sem, val)
        for k, (sem, val) in best.items():
            if e.waited.get(k, 0) >= val:
                continue
            e.h.wait_ge(sem, val)
            e.waited[k] = val

    @staticmethod
    def _deps(reads, writes):
        toks = []
        for b in reads:
            if b.w is not None:
                toks.append(b.w)
            if b.excl:
                toks.extend(b.r.values())
        for b in writes:
            if b.w is not None:
                toks.append(b.w)
            toks.extend(b.r.values())
        return toks

    mute = False

    def op(self, en, fn, reads=(), writes=(), signal=True):
        if self.mute:
            return None
        e = self.E[en]
        toks = self._deps(reads, writes)
        if en == "pe":
            own = {id(s) for s in e.sems}
            toks = [t for t in toks if id(t[0]) not in own]
        self._wait(e, toks)
        inst = fn()
        e.pr.extend(reads)
        e.pw.extend(writes)
        if signal:
            e.n += 1
            tok = e.tok()
            inst.then_inc(tok[0], 1)
            wset = {id(b) for b in e.pw}
            for b in e.pw:
                b.w = tok
                b.r = {}
            for b in e.pr:
                if id(b) not in wset:
                    b.r[id(tok[0])] = tok
            e.pw, e.pr = [], []
        return inst

    def dma(self, en, out_ap, in_ap, reads=(), writes=(), cast=False):
        if self.mute:
            return
        e = self.E[en]
        if en == "pool":
            slot = self.dsems[24 + self.di["sw"]]
            self.di["sw"] = (self.di["sw"] + 1) % 12
        else:
            slot = self.dsems[self.di["hw"]]
            self.di["hw"] = (self.di["hw"] + 1) % 24
        sem, val = slot
        toks = self._deps(reads, writes)
        if val > 0:
            toks.append((sem, val))
        self._wait(e, toks)
        e.h.dma_start(out=out_ap, in_=in_ap).then_inc(sem, 16)
        slot[1] = val + 16
        tok = (sem, val + 16)
        for b in writes:
            b.w = tok
            b.r = {}
        for b in reads:
            b.r[id(sem)] = tok

    def barrier(self):
        toks = []
        for e in self.E.values():
            t = e.tok()
            if t is not None:
                toks.append(t)
        for sem, val in self.dsems:
            if val > 0:
                toks.append((sem, val))
        for e in self.E.values():
            self._wait(e, toks)


import itertools
import os
_uid = itertools.count()
_SKIP = os.environ.get("K_SKIP", "")


def build_nc(debug=False, stop=None):
    nc = bass.Bass("TRN2", target_bir_lowering=False)
    es = contextlib.ExitStack()

    def din(name, shape, dt=F32):
        return nc.dram_tensor(name, list(shape), dt, kind="ExternalInput").ap()

    def dscr(name, shape, dt=F32, out=False):
        kind = "ExternalOutput" if out else "Internal"
        return nc.dram_tensor(name, list(shape), dt, kind=kind).ap()

    xT_full = din("xT_full", [128, KC, NF])
    xT_ctx = din("xT_ctx", [128, KC, CTX])
    xT_carry = din("xT_carry", [128, KC, NCARRY])
    mk_full = din("mk_full", [128, NF])
    mk_carry = din("mk_carry", [128, NCARRY])
    pf_full = din("pf_full", [1, NF])
    pf_carry = din("pf_carry", [1, NCARRY])
    smalls = din("smalls", [128, S_NS])
    pc = din("pc", [128, P_NP])
    gw_all = din("gw_all", [128, 5 * 2 * 4 * 128])
    ada_w = din("ada_w", [2, D, 6 * D])
    ffn_w_in = din("ffn_w_in", [2, D, 2 * DFF])
    ffn_w_out = din("ffn_w_out", [2, DFF, D])
    rc_w_in = din("rc_w_in", [D, 2560])
    rc_w_out = din("rc_w_out", [D, D])
    at_w_qkv = din("at_w_qkv", [D, 1536])
    at_w_out = din("at_w_out", [D, D])
    rope_q = din("rope_q", [128, 2, OWN])
    rope_k = din("rope_k", [128, 2, OWN + 256])
    amask = din("amask", [128, 4, 512])
    consts = din("consts", [128, 3 * 128])

    xmid = dscr("xmid", [128, KC, NF + CTX], out=debug)
    xl1 = dscr("xl1", [128, KC, NF + CTX], out=debug)
    xmid1 = dscr("xmid1", [128, KC, OWN])
    outT = dscr("outT", [128, KC, OWN], out=True)
    modv_dbg = dscr("modv_dbg", [128, 192], out=True) if debug else None

    with es:
        T = Tracker(nc, es)
        sb = lambda st, name, shape, dt=F32: st.enter_context(nc.sbuf_tensor("s_" + name + "_" + str(next(_uid)), list(shape), dt))

        sm = sb(es, "sm", [128, S_NS]); b_sm = Buf("sm")
        pcs = sb(es, "pcs", [128, P_NP]); b_pc = Buf("pc")
        cst = sb(es, "cst", [128, 384], BF16); b_cst = Buf("cst")
        modv = sb(es, "modv", [128, 2, 48, 2]); b_modv = Buf("modv")
        nA = sb(es, "nA", [128, 5, KC, 2]); b_nA = Buf("nA")
        gwb = sb(es, "gwb", [128, 5, 2, 4, 128], BF16); b_gw = Buf("gw")
        cc = sb(es, "cc", [128, 2, 5, 4]); b_cc = Buf("cc")
        fin = sb(es, "fin", [128, 8, 4]); b_fin = Buf("fin")
        ones_bf = sb(es, "ones_bf", [128, 128], BF16); b_ones = Buf("ones")
        negrow = sb(es, "negrow", [1, 128], BF16)
        pfF = sb(es, "pfF", [1, NF], BF16); b_pfF = Buf("pfF")

        psum = [es.enter_context(nc.psum_tensor(f"ps{i}", [128, 512], F32)) for i in range(8)]
        b_ps = [Buf(f"ps{i}", excl=True) for i in range(8)]
        pctr = [0]

        def next_ps():
            i = pctr[0] % 8
            pctr[0] += 1
            return psum[i], b_ps[i]

        ident = cst[:, 0:128]
        ropeP = cst[:, 128:256]

        T.dma("sp", sm[:], smalls, writes=[b_sm])
        T.dma("sp", pcs[:], pc, writes=[b_pc])
        T.dma("pool", cst[:], consts, writes=[b_cst], cast=True)
        for i in range(5):
            T.dma("pool", gwb[:, i].rearrange("p b c d -> p (b c d)"), gw_all[:, i * 1024:(i + 1) * 1024], writes=[b_gw], cast=True)
        T.dma("pool", pfF[:, 0:1154], pf_full[:, 0:1154], writes=[b_pfF], cast=True)
        T.dma("pool", pfF[:, 1154:NF], pf_full[:, 1154:NF], writes=[b_pfF], cast=True)
        T.op("dve", lambda: nc.vector.memset(ones_bf[:], 1.0), writes=[b_ones])
        T.op("dve", lambda: nc.vector.memset(negrow[:], NEGBIG), writes=[b_ones])

        with contextlib.ExitStack() as st:
            tmp = sb(st, "lam_tmp", [128, 20]); b_tmp = Buf("lamtmp")
            T.op("act", lambda: nc.scalar.activation(out=tmp[:], in_=pcs[:, P_LAM:P_LAM + 20], func=AF.Exp, scale=-1.0), reads=[b_pc], writes=[b_tmp])
            T.op("act", lambda: nc.scalar.activation(out=tmp[:], in_=tmp[:], func=AF.Ln, bias=1.0), writes=[b_tmp])
            T.op("dve", lambda: nc.vector.tensor_scalar(out=cc[:, 0].rearrange("p a b -> p (a b)"), in0=tmp[:], scalar1=-8.0, scalar2=None, op0=ALU.mult), reads=[b_tmp], writes=[b_cc])
            T.op("dve", lambda: nc.vector.tensor_scalar(out=cc[:, 1].rearrange("p a b -> p (a b)"), in0=tmp[:], scalar1=-16.0, scalar2=None, op0=ALU.mult), reads=[b_tmp], writes=[b_cc])
            T.barrier()

        if stop is not None and stop.startswith("only1"):
            T.mute = True
            stop = stop[5:] or None
        with contextlib.ExitStack() as st:
            sil = sb(st, "sil", [128, KC, 2], BF16); b_sil = Buf("sil")
            T.op("act", lambda: nc.scalar.activation(out=sil[:].rearrange("p a b -> p (a b)"), in_=pcs[:, P_CVEC:P_CVEC + 16], func=AF.Silu), reads=[b_pc], writes=[b_sil])
            wb = [sb(st, f"adaw{i}", [128, KC, 1536], BF16) for i in range(2)]
            b_wb = [Buf("adaw0"), Buf("adaw1")]
            cnt = 0
            for l in range(2):
                ps, bps = next_ps()
                for piece in range(4):
                    w, bw = wb[cnt % 2], b_wb[cnt % 2]
                    cnt += 1
                    src = ada_w[l].rearrange("(kc p) f -> p kc f", p=128)[:, :, piece * 1536:(piece + 1) * 1536]
                    T.dma("pool", w[:], src, writes=[bw], cast=True)
                    for ft in range(12):
                        col = (piece * 12 + ft) * 2
                        for kc in range(KC):
                            T.op("pe", lambda kc=kc, ft=ft, col=col, w=w, ps=ps: nc.tensor.matmul(
                                ps[:, col:col + 2], w[:, kc, ft * 128:(ft + 1) * 128], sil[:, kc, :],
                                start=(kc == 0), stop=(kc == KC - 1)),
                                reads=[bw, b_sil], writes=[bps], signal=(kc == KC - 1))
                for j in range(2):
                    T.op("dve", lambda l=l, j=j, ps=ps: nc.vector.tensor_tensor(
                        out=modv[:, l, :, j], in0=ps[:, 0:96].rearrange("p (a b) -> p a b", b=2)[:, :, j],
                        in1=sm[:, S_ADAB + l * 48:S_ADAB + (l + 1) * 48], op=ALU.add),
                        reads=[bps, b_sm], writes=[b_modv])
            for idx, (l, m, goff) in enumerate([(0, 1, S_GMIX), (0, 4, S_GFFN), (1, 1, S_GMIX + 8), (1, 4, S_GFFN + 8)]):
                for j in range(2):
                    T.op("dve", lambda idx=idx, l=l, m=m, goff=goff, j=j: nc.vector.scalar_tensor_tensor(
                        out=nA[:, idx, :, j], in0=modv[:, l, m * 8:(m + 1) * 8, j], scalar=1.0,
                        in1=sm[:, goff:goff + 8], op0=ALU.add, op1=ALU.mult),
                        reads=[b_modv, b_sm], writes=[b_nA])
            if debug:
                T.dma("sp", modv_dbg, modv[:].rearrange("p a b c -> p (a b c)"), reads=[b_modv])
            T.barrier()

        def rmsnorm_thunks(st_tiles, xt, bx, n, A_ap, B_ap, hb, bh):
            sq, bsq, rs, brs, tt, btt = st_tiles
            hold = {}
            th = []
            th.append(lambda: T.op("act", lambda: nc.scalar.activation(out=sq[:, :, :n], in_=xt[:, :, :n], func=AF.Square), reads=[bx], writes=[bsq]))

            def mm():
                ps, bps = next_ps()
                hold["ps"] = (ps, bps)
                for kc in range(KC):
                    T.op("pe", lambda: nc.tensor.matmul(ps[:, :n], ones_bf[:], sq[:, kc, :n], start=(kc == 0), stop=(kc == KC - 1)),
                         reads=[bsq, b_ones], writes=[bps], signal=(kc == KC - 1))
            th.append(mm)
            th.append(lambda: T.op("act", lambda: nc.scalar.activation(out=rs[:, :n], in_=hold["ps"][0][:, :n], func=AF.Sqrt, scale=1.0 / D, bias=EPS),
                                   reads=[hold["ps"][1]], writes=[brs]))
            th.append(lambda: T.op("dve", lambda: nc.vector.reciprocal(out=rs[:, :n], in_=rs[:, :n]), writes=[brs]))
            for ch in range(KC):
                th.append(lambda ch=ch: T.op("dve", lambda: nc.vector.scalar_tensor_tensor(
                    out=tt[:, ch, :n], in0=xt[:, ch, :n], scalar=A_ap(ch), in1=rs[:, :n], op0=ALU.mult, op1=ALU.mult),
                    reads=[bx, brs, b_nA, b_sm], writes=[btt]))
            for ch in range(KC):
                if B_ap is not None:
                    th.append(lambda ch=ch: T.op("act", lambda: nc.scalar.activation(out=hb[:, ch, :n], in_=tt[:, ch, :n], func=AF.Identity, bias=B_ap(ch)),
                                                 reads=[btt, b_modv], writes=[bh]))
                else:
                    th.append(lambda ch=ch: T.op("act", lambda: nc.scalar.activation(out=hb[:, ch, :n], in_=tt[:, ch, :n], func=AF.Identity),
                                                 reads=[btt], writes=[bh]))
            return th

        def rmsnorm_mod(st_tiles, xt, bx, n, A_ap, B_ap, hb, bh):
            for t_ in rmsnorm_thunks(st_tiles, xt, bx, n, A_ap, B_ap, hb, bh):
                t_()

        def pump(nxt, k=1):
            for _ in range(k):
                if nxt:
                    nxt.pop(0)()

        def norm_tiles(st, n, tag):
            sq = sb(st, f"sq{tag}", [128, KC, n], BF16)
            rs = sb(st, f"rs{tag}", [128, n])
            tt = sb(st, f"tt{tag}", [128, KC, n])
            return (sq, Buf("sq"), rs, Buf("rs"), tt, Buf("tt"))

        def build_diag(dg, b_dg, taps_ap_fn, count):
            for i in range(count):
                T.op("dve", lambda i=i: nc.vector.tensor_scalar(out=dg[:, i, :], in0=ident, scalar1=taps_ap_fn(i), scalar2=None, op0=ALU.mult),
                     reads=[b_cst, b_pc, b_sm], writes=[b_dg])

        def lru_dir(n_cols, tiles, xcb, b_xcb, xcf, b_xcf, idx, g, pf_row, b_pf, pf_off, bufs, init_ap, reverse, hout, b_hout):
            (r, br), (ig, big), (a, ba) = bufs
            for (c0, n) in tiles:
                for q, (dst, bdst) in enumerate(((r, br), (ig, big))):
                    ps, bps = next_ps()
                    T.op("pe", lambda ps=ps, q=q, c0=c0, n=n: nc.tensor.matmul(ps[:, :n], gwb[:, idx, q, g, :], xcb[:, c0:c0 + n], start=True, stop=(pf_row is None)),
                         reads=[b_gw, b_xcb], writes=[bps], signal=(pf_row is None))
                    if pf_row is not None:
                        T.op("pe", lambda ps=ps, c0=c0, n=n: nc.tensor.matmul(ps[:, :n], negrow[:], pf_row[:, pf_off + c0:pf_off + c0 + n], start=False, stop=True),
                             reads=[b_pf, b_ones], writes=[bps])
                    T.op("act", lambda ps=ps, q=q, dst=dst, c0=c0, n=n: nc.scalar.activation(
                        out=dst[:, c0:c0 + n], in_=ps[:, :n], func=AF.Sigmoid,
                        bias=pcs[:, P_GB + idx * 8 + q * 4 + g:P_GB + idx * 8 + q * 4 + g + 1]),
                        reads=[bps, b_pc], writes=[bdst])
            N = n_cols
            T.op("act", lambda: nc.scalar.activation(out=a[:, :N], in_=r[:, :N], func=AF.Exp, scale=cc[:, 0, idx, g:g + 1]), reads=[br, b_cc], writes=[ba])
            T.op("act", lambda: nc.scalar.activation(out=r[:, :N], in_=r[:, :N], func=AF.Exp, scale=cc[:, 1, idx, g:g + 1]), reads=[b_cc], writes=[br])
            T.op("act", lambda: nc.scalar.activation(out=r[:, :N], in_=r[:, :N], func=AF.Sqrt, scale=-1.0, bias=1.0000002), writes=[br])
            T.op("dve", lambda: nc.vector.tensor_tensor(out=ig[:, :N], in0=ig[:, :N], in1=r[:, :N], op=ALU.mult), reads=[br], writes=[big])
            T.op("dve", lambda: nc.vector.tensor_tensor(out=ig[:, :N], in0=ig[:, :N], in1=xcf[:, :N], op=ALU.mult), reads=[b_xcf], writes=[big])
            if reverse:
                T.op("dve", lambda: nc.vector.tensor_tensor_scan(out=hout[:, :N][:, ::-1], data0=a[:, :N][:, ::-1], data1=ig[:, :N][:, ::-1],
                                                               initial=init_ap, op0=ALU.mult, op1=ALU.add),
                     reads=[ba, big, b_fin], writes=[b_hout])
            else:
                T.op("dve", lambda: nc.vector.tensor_tensor_scan(out=hout[:, :N], data0=a[:, :N], data1=ig[:, :N],
                                                               initial=init_ap, op0=ALU.mult, op1=ALU.add),
                     reads=[ba, big, b_fin], writes=[b_hout])

        def conv_pe(dg, b_dg, dbase, ntaps, off0, src, b_src, s0, tiles, evac):
            for (c0, n) in tiles:
                ps, bps = next_ps()
                for o in range(ntaps):
                    T.op("pe", lambda ps=ps, o=o, c0=c0, n=n: nc.tensor.matmul(
                        ps[:, :n], dg[:, dbase + o, :], src[:, s0 + c0 + off0 + o:s0 + c0 + off0 + o + n], start=(o == 0), stop=(o == ntaps - 1)),
                        reads=[b_dg, b_src], writes=[bps], signal=(o == ntaps - 1))
                evac(ps, bps, c0, n)

        def tiles_of(n, step=512):
            return [(c0, min(step, n - c0)) for c0 in range(0, n, step)]

        with contextlib.ExitStack() as L0:
            w_in = sb(L0, "rc_w_in", [128, KC, 2560], BF16); b_win = Buf("rc_w_in")
            for h in range(2):
                T.dma("pool", w_in[:, :, h * 1280:(h + 1) * 1280], rc_w_in.rearrange("(kc p) f -> p kc f", p=128)[:, :, h * 1280:(h + 1) * 1280], writes=[b_win], cast=True)
            dgN = sb(L0, "dgN", [128, 20, 128], BF16); b_dgN = Buf("dgN")
            build_diag(dgN, b_dgN, lambda i: pcs[:, P_TAPS + i:P_TAPS + i + 1], 20)
            dgB = sb(L0, "dgB", [128, 12, 128], BF16); b_dgB = Buf("dgB")
            build_diag(dgB, b_dgB, lambda i: sm[:, S_CBW + i:S_CBW + i + 1], 12)
            T.op("dve", lambda: nc.vector.memset(fin[:], 0.0), writes=[b_fin])
            ya_c = sb(L0, "ya_c", [128, 4, CTX], BF16); b_ya_c = Buf("ya_c")
            yb_c = sb(L0, "yb_c", [128, 4, CTX], BF16); b_yb_c = Buf("yb_c")

            def flag(i):
                return pcs[:, P_FLAGS + i:P_FLAGS + i + 1]

            def phase_A(stA, xsrc, mksrc, ncols, j, xa, b_xa, cv, b_cv, ya, b_ya, yb, b_yb):
                xt2 = [sb(stA, f"ax{i}", [128, KC, 256]) for i in range(2)]; bxt2 = [Buf("ax0"), Buf("ax1")]
                mk2 = [sb(stA, f"am{i}", [128, 256]) for i in range(2)]; bmk2 = [Buf("am0"), Buf("am1")]
                hbA2 = [sb(stA, f"ah{i}", [128, KC, 256], BF16) for i in range(2)]; b_hbA2 = [Buf("ah0"), Buf("ah1")]
                cgt = sb(stA, "acg", [128, 256]); b_cgt = Buf("acg")
                ntA = norm_tiles(stA, 256, "a")
                tlA = tiles_of(ncols, 256)

                def startA(ti):
                    c0_, n_ = tlA[ti]
                    xt_, bx_ = xt2[ti % 2], bxt2[ti % 2]
                    mk_, bmk_ = mk2[ti % 2], bmk2[ti % 2]
                    T.dma("sp", xt_[:, :, :n_], xsrc[:, :, c0_:c0_ + n_], writes=[bx_])
                    if mksrc is not None:
                        T.dma("sp", mk_[:, :n_], mksrc[:, c0_:c0_ + n_], writes=[bmk_])
                    else:
                        T.op("dve", lambda: nc.vector.memset(mk_[:], 1.0), writes=[bmk_])
                    return rmsnorm_thunks(ntA, xt_, bx_, n_, lambda ch: nA[:, 0, ch, j:j + 1], lambda ch: modv[:, 0, 0 * 8 + ch, j:j + 1],
                                          hbA2[ti % 2], b_hbA2[ti % 2])
                for t_ in startA(0):
                    t_()
                for ti, (c0, n) in enumerate(tlA):
                    xt, bx = xt2[ti % 2], bxt2[ti % 2]
                    mk, bmk = mk2[ti % 2], bmk2[ti % 2]
                    hbA, b_hbA = hbA2[ti % 2], b_hbA2[ti % 2]
                    nxt = startA(ti + 1) if ti + 1 < len(tlA) else []

                    def proj(m):
                        ps, bps = next_ps()
                        for kc in range(KC):
                            T.op("pe", lambda: nc.tensor.matmul(ps[:, :n], w_in[:, kc, m * 128:(m + 1) * 128], hbA[:, kc, :n],
                                                                start=(kc == 0), stop=(kc == KC - 1)),
                                 reads=[b_win, b_hbA], writes=[bps], signal=(kc == KC - 1))
                        pump(nxt)
                        return ps, bps
                    for g in range(4):
                        ps, bps = proj(g)
                        T.op("dve", lambda: nc.vector.tensor_tensor(out=xa[:, g, 2 + c0:2 + c0 + n], in0=ps[:, :n], in1=mk[:, :n], op=ALU.mult),
                             reads=[bps, bmk], writes=[b_xa])
                        ps, bps = proj(4 + g)
                        T.op("act", lambda: nc.scalar.activation(out=ya[:, g, c0:c0 + n], in_=ps[:, :n], func=AF.Gelu_apprx_tanh),
                             reads=[bps], writes=[b_ya])
                        ps, bps = proj(8 + g)
                        T.op("act", lambda: nc.scalar.activation(out=yb[:, g, c0:c0 + n], in_=ps[:, :n], func=AF.Identity),
                             reads=[bps], writes=[b_yb])
                        ps, bps = proj(12 + g)
                        T.op("dve", lambda: nc.vector.tensor_tensor(out=cgt[:, :n], in0=ps[:, :n], in1=mk[:, :n], op=ALU.mult),
                             reads=[bps, bmk], writes=[b_cgt])
                        ps, bps = proj(16 + g)
                        T.op("dve", lambda: nc.vector.tensor_tensor(out=cv[:, g, 1 + c0:1 + c0 + n], in0=ps[:, :n], in1=cgt[:, :n], op=ALU.mult),
                             reads=[bps, b_cgt], writes=[b_cv])
                    while nxt:
                        pump(nxt)

            def phase_BC(stB, ncols, xa, b_xa, cv, b_cv, ya, b_ya, yb, b_yb, pf_row, b_pf, initF, initB, finF, finB, lo, hi):
                N = hi - lo
                xcf = sb(stB, "b_xcf", [128, N]); b_xcf = Buf("b_xcf")
                xcb = sb(stB, "b_xcb", [128, N], BF16); b_xcb = Buf("b_xcb")
                r_ = sb(stB, "b_r", [128, N]); ig_ = sb(stB, "b_ig", [128, N]); a_ = sb(stB, "b_a", [128, N])
                bufs = ((r_, Buf("b_r")), (ig_, Buf("b_ig")), (a_, Buf("b_a")))
                big = bufs[1][1]
                hf = sb(stB, "b_hf", [128, N]); b_hf = Buf("b_hf")
                tl = tiles_of(N)
                for g in range(4):
                    def evac(ps, bps, c0, n):
                        T.op("act", lambda: nc.scalar.activation(out=xcf[:, c0:c0 + n], in_=ps[:, :n], func=AF.Identity, bias=sm[:, S_CAB + g:S_CAB + g + 1]),
                             reads=[bps, b_sm], writes=[b_xcf])
                        T.op("act", lambda: nc.scalar.activation(out=xcb[:, c0:c0 + n], in_=ps[:, :n], func=AF.Identity, bias=sm[:, S_CAB + g:S_CAB + g + 1]),
                             reads=[bps, b_sm], writes=[b_xcb])
                    conv_pe(dgN, b_dgN, g * 5, 5, -2, xa[:, g, :], b_xa, 2 + lo, tl, evac)
                    lru_dir(N, tl, xcb, b_xcb, xcf, b_xcf, 0, g, pf_row, b_pf, lo, bufs, initF(g), False, hf, b_hf)
                    if finF is not None:
                        T.op("dve", lambda: nc.vector.tensor_copy(out=fin[:, finF, g:g + 1], in_=hf[:, N - 1:N]), reads=[b_hf], writes=[b_fin])
                    lru_dir(N, tl, xcb, b_xcb, xcf, b_xcf, 1, g, pf_row, b_pf, lo, bufs, initB(g), True, ig_, big)
                    if finB is not None:
                        T.op("dve", lambda: nc.vector.tensor_copy(out=fin[:, finB, g:g + 1], in_=ig_[:, 0:1]), reads=[big], writes=[b_fin])
                    T.op("dve", lambda: nc.vector.tensor_tensor(out=hf[:, :N], in0=hf[:, :N], in1=ig_[:, :N], op=ALU.add), reads=[big], writes=[b_hf])
                    T.op("dve", lambda: nc.vector.tensor_tensor(out=ya[:, g, lo:hi], in0=hf[:, :N], in1=ya[:, g, lo:hi], op=ALU.mult),
                         reads=[b_hf], writes=[b_ya])

                    def evacB(ps, bps, c0, n):
                        T.op("dve", lambda: nc.vector.tensor_tensor(out=yb[:, g, c0:c0 + n], in0=ps[:, :n], in1=yb[:, g, c0:c0 + n], op=ALU.mult),
                             reads=[bps], writes=[b_yb])
                    conv_pe(dgB, b_dgB, g * 3, 3, -1, cv[:, g, :], b_cv, 1, tiles_of(ncols), evacB)

            with contextlib.ExitStack() as stc:
                xa = sb(stc, "xa_c", [128, 4, CTX + 4], BF16); b_xa = Buf("xa_c")
                cv = sb(stc, "cv_c", [128, 4, CTX + 2], BF16); b_cv = Buf("cv_c")
                T.op("dve", lambda: nc.vector.memset(xa[:], 0.0), writes=[b_xa])
                T.op("dve", lambda: nc.vector.memset(cv[:], 0.0), writes=[b_cv])
                with contextlib.ExitStack() as stA:
                    phase_A(stA, xT_ctx, None, CTX, 1, xa, b_xa, cv, b_cv, ya_c, b_ya_c, yb_c, b_yb_c)
                    T.barrier()
                with contextlib.ExitStack() as stB:
                    phase_BC(stB, CTX, xa, b_xa, cv, b_cv, ya_c, b_ya_c, yb_c, b_yb_c, None, None, lambda g: 0.0, lambda g: 0.0, 0, 1, 0, CTX)
                    T.barrier()

            with contextlib.ExitStack() as st:
                pfC = sb(st, "pfC", [1, NCARRY], BF16); b_pfC = Buf("pfC")
                for i in range(3):
                    T.dma("pool", pfC[:, i * 2048:(i + 1) * 2048], pf_carry[:, i * 2048:(i + 1) * 2048], writes=[b_pfC], cast=True)
                dgC = sb(st, "dgC", [128, 60, 128], BF16); b_dgC = Buf("dgC")
                build_diag(dgC, b_dgC, lambda i: pcs[:, P_TAPS + 40 + i:P_TAPS + 40 + i + 1], 60)
                xac = sb(st, "xac", [128, 4, NCARRY + 4], BF16); b_xac = Buf("xac")
                T.op("dve", lambda: nc.vector.memset(xac[:, :, 0:2], 0.0), writes=[b_xac])
                T.op("dve", lambda: nc.vector.memset(xac[:, :, NCARRY + 2:NCARRY + 4], 0.0), writes=[b_xac])
                with contextlib.ExitStack() as stA:
                    CN = 256
                    xt2 = [sb(stA, f"cx{i}", [128, KC, CN]) for i in range(2)]; bxt2 = [Buf("cx0"), Buf("cx1")]
                    mk2 = [sb(stA, f"cm{i}", [128, CN]) for i in range(2)]; bmk2 = [Buf("cm0"), Buf("cm1")]
                    hbC2 = [sb(stA, f"ch{i}", [128, KC, CN], BF16) for i in range(2)]; b_hbC2 = [Buf("ch0"), Buf("ch1")]
                    nt = norm_tiles(stA, CN, "c")
                    tlC = tiles_of(NCARRY, CN)

                    def startC(ti):
                        c0_, n_ = tlC[ti]
                        xt_, bx_ = xt2[ti % 2], bxt2[ti % 2]
                        mk_, bmk_ = mk2[ti % 2], bmk2[ti % 2]
                        T.dma("sp", xt_[:], xT_carry[:, :, c0_:c0_ + n_], writes=[bx_])
                        T.dma("sp", mk_[:], mk_carry[:, c0_:c0_ + n_], writes=[bmk_])
                        return rmsnorm_thunks(nt, xt_, bx_, n_, lambda ch: nA[:, 0, ch, 0:1], lambda ch: modv[:, 0, 0 * 8 + ch, 0:1],
                                              hbC2[ti % 2], b_hbC2[ti % 2])
                    for t_ in startC(0):
                        t_()
                    for ti, (c0, n) in enumerate(tlC):
                        xt, bx = xt2[ti % 2], bxt2[ti % 2]
                        mk, bmk = mk2[ti % 2], bmk2[ti % 2]
                        hb, b_hb = hbC2[ti % 2], b_hbC2[ti % 2]
                        nxt = startC(ti + 1) if ti + 1 < len(tlC) else []
                        for g in range(4):
                            pump(nxt, 5)
                            ps, bps = next_ps()
                            for kc in range(KC):
                                T.op("pe", lambda: nc.tensor.matmul(ps[:, :n], w_in[:, kc, g * 128:(g + 1) * 128], hb[:, kc, :n],
                                                                    start=(kc == 0), stop=(kc == KC - 1)),
                                     reads=[b_win, b_hb], writes=[bps], signal=(kc == KC - 1))
                            T.op("dve", lambda: nc.vector.tensor_tensor(out=xac[:, g, 2 + c0:2 + c0 + n], in0=ps[:, :n], in1=mk[:, :n], op=ALU.mult),
                                 reads=[bps, bmk], writes=[b_xac])
                    T.barrier()
                with contextlib.ExitStack() as stB:
                    SEGN = 2048
                    xcf = sb(stB, "c_xcf", [128, SEGN]); b_xcf = Buf("c_xcf")
                    xcb = sb(stB, "c_xcb", [128, SEGN], BF16); b_xcb = Buf("c_xcb")
                    r_ = sb(stB, "c_r", [128, SEGN]); ig_ = sb(stB, "c_ig", [128, SEGN]); a_ = sb(stB, "c_a", [128, SEGN])
                    bufs = ((r_, Buf("c_r")), (ig_, Buf("c_ig")), (a_, Buf("c_a")))
                    big = bufs[1][1]
                    for s in range(3):
                        idx = 2 + s
                        T.op("dve", lambda: nc.vector.tensor_scalar(out=fin[:, 5, :], in0=fin[:, 0, :], scalar1=flag(3 * s), scalar2=None, op0=ALU.mult), reads=[b_pc], writes=[b_fin])
                        T.op("dve", lambda: nc.vector.scalar_tensor_tensor(out=fin[:, 5, :], in0=fin[:, 1, :], scalar=flag(3 * s + 1), in1=fin[:, 5, :], op0=ALU.mult, op1=ALU.add), reads=[b_pc], writes=[b_fin])
                        if s > 0:
                            T.op("dve", lambda: nc.vector.scalar_tensor_tensor(out=fin[:, 5, :], in0=fin[:, 2 + s - 1, :], scalar=flag(3 * s + 2), in1=fin[:, 5, :], op0=ALU.mult, op1=ALU.add), reads=[b_pc], writes=[b_fin])
                        for g in range(4):
                            def evac(ps, bps, c0, n):
                                T.op("act", lambda: nc.scalar.activation(out=xcf[:, c0:c0 + n], in_=ps[:, :n], func=AF.Identity, bias=sm[:, S_CAB + g:S_CAB + g + 1]),
                                     reads=[bps, b_sm], writes=[b_xcf])
                                T.op("act", lambda: nc.scalar.activation(out=xcb[:, c0:c0 + n], in_=ps[:, :n], func=AF.Identity, bias=sm[:, S_CAB + g:S_CAB + g + 1]),
                                     reads=[bps, b_sm], writes=[b_xcb])
                            tl = tiles_of(SEGN)
                            conv_pe(dgC, b_dgC, (s * 4 + g) * 5, 5, -2, xac[:, g, :], b_xac, 2 + s * SEGN, tl, evac)
                            lru_dir(SEGN, tl, xcb, b_xcb, xcf, b_xcf, idx, g, pfC, b_pfC, s * SEGN, bufs,
                                    fin[:, 5, g:g + 1], False, ig_, big)
                            T.op("dve", lambda: nc.vector.tensor_copy(out=fin[:, 2 + s, g:g + 1], in_=ig_[:, SEGN - 1:SEGN]), reads=[big], writes=[b_fin])
                    T.op("dve", lambda: nc.vector.tensor_scalar(out=fin[:, 6, :], in0=fin[:, 0, :], scalar1=flag(9), scalar2=None, op0=ALU.mult), reads=[b_pc], writes=[b_fin])
                    for s in range(3):
                        T.op("dve", lambda: nc.vector.scalar_tensor_tensor(out=fin[:, 6, :], in0=fin[:, 2 + s, :], scalar=flag(10 + s), in1=fin[:, 6, :], op0=ALU.mult, op1=ALU.add), reads=[b_pc], writes=[b_fin])
                    T.op("dve", lambda: nc.vector.tensor_scalar(out=fin[:, 7, :], in0=fin[:, 1, :], scalar1=flag(13), scalar2=None, op0=ALU.mult), reads=[b_pc], writes=[b_fin])
                    T.op("dve", lambda: nc.vector.scalar_tensor_tensor(out=fin[:, 7, :], in0=fin[:, 4, :], scalar=flag(14), in1=fin[:, 7, :], op0=ALU.mult, op1=ALU.add), reads=[b_pc], writes=[b_fin])
                    T.barrier()

            with contextlib.ExitStack() as stf:
                ya_f = sb(stf, "ya_f", [128, 4, NF], BF16); b_ya_f = Buf("ya_f")
                yb_f = sb(stf, "yb_f", [128, 4, NF], BF16); b_yb_f = Buf("yb_f")
                with contextlib.ExitStack() as stx:
                    xa = sb(stx, "xa_f", [128, 4, NF + 4], BF16); b_xa = Buf("xa_f")
                    cv = sb(stx, "cv_f", [128, 4, NF + 2], BF16); b_cv = Buf("cv_f")
                    T.op("dve", lambda: nc.vector.memset(xa[:, :, 0:2], 0.0), writes=[b_xa])
                    T.op("dve", lambda: nc.vector.memset(xa[:, :, NF + 2:NF + 4], 0.0), writes=[b_xa])
                    T.op("dve", lambda: nc.vector.memset(cv[:, :, 0:1], 0.0), writes=[b_cv])
                    T.op("dve", lambda: nc.vector.memset(cv[:, :, NF + 1:NF + 2], 0.0), writes=[b_cv])
                    with contextlib.ExitStack() as stA:
                        phase_A(stA, xT_full, mk_full, NF, 0, xa, b_xa, cv, b_cv, ya_f, b_ya_f, yb_f, b_yb_f)
                        T.barrier()
                    with contextlib.ExitStack() as stB:
                        phase_BC(stB, NF, xa, b_xa, cv, b_cv, ya_f, b_ya_f, yb_f, b_yb_f, pfF, b_pfF,
                                 lambda g: fin[:, 6, g:g + 1], lambda g: fin[:, 7, g:g + 1], None, None, 2, NF - 2)
                        T.barrier()

                with contextlib.ExitStack() as st:
                    w_out = sb(st, "rc_w_out", [128, KC, D], BF16); b_wout = Buf("rc_w_out")
                    T.dma("pool", w_out[:], rc_w_out.rearrange("(kc p) f -> p kc f", p=128), writes=[b_wout], cast=True)
                    xt2 = [sb(st, f"dx{i}", [128, KC, 512]) for i in range(2)]; bxt2 = [Buf("dx0"), Buf("dx1")]
                    regions = [(xT_full, c0, n, 0, c0, ya_f, yb_f, c0) for (c0, n) in tiles_of(NF)] + [(xT_ctx, 0, CTX, 1, NF, ya_c, yb_c, 0)]
                    b_xmid = Buf("xmid")
                    for ti, (xsrc, c0, n, j, wc, ya, yb, yc0) in enumerate(regions):
                        xt, bx = xt2[ti % 2], bxt2[ti % 2]
                        T.dma("sp", xt[:, :, :n], xsrc[:, :, c0:c0 + n], writes=[bx])
                        for m in range(KC):
                            ps, bps = next_ps()
                            for kc in range(KC):
                                src = ya if kc < 4 else yb
                                T.op("pe", lambda: nc.tensor.matmul(ps[:, :n], w_out[:, kc, m * 128:(m + 1) * 128], src[:, kc % 4, yc0:yc0 + n],
                                                                    start=(kc == 0), stop=(kc == KC - 1)),
                                     reads=[b_wout, b_ya_f, b_yb_f, b_ya_c, b_yb_c], writes=[bps], signal=(kc == KC - 1))
                            T.op("dve", lambda: nc.vector.scalar_tensor_tensor(
                                out=xt[:, m, :n], in0=ps[:, :n], scalar=modv[:, 0, 2 * 8 + m, j:j + 1], in1=xt[:, m, :n], op0=ALU.mult, op1=ALU.add),
                                reads=[bps, b_modv], writes=[bx])
                        T.dma("sp", xmid[:, :, wc:wc + n], xt[:, :, :n], reads=[bx])
                    T.barrier()

        def ffn_phase(layer, src_dram, dst_dram, regions, final_norm):
            with contextlib.ExitStack() as st:
                wi = sb(st, "ffn_wi", [128, KC, 2 * DFF], BF16); b_wi = Buf("ffn_wi")
                wo = sb(st, "ffn_wo", [128, FT, D], BF16); b_wo = Buf("ffn_wo")
                for h in range(4):
                    T.dma("pool", wi[:, :, h * 1408:(h + 1) * 1408], ffn_w_in[layer].rearrange("(kc p) f -> p kc f", p=128)[:, :, h * 1408:(h + 1) * 1408], writes=[b_wi], cast=True)
                for h in range(2):
                    T.dma("pool", wo[:, h * 11:(h + 1) * 11, :], ffn_w_out[layer].rearrange("(kc p) f -> p kc f", p=128)[:, h * 11:(h + 1) * 11, :], writes=[b_wo], cast=True)
                NT = 256
                xt2 = [sb(st, f"fx{i}", [128, KC, NT]) for i in range(2)]; bxt2 = [Buf("fx0"), Buf("fx1")]
                hb2 = [sb(st, f"fh{i}", [128, KC, NT], BF16) for i in range(2)]; b_hb2 = [Buf("fh0"), Buf("fh1")]
                act = sb(st, "fact", [128, FT, NT], BF16); b_act = Buf("fact")
                sg2 = [sb(st, f"fsg{i}", [128, NT]) for i in range(2)]; bsg2 = [Buf("fsg0"), Buf("fsg1")]
                nt = norm_tiles(st, NT, "f")
                b_dst = Buf("ffn_dst")
                tiles = []
                for (c0, n, j) in regions:
                    tiles += [(c0 + t0, tn, j) for (t0, tn) in tiles_of(n, NT)]
                ni = 1 + 2 * layer

                def start_tile(ti):
                    c0_, n_, j_ = tiles[ti]
                    xt_, bx_ = xt2[ti % 2], bxt2[ti % 2]
                    T.dma("sp", xt_[:, :, :n_], src_dram[:, :, c0_:c0_ + n_], writes=[bx_])
                    return rmsnorm_thunks(nt, xt_, bx_, n_, lambda ch: nA[:, ni, ch, j_:j_ + 1], lambda ch: modv[:, layer, 3 * 8 + ch, j_:j_ + 1],
                                          hb2[ti % 2], b_hb2[ti % 2])
                for t_ in start_tile(0):
                    t_()
                for ti, (c0, n, j) in enumerate(tiles):
                    xt, bx = xt2[ti % 2], bxt2[ti % 2]
                    hb, b_hb = hb2[ti % 2], b_hb2[ti % 2]
                    nxt = start_tile(ti + 1) if ti + 1 < len(tiles) else []
                    for f in range(FT):
                        if nxt and f >= 1:
                            nxt.pop(0)()
                        psg, bpsg = next_ps()
                        psu, bpsu = next_ps()
                        for kc in range(KC):
                            T.op("pe", lambda kc=kc, f=f, psg=psg: nc.tensor.matmul(psg[:, :n], wi[:, kc, f * 128:(f + 1) * 128], hb[:, kc, :n], start=(kc == 0), stop=(kc == KC - 1)),
                                 reads=[b_wi, b_hb], writes=[bpsg], signal=(kc == KC - 1))
                        for kc in range(KC):
                            T.op("pe", lambda kc=kc, f=f, psu=psu: nc.tensor.matmul(psu[:, :n], wi[:, kc, DFF + f * 128:DFF + (f + 1) * 128], hb[:, kc, :n], start=(kc == 0), stop=(kc == KC - 1)),
                                 reads=[b_wi, b_hb], writes=[bpsu], signal=(kc == KC - 1))
                        sg, bsg = sg2[f % 2], bsg2[f % 2]
                        T.op("act", lambda psg=psg, sg=sg: nc.scalar.activation(out=sg[:, :n], in_=psg[:, :n], func=AF.Silu), reads=[bpsg], writes=[bsg])
                        T.op("dve", lambda psu=psu, sg=sg, f=f: nc.vector.tensor_tensor(out=act[:, f, :n], in0=psu[:, :n], in1=sg[:, :n], op=ALU.mult),
                             reads=[bpsu, bsg], writes=[b_act])
                    while nxt:
                        nxt.pop(0)()
                    for m in range(KC):
                        ps, bps = next_ps()
                        for f in range(FT):
                            T.op("pe", lambda ps=ps, f=f, m=m: nc.tensor.matmul(ps[:, :n], wo[:, f, m * 128:(m + 1) * 128], act[:, f, :n], start=(f == 0), stop=(f == FT - 1)),
                                 reads=[b_wo, b_act], writes=[bps], signal=(f == FT - 1))
                        T.op("dve", lambda ps=ps, m=m, xt=xt: nc.vector.scalar_tensor_tensor(
                            out=xt[:, m, :n], in0=ps[:, :n], scalar=modv[:, layer, 5 * 8 + m, j:j + 1], in1=xt[:, m, :n], op0=ALU.mult, op1=ALU.add),
                            reads=[bps, b_modv], writes=[bx])
                    if final_norm:
                        sq, bsq, rs, brs, tt, btt = nt
                        T.op("act", lambda xt=xt: nc.scalar.activation(out=sq[:, :, :n], in_=xt[:, :, :n], func=AF.Square), reads=[bx], writes=[bsq])
                        ps, bps = next_ps()
                        for kc in range(KC):
                            T.op("pe", lambda kc=kc, ps=ps: nc.tensor.matmul(ps[:, :n], ones_bf[:], sq[:, kc, :n], start=(kc == 0), stop=(kc == KC - 1)),
                                 reads=[bsq, b_ones], writes=[bps], signal=(kc == KC - 1))
                        T.op("act", lambda ps=ps: nc.scalar.activation(out=rs[:, :n], in_=ps[:, :n], func=AF.Sqrt, scale=1.0 / D, bias=EPS), reads=[bps], writes=[brs])
                        T.op("dve", lambda: nc.vector.reciprocal(out=rs[:, :n], in_=rs[:, :n]), writes=[brs])
                        for ch in range(KC):
                            T.op("dve", lambda ch=ch, xt=xt: nc.vector.scalar_tensor_tensor(
                                out=xt[:, ch, :n], in0=xt[:, ch, :n], scalar=sm[:, S_GFIN + ch:S_GFIN + ch + 1], in1=rs[:, :n], op0=ALU.mult, op1=ALU.mult),
                                reads=[brs, b_sm], writes=[bx])
                    T.dma("sp", dst_dram[:, :, c0:c0 + n], xt[:, :, :n], reads=[bx])
                T.barrier()

        ffn_phase(0, xmid, xl1, [(0, NF, 0), (NF, CTX, 1)], False)

        T.mute = False
        NKEY = OWN + 256 + CTX
        NBLK = NKEY // 128
        with _StopGuard(), contextlib.ExitStack() as L1:
            qT = sb(L1, "qT", [128, KC, OWN], BF16); b_qT = Buf("qT")
            kTe = sb(L1, "kTe", [128, 4, NKEY], BF16); kTo = sb(L1, "kTo", [128, 4, NKEY], BF16); b_kT = Buf("kT")
            T.op("dve", lambda: nc.vector.memset(kTe[:].rearrange("p a b -> p (a b)"), 0.0), writes=[b_kT])
            T.op("dve", lambda: nc.vector.memset(kTo[:].rearrange("p a b -> p (a b)"), 0.0), writes=[b_kT])
            vv = sb(L1, "vv", [128, NBLK, 4, 2, 128], BF16); b_vv = Buf("vv")
            T.op("dve", lambda: nc.vector.memset(vv[:].rearrange("p a b c d -> p (a b c d)"), 0.0), writes=[b_vv])
            b_xl1 = Buf("xl1r")
            with contextlib.ExitStack() as st:
                wq = sb(st, "wq", [128, KC, D], BF16); b_wq = Buf("wq")
                wkd = sb(st, "wkd", [128, KC, 4, 2, 64], BF16); b_wkd = Buf("wkd")
                wv = sb(st, "wv", [128, KC, 256], BF16); b_wv = Buf("wv")
                qkv_v = at_w_qkv.rearrange("(kc p) f -> p kc f", p=128)
                T.dma("pool", wq[:], qkv_v[:, :, 0:1024], writes=[b_wq], cast=True)
                for jh in range(4):
                    for dup in range(2):
                        T.dma("pool", wkd[:, :, jh, dup, :], qkv_v[:, :, 1024 + jh * 64:1024 + (jh + 1) * 64], writes=[b_wkd], cast=True)
                T.dma("pool", wv[:], qkv_v[:, :, 1280:1536], writes=[b_wv], cast=True)
                NT = 256
                xt2 = [sb(st, f"px{i}", [128, KC, NT]) for i in range(1)] * 2; bxt2 = [Buf("px0")] * 2
                rk2 = [sb(st, f"prk{i}", [128, 2, NT]) for i in range(1)] * 2; brk2 = [Buf("prk0")] * 2
                rq2 = [sb(st, f"prq{i}", [128, 2, NT]) for i in range(1)] * 2; brq2 = [Buf("prq0")] * 2
                hb = sb(st, "ph", [128, KC, NT], BF16); b_hb = Buf("ph")
                xb2 = [sb(st, f"pxb{i}", [128, NT], BF16) for i in range(2)]; bxb2 = [Buf("pxb0"), Buf("pxb1")]
                t12 = [sb(st, f"pt1{i}", [128, NT]) for i in range(2)]; bt12 = [Buf("pt10"), Buf("pt11")]
                t22 = [sb(st, f"pt2{i}", [128, NT]) for i in range(2)]; bt22 = [Buf("pt20"), Buf("pt21")]
                nt = norm_tiles(st, NT, "p")
                rcnt = [0]

                def rope_evac(ps, bps, n, tab, btab, tcol, dst_ap, bdst, dst_halves=None):
                    i = rcnt[0] % 2
                    rcnt[0] += 1
                    xb, bxb, t1, bt1, t2, bt2 = xb2[i], bxb2[i], t12[i], bt12[i], t22[i], bt22[i]
                    T.op("act", lambda: nc.scalar.activation(out=xb[:, :n], in_=ps[:, :n], func=AF.Identity), reads=[bps], writes=[bxb])
                    ps2, bps2 = next_ps()
                    T.op("pe", lambda: nc.tensor.matmul(ps2[:, :n], ropeP, xb[:, :n], start=True, stop=True), reads=[bxb, b_cst], writes=[bps2])
                    T.op("dve", lambda: nc.vector.tensor_tensor(out=t1[:, :n], in0=ps[:, :n], in1=tab[:, 0, tcol:tcol + n], op=ALU.mult), reads=[bps, btab], writes=[bt1])
                    T.op("dve", lambda: nc.vector.tensor_tensor(out=t2[:, :n], in0=ps2[:, :n], in1=tab[:, 1, tcol:tcol + n], op=ALU.mult), reads=[bps2, btab], writes=[bt2])
                    if dst_halves is None:
                        T.op("dve", lambda: nc.vector.tensor_tensor(out=dst_ap, in0=t1[:, :n], in1=t2[:, :n], op=ALU.add), reads=[bt1, bt2], writes=[bdst])
                    else:
                        for (r0, dap) in dst_halves:
                            T.op("dve", lambda: nc.vector.tensor_tensor(out=dap, in0=t1[r0:r0 + 64, :n], in1=t2[r0:r0 + 64, :n], op=ALU.add), reads=[bt1, bt2], writes=[bdst])

                p1_tiles = NKEY // NT
                if stop is not None and stop.startswith("p1:"):
                    p1_tiles = int(stop.split(":")[1])
                    stop = "p1"
                for ti in range(p1_tiles):
                    k0 = ti * NT
                    is_ctx = k0 >= OWN + 256
                    j = 1 if is_ctx else 0
                    src0 = (2308 + (k0 - 2304)) if is_ctx else (k0 + 2)
                    xt, bx = xt2[ti % 2], bxt2[ti % 2]
                    rk, brk = rk2[ti % 2], brk2[ti % 2]
                    rq, brq = rq2[ti % 2], brq2[ti % 2]
                    T.dma("sp", xt[:], xl1[:, :, src0:src0 + NT], reads=[b_xl1], writes=[bx])
                    oa = max(k0, 128) - k0
                    ob = min(k0 + NT, 128 + OWN) - k0
                    has_q = (not is_ctx) and ob > oa
                    if not is_ctx:
                        T.dma("sp", rk[:], rope_k[:, :, k0:k0 + NT], writes=[brk])
                    if has_q:
                        qa = k0 + oa - 128
                        T.dma("sp", rq[:, :, oa:ob], rope_q[:, :, qa:qa + (ob - oa)], writes=[brq])
                    rmsnorm_mod(nt, xt, bx, NT, lambda ch: nA[:, 2, ch, j:j + 1], lambda ch: modv[:, 1, 0 * 8 + ch, j:j + 1], hb, b_hb)
                    for jh in range(0 if "K" in _SKIP else 4):
                        ps, bps = next_ps()
                        for kc in range(KC):
                            T.op("pe", lambda: nc.tensor.matmul(ps[:, :NT], wkd[:, kc, jh].rearrange("p a b -> p (a b)"), hb[:, kc, :],
                                                                start=(kc == 0), stop=(kc == KC - 1)),
                                 reads=[b_wkd, b_hb], writes=[bps], signal=(kc == KC - 1))
                        halves = [(0, kTe[0:64, jh, k0:k0 + NT]), (64, kTo[64:128, jh, k0:k0 + NT])]
                        if is_ctx:
                            for (r0, dap) in halves:
                                T.op("act", lambda: nc.scalar.activation(out=dap, in_=ps[r0:r0 + 64, :NT], func=AF.Identity), reads=[bps], writes=[b_kT])
                        else:
                            rope_evac(ps, bps, NT, rk, brk, 0, None, b_kT, dst_halves=halves)
                    if has_q and "Q" not in _SKIP:
                        nq = ob - oa
                        for m in range(KC):
                            ps, bps = next_ps()
                            for kc in range(KC):
                                T.op("pe", lambda: nc.tensor.matmul(ps[:, :nq], wq[:, kc, m * 128:(m + 1) * 128], hb[:, kc, oa:ob],
                                                                    start=(kc == 0), stop=(kc == KC - 1)),
                                     reads=[b_wq, b_hb], writes=[bps], signal=(kc == KC - 1))
                            rope_evac(ps, bps, nq, rq, brq, oa, qT[:, m, qa:qa + nq], b_qT)
                    for bi in range(0 if "V" in _SKIP else NT // 128):
                        blk = k0 // 128 + bi
                        ps, bps = next_ps()
                        for kc in range(KC):
                            T.op("pe", lambda: nc.tensor.matmul(ps[:, :256], hb[:, kc, bi * 128:(bi + 1) * 128], wv[:, kc, :],
                                                                start=(kc == 0), stop=(kc == KC - 1)),
                                 reads=[b_wv, b_hb], writes=[bps], signal=(kc == KC - 1))
                        T.op("act", lambda: nc.scalar.activation(out=vv[:, blk, :, 0, 0:64], in_=ps[:, 0:256].rearrange("p (a b) -> p a b", b=64), func=AF.Identity),
                             reads=[bps], writes=[b_vv])
                        T.op("dve", lambda: nc.vector.tensor_copy(out=vv[:, blk, :, 1, 64:128], in_=ps[:, 0:256].rearrange("p (a b) -> p a b", b=64)),
                             reads=[bps], writes=[b_vv])
                T.barrier()

            with contextlib.ExitStack() as st:
                oT = sb(st, "oT", [128, KC, OWN], BF16); b_oT = Buf("oT")
                with contextlib.ExitStack() as st2:
                    amb = sb(st2, "amb", [128, 4, 512], BF16); b_amb = Buf("amb")
                    for mi in range(4):
                        T.dma("pool", amb[:, mi, :], amask[:, mi, :], writes=[b_amb], cast=True)
                    ones_eo = sb(st2, "ones_eo", [128, 2, 128], BF16); b_oeo = Buf("ones_eo")
                    T.op("dve", lambda: nc.vector.memset(ones_eo[:].rearrange("p a b -> p (a b)"), 0.0), writes=[b_oeo])
                    T.op("dve", lambda: nc.vector.memset(ones_eo[:, 0, 0:64], 1.0), writes=[b_oeo])
                    T.op("dve", lambda: nc.vector.memset(ones_eo[:, 1, 64:128], 1.0), writes=[b_oeo])
                    es16 = sb(st2, "es16", [128, 16]); b_es = Buf("es16")
                    T.op("act", lambda: nc.scalar.activation(out=es16[:], in_=pcs[:, P_SINK:P_SINK + 16], func=AF.Exp), reads=[b_pc], writes=[b_es])
                    onesf = sb(st2, "onesf", [128, 128]); b_onesf = Buf("onesf")
                    T.op("dve", lambda: nc.vector.memset(onesf[:], 1.0), writes=[b_onesf])
                    esk = sb(st2, "esk", [128, 4, 256]); b_esk = Buf("esk")
                    for jh in range(4):
                        for g in range(4):
                            r0 = (g % 2) * 64
                            c0 = (g // 2) * 128
                            T.op("dve", lambda: nc.vector.tensor_scalar(out=esk[r0:r0 + 64, jh, c0:c0 + 128], in0=onesf[r0:r0 + 64, :],
                                                                        scalar1=es16[r0:r0 + 64, 4 * jh + g:4 * jh + g + 1], scalar2=None, op0=ALU.mult),
                                 reads=[b_es, b_onesf], writes=[b_esk])
                    pT2 = [sb(st2, f"pT{i}", [128, 5, 512], BF16) for i in range(2)]; bpT2 = [Buf("pT0"), Buf("pT1")]
                    dn2 = [sb(st2, f"dn{i}", [128, 256]) for i in range(2)]; bdn2 = [Buf("dn0"), Buf("dn1")]
                    units = [(i, jh) for i in range(16) for jh in range(4)]

                    def key_blocks(i):
                        return [(i, 2 if i == 0 else 0), (i + 1, None), (i + 2, 3 if i == 15 else 1), (18, None), (19, None)]

                    def emit_scores(u):
                        i, jh = units[u]
                        pT, bpT = pT2[u % 2], bpT2[u % 2]
                        qc = slice(i * 128, (i + 1) * 128)
                        for kb, (blk, mk) in enumerate(key_blocks(i)):
                            ps, bps = next_ps()
                            kc_ = slice(blk * 128, (blk + 1) * 128)
                            first = True
                            if mk is not None:
                                T.op("pe", lambda: nc.tensor.matmul(ps[:, :], ident, amb[:, mk, :], start=True, stop=False),
                                     reads=[b_cst, b_amb], writes=[bps], signal=False)
                                first = False
                            T.op("pe", lambda: nc.tensor.matmul(ps[:, 0:256], kTe[:, jh, kc_], qT[:, 2 * jh:2 * jh + 2, qc], start=first, stop=False),
                                 reads=[b_kT, b_qT], writes=[bps], signal=False)
                            T.op("pe", lambda: nc.tensor.matmul(ps[:, 256:512], kTo[:, jh, kc_], qT[:, 2 * jh:2 * jh + 2, qc], start=False, stop=True),
                                 reads=[b_kT, b_qT], writes=[bps])
                            T.op("act", lambda: nc.scalar.activation(out=pT[:, kb, :], in_=ps[:, :], func=AF.Exp), reads=[bps], writes=[bpT])

                    def emit_pv(u):
                        i, jh = units[u]
                        pT, bpT = pT2[u % 2], bpT2[u % 2]
                        dn, bdn = dn2[u % 2], bdn2[u % 2]
                        blks = [b_ for (b_, _) in key_blocks(i)]
                        psO, bpsO = next_ps()
                        for kb, blk in enumerate(blks):
                            for par in range(2):
                                last = (kb == 4 and par == 1)
                                T.op("pe", lambda: nc.tensor.matmul(psO[:, 0:256], vv[:, blk, jh, par, :], pT[:, kb, par * 256:(par + 1) * 256],
                                                                    start=(kb == 0 and par == 0), stop=last),
                                     reads=[b_vv, bpT], writes=[bpsO], signal=last)
                        psD, bpsD = next_ps()
                        for kb, blk in enumerate(blks):
                            for par in range(2):
                                last = (kb == 4 and par == 1)
                                T.op("pe", lambda: nc.tensor.matmul(psD[:, 0:256], ones_eo[:, par, :], pT[:, kb, par * 256:(par + 1) * 256],
                                                                    start=(kb == 0 and par == 0), stop=last),
                                     reads=[b_oeo, bpT], writes=[bpsD], signal=last)
                        T.op("dve", lambda: nc.vector.tensor_tensor(out=dn[:], in0=psD[:, 0:256], in1=esk[:, jh, :], op=ALU.add), reads=[bpsD, b_esk], writes=[bdn])
                        T.op("dve", lambda: nc.vector.reciprocal(out=dn[:], in_=dn[:]), writes=[bdn])
                        T.op("dve", lambda: nc.vector.tensor_tensor(out=oT[:, 2 * jh:2 * jh + 2, i * 128:(i + 1) * 128],
                                                                    in0=psO[:, 0:256].rearrange("p (a b) -> p a b", b=128),
                                                                    in1=dn[:].rearrange("p (a b) -> p a b", b=128), op=ALU.mult),
                             reads=[bpsO, bdn], writes=[b_oT])

                    nun = 0 if stop == "p1" else len(units)
                    if nun:
                        emit_scores(0)
                    for u in range(nun):
                        if u + 1 < nun:
                            emit_scores(u + 1)
                        emit_pv(u)
                    T.barrier()

                with contextlib.ExitStack() as st3:
                    w_o = sb(st3, "at_w_o", [128, KC, D], BF16); b_wo_ = Buf("at_w_o")
                    T.dma("pool", w_o[:], at_w_out.rearrange("(kc p) f -> p kc f", p=128), writes=[b_wo_], cast=True)
                    NT = 256
                    xt2 = [sb(st3, f"ox{i}", [128, KC, NT]) for i in range(2)]; bxt2 = [Buf("ox0"), Buf("ox1")]
                    b_xm1 = Buf("xmid1")
                    for ti, (c0, n) in enumerate(tiles_of(OWN, NT) if stop not in ("p1", "p2") else []):
                        xt, bx = xt2[ti % 2], bxt2[ti % 2]
                        T.dma("sp", xt[:, :, :n], xl1[:, :, 130 + c0:130 + c0 + n], reads=[b_xl1], writes=[bx])
                        for m in range(KC):
                            ps, bps = next_ps()
                            for kc in range(KC):
                                T.op("pe", lambda: nc.tensor.matmul(ps[:, :n], w_o[:, kc, m * 128:(m + 1) * 128], oT[:, kc, c0:c0 + n],
                                                                    start=(kc == 0), stop=(kc == KC - 1)),
                                     reads=[b_wo_, b_oT], writes=[bps], signal=(kc == KC - 1))
                            T.op("dve", lambda: nc.vector.scalar_tensor_tensor(
                                out=xt[:, m, :n], in0=ps[:, :n], scalar=modv[:, 1, 2 * 8 + m, 0:1], in1=xt[:, m, :n], op0=ALU.mult, op1=ALU.add),
                                reads=[bps, b_modv], writes=[bx])
                        T.dma("sp", xmid1[:, :, c0:c0 + n], xt[:, :, :n], reads=[bx])
                    T.barrier()

        if stop is None:
            ffn_phase(1, xmid1, outT, [(0, OWN, 0)], True)

        fin_toks = []
        for sem, val in T.dsems:
            if val > 0:
                fin_toks.append((sem, val))
        T._wait(T.E["sp"], fin_toks)
    return nc


def _fm(a):
    T_ = a.shape[0]
    return np.ascontiguousarray(a.reshape(T_, KC, 128).transpose(2, 1, 0))


def _chan(v):
    return np.ascontiguousarray(v.reshape(4, 128).T)


def _blockdiag(w):
    out = np.zeros((128, 4, 128), np.float32)
    for g in range(4):
        for hh in range(2):
            out[hh * 64:(hh + 1) * 64, g, hh * 64:(hh + 1) * 64] = w[2 * g + hh]
    return out


def prepare_inputs(inp):
    f32 = np.float32
    x = np.asarray(inp["x"], f32); c = np.asarray(inp["c"], f32); ctx = np.asarray(inp["ctx"], f32)
    c_ctx = np.asarray(inp["c_ctx"], f32)
    shared = {}
    for k in ["ada_w", "ffn_w_in", "ffn_w_out"]:
        shared[k] = np.ascontiguousarray(np.asarray(inp[k], f32))
    shared["rc_w_in"] = np.ascontiguousarray(np.asarray(inp["rc_w_in"], f32)[0])
    shared["rc_w_out"] = np.ascontiguousarray(np.asarray(inp["rc_w_out"], f32)[0])
    shared["at_w_qkv"] = np.ascontiguousarray(np.asarray(inp["at_w_qkv"], f32)[0])
    shared["at_w_out"] = np.ascontiguousarray(np.asarray(inp["at_w_out"], f32)[0])

    sm = np.zeros((128, S_NS), f32)
    ada_b = np.asarray(inp["ada_b"], f32)
    for l in range(2):
        sm[:, S_ADAB + l * 48:S_ADAB + (l + 1) * 48] = ada_b[l].reshape(48, 128).T
        sm[:, S_GMIX + l * 8:S_GMIX + (l + 1) * 8] = np.asarray(inp["norm_mix_g"], f32)[l].reshape(8, 128).T
        sm[:, S_GFFN + l * 8:S_GFFN + (l + 1) * 8] = np.asarray(inp["norm_ffn_g"], f32)[l].reshape(8, 128).T
    sm[:, S_GFIN:S_GFIN + 8] = np.asarray(inp["norm_final_g"], f32).reshape(8, 128).T
    sm[:, S_CAB:S_CAB + 4] = _chan(np.asarray(inp["rc_conv_a_b"], f32)[0])
    cbw = np.asarray(inp["rc_conv_b_w"], f32)[0]
    for g in range(4):
        for o in range(3):
            sm[:, S_CBW + g * 3 + o] = cbw[o, g * 128:(g + 1) * 128]
    shared["smalls"] = sm

    cst = np.zeros((128, 384), f32)
    cst[:, 0:128] = np.eye(128, dtype=f32)
    P = np.zeros((128, 128), f32)
    for m in range(128):
        d = m % 64
        base = m - d
        within = d % 32
        if within < 16:
            P[base + (d + 16), m] = -1.0
        else:
            P[base + (d - 16), m] = 1.0
    cst[:, 128:256] = P
    shared["consts"] = cst

    caw = np.asarray(inp["rc_conv_a_w"], f32)[0]
    r_w = np.asarray(inp["rc_gate_r_w"], f32)[0]; r_b = np.asarray(inp["rc_gate_r_b"], f32)[0]
    i_w = np.asarray(inp["rc_gate_i_w"], f32)[0]; i_b = np.asarray(inp["rc_gate_i_b"], f32)[0]
    lam = np.asarray(inp["rc_lambda"], f32)[0]
    sink = np.asarray(inp["at_sink"], f32)[0]

    inv_freq = (10000.0 ** (-np.arange(16, dtype=f32) / 16)).astype(f32)

    def rope_tab(tok, scale):
        tok = np.asarray(tok)
        row = (tok // 64).astype(f32); col = (tok % 64).astype(f32)
        ang = np.stack([row[:, None] * inv_freq, col[:, None] * inv_freq], axis=1).astype(f32)
        cos = np.cos(ang).astype(f32); sin = np.sin(ang).astype(f32)
        tab = np.zeros((128, 2, len(tok)), f32)
        for p in range(128):
            d = p % 64
            ax = d // 32; fr = d % 16
            tab[p, 0] = cos[:, ax, fr] * scale
            tab[p, 1] = sin[:, ax, fr] * scale
        return tab

    per_core = []
    for core in range(8):
        b, k = core // 4, core % 4
        m = {}
        t0 = 2048 * k - 130
        xf = np.zeros((NF, D), f32); mk = np.zeros((NF,), f32)
        lo = max(0, -t0); hi = min(NF, SEQ - t0)
        xf[lo:hi] = x[b, t0 + lo:t0 + hi]; mk[lo:hi] = 1.0
        m["xT_full"] = _fm(xf)
        m["mk_full"] = np.ascontiguousarray(np.broadcast_to(mk[None, :], (128, NF)))
        m["pf_full"] = np.ascontiguousarray((1.0 - mk)[None, :])
        m["xT_ctx"] = _fm(ctx[b])
        xc_ = np.zeros((NCARRY, D), f32); mkc = np.zeros((NCARRY,), f32); pfc = np.ones((NCARRY,), f32)
        if k >= 1:
            nfw = 2048 * k - 128
            xc_[126:126 + nfw] = x[b, 0:nfw]; mkc[126:126 + nfw] = 1.0; pfc[126:126 + nfw] = 0.0
            xc_[2048 * k - 2] = x[b, nfw]; mkc[2048 * k - 2] = 1.0
        if k <= 2:
            tlast = 2048 * (k + 1) + 128
            toks = np.arange(SEQ - 1, tlast - 1, -1)
            c0 = 2048 * k + 126
            xc_[c0:c0 + len(toks)] = x[b, toks]; mkc[c0:c0 + len(toks)] = 1.0; pfc[c0:c0 + len(toks)] = 0.0
            assert c0 + len(toks) == 6142
            xc_[6142] = x[b, tlast - 1]; xc_[6143] = x[b, tlast - 2]; mkc[6142:6144] = 1.0
        m["xT_carry"] = _fm(xc_)
        m["mk_carry"] = np.ascontiguousarray(np.broadcast_to(mkc[None, :], (128, NCARRY)))
        m["pf_carry"] = np.ascontiguousarray(pfc[None, :])
        pcv = np.zeros((128, P_NP), f32)
        pcv[:, P_CVEC:P_CVEC + 16] = np.stack([c[b].reshape(8, 128).T, c_ctx.reshape(8, 128).T], axis=-1).reshape(128, 16)
        dirs = [0, 1] + [0 if s < k else 1 for s in range(3)]
        natural = [True, True] + [s < k for s in range(3)]
        gw = np.zeros((128, 5, 2, 4, 128), f32)
        for idx in range(5):
            d = dirs[idx]
            gw[:, idx, 0] = _blockdiag(r_w[d]); gw[:, idx, 1] = _blockdiag(i_w[d])
            pcv[:, P_GB + idx * 8:P_GB + idx * 8 + 4] = _chan(r_b[d].reshape(512))
            pcv[:, P_GB + idx * 8 + 4:P_GB + idx * 8 + 8] = _chan(i_b[d].reshape(512))
            pcv[:, P_LAM + idx * 4:P_LAM + idx * 4 + 4] = _chan(lam[d])
            for g in range(4):
                for o5 in range(5):
                    o = o5 - 2
                    if natural[idx]:
                        j = o + 2
                    else:
                        j = 2 - o
                    if 0 <= j <= 3:
                        pcv[:, P_TAPS + (idx * 4 + g) * 5 + o5] = caw[j, g * 128:(g + 1) * 128]
        m["gw_all"] = gw.reshape(128, -1)
        fl = np.zeros(16, f32)
        for s in range(3):
            fwd = s < k
            if s == 0:
                fl[0 if fwd else 1] = 1.0
            elif fwd or s > k:
                fl[3 * s + 2] = 1.0
            else:
                fl[3 * s + 1] = 1.0
        if k == 0:
            fl[9] = 1.0
        else:
            fl[10 + (k - 1)] = 1.0
        if k == 3:
            fl[13] = 1.0
        else:
            fl[14] = 1.0
        pcv[:, P_FLAGS:P_FLAGS + 16] = fl[None, :]
        pcv[:, P_SINK:P_SINK + 16] = sink[None, :]
        m["pc"] = pcv
        m["rope_q"] = rope_tab(np.arange(2048 * k, 2048 * (k + 1)), 0.125)
        m["rope_k"] = rope_tab(np.clip(np.arange(2048 * k - 128, 2048 * (k + 1) + 128), 0, SEQ - 1), 1.0)
        am = np.zeros((128, 4, 512), f32)
        jj = np.arange(128)[:, None]; ii = np.arange(128)[None, :]
        prev = np.where(jj >= ii, 0.0, NEGBIG).astype(f32)
        nxt = np.where(jj <= ii, 0.0, NEGBIG).astype(f32)
        am[:, 0] = np.tile(prev, (1, 4)); am[:, 1] = np.tile(nxt, (1, 4))
        am[:, 2] = NEGBIG if k == 0 else am[:, 0]
        am[:, 3] = NEGBIG if k == 3 else am[:, 1]
        m["amask"] = am
        m.update(shared)
        per_core.append(m)
    return per_core


_NC_CACHE = {}


def kernel(**inputs):
    if "nc" not in _NC_CACHE:
        _NC_CACHE["nc"] = build_nc(False)
    nc = _NC_CACHE["nc"]
    in_maps = prepare_inputs(inputs)
    res = run_bass_kernel_spmd(nc, in_maps, core_ids=list(range(8)))
    out = np.zeros((2, SEQ, D), np.float32)
    for core in range(8):
        b, k = core // 4, core % 4
        oT = np.asarray(res.results[core]["outT"])
        out[b, 2048 * k:2048 * (k + 1)] = oT.transpose(2, 1, 0).reshape(OWN, D)
    return out
```
